# Optimizing a Trainium2 kernel written in Bass

```python
import math
import jax, jax.numpy as jnp
from jax import lax
import numpy as np

D_MODEL = 2048
BATCH = 4
SEQ = 4096
DEPTH = 1

GDN_HEADS = 8
GDN_DK = 128
GDN_DV = 128
CONV_WIDTH = 4
CHUNK = 64
DSA_HEADS = 8
DSA_KV_HEADS = 2
DSA_HEAD_DIM = 128
IDX_HEADS = 16
IDX_DIM = 64
TOPK_MAX = 256
Q_BLOCK = 128
ROPE_THETA = 500000.0
ROPE_FRACTION = 4
EPS = 1e-6

GDN_W = GDN_HEADS * GDN_DV
DSA_W = DSA_HEADS * DSA_HEAD_DIM
MIX_W = GDN_W + DSA_W
CONV_CH = 2 * GDN_HEADS * GDN_DK + GDN_HEADS * GDN_DV
IN_SIZES = (
    GDN_HEADS * GDN_DK,
    GDN_HEADS * GDN_DK,
    GDN_HEADS * GDN_DV,
    GDN_W,
    GDN_HEADS,
    GDN_HEADS,
    DSA_HEADS * DSA_HEAD_DIM,
    DSA_KV_HEADS * DSA_HEAD_DIM,
    DSA_KV_HEADS * DSA_HEAD_DIM,
    DSA_W,
    IDX_HEADS * IDX_DIM,
    IDX_DIM,
    IDX_HEADS,
)
IN_COLS = sum(IN_SIZES)

kernel_name = "hymba_gdn_dsa_hybrid_layer"


def rms_norm(x, g):
    xf = x.astype(jnp.float32)
    out = xf * lax.rsqrt(jnp.mean(xf * xf, axis=-1, keepdims=True) + EPS)
    return (out * g.astype(jnp.float32)).astype(x.dtype)


def l2_normalize(x):
    return x * lax.rsqrt(jnp.sum(x * x, axis=-1, keepdims=True) + EPS)


def partial_rope(x, positions):
    d = x.shape[-1]
    rot = d // ROPE_FRACTION
    half = rot // 2
    inv_freq = ROPE_THETA ** (-(jnp.arange(half, dtype=jnp.float32) * 2.0 / rot))
    ang = positions.astype(jnp.float32)[..., None] * inv_freq
    cos = jnp.cos(ang)[:, :, None, :]
    sin = jnp.sin(ang)[:, :, None, :]
    x1, x2, rest = x[..., :half], x[..., half:rot], x[..., rot:]
    return jnp.concatenate([x1 * cos - x2 * sin, x2 * cos + x1 * sin, rest], axis=-1)


def causal_depthwise_conv(x, w):
    c = x.shape[-1]
    return lax.conv_general_dilated(
        x, w[:, None, :].astype(x.dtype), window_strides=(1,), padding=[(w.shape[0] - 1, 0)],
        dimension_numbers=('NWC', 'WIO', 'NWC'), feature_group_count=c)


def gated_delta_rule(q, k, v, g, beta):
    b, t, h, dk = q.shape
    dv = v.shape[-1]
    n = t // CHUNK
    q = l2_normalize(q) * (dk ** -0.5)
    k = l2_normalize(k)

    def to_chunks(a):
        return a.reshape(b, n, CHUNK, h, a.shape[-1]).transpose(1, 0, 3, 2, 4)

    qc, kc, vc = to_chunks(q), to_chunks(k), to_chunks(v)
    gc = jnp.cumsum(g.reshape(b, n, CHUNK, h).transpose(1, 0, 3, 2), axis=-1)
    bc = beta.reshape(b, n, CHUNK, h).transpose(1, 0, 3, 2)
    pos = jnp.arange(CHUNK)
    incl = pos[:, None] >= pos[None, :]
    strict = pos[:, None] > pos[None, :]
    diff = gc[..., :, None] - gc[..., None, :]
    decay = jnp.where(incl, jnp.exp(jnp.where(incl, diff, 0.0)), 0.0)
    kb = kc * bc[..., None]
    vb = vc * bc[..., None]
    lmat = jnp.where(strict, jnp.einsum('nbhid,nbhjd->nbhij', kb, kc) * decay, 0.0)
    eye = jnp.eye(CHUNK, dtype=q.dtype)
    tmat = lax.linalg.triangular_solve(lmat + eye, jnp.broadcast_to(eye, lmat.shape),
                                       left_side=True, lower=True, unit_diagonal=True)
    w = jnp.einsum('nbhij,nbhjd->nbhid', tmat, kb * jnp.exp(gc)[..., None])
    u = jnp.einsum('nbhij,nbhjd->nbhid', tmat, vb)
    a_intra = jnp.einsum('nbhid,nbhjd->nbhij', qc, kc) * decay

    def step(state, xs):
        qn, kn, un, wn, gn, an = xs
        v_new = un - jnp.einsum('bhcd,bhde->bhce', wn, state)
        o = (jnp.einsum('bhcd,bhde->bhce', qn * jnp.exp(gn)[..., None], state)
             + jnp.einsum('bhij,bhje->bhie', an, v_new))
        g_last = gn[..., -1]
        state = (state * jnp.exp(g_last)[..., None, None]
                 + jnp.einsum('bhcd,bhce->bhde', kn * jnp.exp(g_last[..., None] - gn)[..., None], v_new))
        return state, o

    s0 = jnp.zeros((b, h, dk, dv), q.dtype)
    _, o = lax.scan(step, s0, (qc, kc, u, w, gc, a_intra))
    return o.transpose(1, 0, 3, 2, 4).reshape(b, t, h, dv)


def dsa_sparse_attention(q, k, v, q_idx, k_idx, w_idx, k_sel):
    b, t, hq, dh = q.shape
    hkv = k.shape[2]
    rep = hq // hkv
    nb = t // Q_BLOCK
    qg = q.reshape(b, t, hkv, rep, dh)
    key_pos = jnp.arange(t)
    scale = dh ** -0.5
    gather = jax.vmap(lambda src, ids: src[ids])

    def block(j):
        t0 = j * Q_BLOCK
        qi = lax.dynamic_slice_in_dim(q_idx, t0, Q_BLOCK, axis=1)
        wi = lax.dynamic_slice_in_dim(w_idx, t0, Q_BLOCK, axis=1)
        qa = lax.dynamic_slice_in_dim(qg, t0, Q_BLOCK, axis=1)
        q_pos = t0 + jnp.arange(Q_BLOCK)
        score = jnp.einsum('bqh,bqhs->bqs', wi,
                           jax.nn.relu(jnp.einsum('bqhd,bsd->bqhs', qi, k_idx)))
        causal = key_pos[None, :] <= q_pos[:, None]
        score = jnp.where(causal[None], score, -jnp.inf)
        _, sel = lax.top_k(score, k_sel)
        valid = sel <= q_pos[None, :, None]
        k_g = gather(k, sel)
        v_g = gather(v, sel)
        logits = jnp.einsum('bqgrd,bqkgd->bqgrk', qa, k_g) * scale
        logits = jnp.where(valid[:, :, None, None, :], logits, -jnp.inf)
        p = jax.nn.softmax(logits, axis=-1)
        o = jnp.einsum('bqgrk,bqkgd->bqgrd', p, v_g)
        return o.reshape(b, Q_BLOCK, hq * dh)

    out = lax.map(block, jnp.arange(nb))
    return out.transpose(1, 0, 2, 3).reshape(b, t, hq * dh)


def setup_inputs(seed: int = 0) -> dict:
    key = jax.random.key(seed)
    ks = jax.random.split(key, 10)
    x = jax.random.normal(ks[0], (BATCH, SEQ, D_MODEL), jnp.float32)
    positions = jnp.broadcast_to(jnp.arange(SEQ, dtype=jnp.int32), (BATCH, SEQ))
    attn_norm_g = 1.0 + 0.02 * jax.random.normal(ks[1], (DEPTH, D_MODEL), jnp.float32)
    w_in = jax.random.normal(ks[2], (DEPTH, D_MODEL, IN_COLS), jnp.float32) * D_MODEL ** -0.5
    gdn_conv_w = jax.random.normal(ks[3], (DEPTH, CONV_WIDTH, CONV_CH), jnp.float32) * CONV_WIDTH ** -0.5
    gdn_a_log = jnp.log(jax.random.uniform(ks[4], (DEPTH, GDN_HEADS), jnp.float32, 1.0, 16.0))
    dt = jnp.exp(jax.random.uniform(ks[5], (DEPTH, GDN_HEADS), jnp.float32,
                                    math.log(1e-3), math.log(1e-1)))
    gdn_dt_bias = dt + jnp.log(-jnp.expm1(-dt))
    gdn_norm_g = 1.0 + 0.02 * jax.random.normal(ks[6], (DEPTH, GDN_DV), jnp.float32)
    w_out = jax.random.normal(ks[7], (DEPTH, MIX_W, D_MODEL), jnp.float32) * MIX_W ** -0.5
    final_norm_g = 1.0 + 0.02 * jax.random.normal(ks[8], (D_MODEL,), jnp.float32)
    return {"x": x, "positions": positions, "attn_norm_g": attn_norm_g, "w_in": w_in,
            "gdn_conv_w": gdn_conv_w, "gdn_a_log": gdn_a_log, "gdn_dt_bias": gdn_dt_bias,
            "gdn_norm_g": gdn_norm_g, "w_out": w_out, "final_norm_g": final_norm_g}


def reference(x, positions, attn_norm_g, w_in, gdn_conv_w, gdn_a_log, gdn_dt_bias,
              gdn_norm_g, w_out, final_norm_g):
    b, t, _ = x.shape
    k_sel = min(TOPK_MAX, t // 4)
    split_points = tuple(int(s) for s in np.cumsum(IN_SIZES)[:-1])
    f32 = jnp.float32
    for l in range(DEPTH):
        h = rms_norm(x, attn_norm_g[l])
        proj = (h @ w_in[l]).astype(f32)
        (gq, gk, gv, gz, ga, gb, aq, ak, av, az, iq, ik, iw) = jnp.split(proj, split_points, axis=-1)

        qkv = jax.nn.silu(causal_depthwise_conv(jnp.concatenate([gq, gk, gv], axis=-1), gdn_conv_w[l].astype(f32)))
        q_a, k_a, v_a = jnp.split(qkv, (GDN_HEADS * GDN_DK, 2 * GDN_HEADS * GDN_DK), axis=-1)
        q_a = q_a.reshape(b, t, GDN_HEADS, GDN_DK)
        k_a = k_a.reshape(b, t, GDN_HEADS, GDN_DK)
        v_a = v_a.reshape(b, t, GDN_HEADS, GDN_DV)
        g_a = -jnp.exp(gdn_a_log[l].astype(f32)) * jax.nn.softplus(ga + gdn_dt_bias[l].astype(f32))
        beta_a = jax.nn.sigmoid(gb)
        o_a = gated_delta_rule(q_a, k_a, v_a, g_a, beta_a)
        o_a = rms_norm(o_a, gdn_norm_g[l]).reshape(b, t, GDN_W) * jax.nn.silu(gz)

        q_b = partial_rope(aq.reshape(b, t, DSA_HEADS, DSA_HEAD_DIM), positions)
        k_b = partial_rope(ak.reshape(b, t, DSA_KV_HEADS, DSA_HEAD_DIM), positions)
        v_b = av.reshape(b, t, DSA_KV_HEADS, DSA_HEAD_DIM)
        q_i = partial_rope(iq.reshape(b, t, IDX_HEADS, IDX_DIM), positions)
        k_i = partial_rope(ik.reshape(b, t, 1, IDX_DIM), positions)[:, :, 0]
        w_i = iw * (IDX_HEADS ** -0.5) * (IDX_DIM ** -0.5)
        o_b = dsa_sparse_attention(q_b, k_b, v_b, q_i, k_i, w_i, k_sel) * jax.nn.silu(az)

        mixed = jnp.concatenate([o_a, o_b], axis=-1).astype(x.dtype) @ w_out[l]
        x = x + mixed
    return rms_norm(x, final_norm_g)
```

```python
from contextlib import ExitStack
import math
import numpy as np
import concourse.bass as bass
import concourse.mybir as mybir

F32 = mybir.dt.float32
BF16 = mybir.dt.bfloat16
I32 = mybir.dt.int32
ALU = mybir.AluOpType
AF = mybir.ActivationFunctionType
AX = mybir.AxisListType

ENGS = ("pe", "act", "dve", "pool", "sp")
SEM_EPOCH = 4000
DMA_SLOTS = 6
DMA_EPOCH = 1500


class Prog:
    def __init__(self, nc, stack):
        self.nc = nc
        self.stack = stack
        self.streams = {e: [] for e in ENGS}
        self.cur = {e: None for e in ENGS}
        self.waited = {e: {} for e in ENGS}
        self.lw = {}
        self.rd = {}
        self.nsem = 0
        self.slots = {e: [[None, 0, None] for _ in range(DMA_SLOTS)] for e in ENGS}
        self.slot_i = {e: 0 for e in ENGS}
        self.n_inst = 0

    def _newsem(self):
        self.nsem += 1
        return self.stack.enter_context(self.nc.semaphore(f"s{self.nsem}"))

    def add(self, eng, fns, r=(), w=(), dma=False):
        if not isinstance(fns, (list, tuple)):
            fns = [fns]
        r = list(r)
        w = list(w) + [k for k in r if isinstance(k, tuple) and k and k[0] == "ps" and k not in w]
        deps = []
        for k in r:
            t = self.lw.get(k)
            if t is not None:
                deps.append(t)
        for k in w:
            t = self.lw.get(k)
            if t is not None:
                deps.append(t)
            for s, v in self.rd.get(k, {}).items():
                deps.append((s, v))
        if dma:
            i = self.slot_i[eng]
            self.slot_i[eng] = (i + 1) % DMA_SLOTS
            slot = self.slots[eng][i]
            if slot[0] is None or slot[1] >= 16 * DMA_EPOCH:
                if slot[2] is not None:
                    deps.append(slot[2])
                slot[0] = self._newsem()
                slot[1] = 0
            elif slot[2] is not None:
                deps.append(slot[2])
            slot[1] += 16
            tok = (slot[0], slot[1])
            slot[2] = tok
            inc = 16
        else:
            c = self.cur[eng]
            if c is None or c[1] >= SEM_EPOCH:
                c = self.cur[eng] = [self._newsem(), 0]
            c[1] += 1
            tok = (c[0], c[1])
            inc = 1
        waits = []
        wd = self.waited[eng]
        own = self.cur[eng][0] if (self.cur[eng] is not None) else None
        for s, v in deps:
            if eng == "pe" and not dma and s is own:
                continue
            if wd.get(id(s), 0) < v:
                wd[id(s)] = v
                waits.append((s, v))
        ww = {}
        for s, v in waits:
            if id(s) not in ww or ww[id(s)][1] < v:
                ww[id(s)] = (s, v)
        self.streams[eng].append((list(ww.values()), fns, tok[0], inc))
        self.n_inst += len(fns) + len(ww)
        for k in w:
            self.lw[k] = tok
            self.rd[k] = {}
        for k in r:
            if k in w:
                continue
            d = self.rd.setdefault(k, {})
            d[tok[0]] = tok[1]
        return tok

    def barrier(self):
        toks = []
        for e in ENGS:
            if self.cur[e] is not None:
                toks.append((self.cur[e][0], self.cur[e][1]))
            for slot in self.slots[e]:
                if slot[2] is not None:
                    toks.append(slot[2])
        for e in ENGS:
            wd = self.waited[e]
            waits = []
            for s, v in toks:
                if v > 0 and wd.get(id(s), 0) < v:
                    wd[id(s)] = v
                    waits.append((s, v))
            self.streams[e].append((waits, [], None, 0))
            self.n_inst += len(waits)
        self.lw.clear()
        self.rd.clear()

    def finish_waits(self, eng="sp"):
        waits = []
        for e in ENGS:
            for slot in self.slots[e]:
                if slot[2] is not None:
                    waits.append(slot[2])
        self.streams[eng].append((waits, [], None, 0))

    def emit(self):
        nc = self.nc
        streams = self.streams

        def replay(name, eng):
            for waits, fns, sem, inc in streams[name]:
                for s, v in waits:
                    eng.wait_ge(s, v)
                for f in fns[:-1]:
                    f(eng)
                if fns:
                    fns[-1](eng).then_inc(sem, inc)

        with nc.Block() as block:
            @block.tensor
            def _(e):
                replay("pe", e)

            @block.scalar
            def _(e):
                replay("act", e)

            @block.vector
            def _(e):
                replay("dve", e)

            @block.gpsimd
            def _(e):
                replay("pool", e)

            @block.sync
            def _(e):
                replay("sp", e)

    def dma(self, eng, out, in_, r, w, **kw):
        return self.add(eng, lambda e: e.dma_start(out=out, in_=in_, **kw), r, w, dma=True)

    def act(self, out, in_, func, r, w, eng="act", **kw):
        return self.add(eng, lambda e: e.activation(out=out, in_=in_, func=func, **kw), r, w)

    def ts(self, eng, out, in0, s1, s2, op0, r, w, op1=None, **kw):
        if op1 is None:
            return self.add(eng, lambda e: e.tensor_scalar(out=out, in0=in0, scalar1=s1, scalar2=None, op0=op0, **kw), r, w)
        return self.add(eng, lambda e: e.tensor_scalar(out=out, in0=in0, scalar1=s1, scalar2=s2, op0=op0, op1=op1, **kw), r, w)

    def tt(self, eng, out, in0, in1, op, r, w):
        return self.add(eng, lambda e: e.tensor_tensor(out=out, in0=in0, in1=in1, op=op), r, w)

    def stt(self, out, in0, scalar, in1, op0, op1, r, w, **kw):
        return self.add("dve", lambda e: e.scalar_tensor_tensor(out=out, in0=in0, scalar=scalar, in1=in1, op0=op0, op1=op1, **kw), r, w)

    def copy(self, eng, out, in_, r, w):
        if eng == "act":
            return self.add(eng, lambda e: e.copy(out=out, in_=in_), r, w)
        return self.add(eng, lambda e: e.tensor_copy(out=out, in_=in_), r, w)

    def memset(self, eng, ap, val, w):
        return self.add(eng, lambda e: e.memset(ap, val), (), w)

    def mm(self, out, lhsT, rhs, start, stop, **kw):
        return lambda e: e.matmul(out, lhsT, rhs, start=start, stop=stop, **kw)

    def tr(self, out, in_, ident):
        return lambda e: e.transpose(out, in_, ident)
from concourse.bass_utils import run_bass_kernel_spmd


D = 2048; KC = 16; T = 4096; NT = 32; TM = 512; NM = 8; TPM = 4; NOWN = 16
EPS = 1e-6
C_QKV = 0; C_GZ = 3072; C_KV = 4096; C_SM = 4608; C_AQ = 4688; C_AZ = 5712; C_IQ = 6736; C_IW = 7760; NCOL = 7776
NIT = 26
TWO_PI = 2.0 * math.pi


def build(debug=False, n_macro=NM, stop=99, nblk=99):
    nc = bass.Bass("TRN2", target_bir_lowering=False)
    dt_in = lambda n, s, d=F32: nc.dram_tensor(n, s, d, kind="ExternalInput").ap()
    x_d = dt_in("x", [T, D]); xo_d = dt_in("xo", [T // 2, D])
    wall_d = dt_in("wall", [128, KC, NCOL]); wout_d = dt_in("wout", [128, KC, D])
    cst_d = dt_in("cst", [128, 4 * 128])
    gn_d = dt_in("gn", [128, KC]); fing_d = dt_in("fing", [D]); gdng_d = dt_in("gdng", [128])
    alog_d = dt_in("alog", [8]); dtb_d = dt_in("dtb", [8]); cw_d = dt_in("cw", [128, 24 * 4])
    pos_d = dt_in("pos", [128, NT], I32); poso_d = dt_in("poso", [128, NOWN], I32)
    invf_d = dt_in("invf", [16]); flg_d = dt_in("flg", [128, 2]); cmask_d = dt_in("cmask", [128, 256])
    out_d = nc.dram_tensor("out", [T // 2, D], F32, kind="ExternalOutput").ap()
    if debug:
        dbg_d = nc.dram_tensor("dbg", [T // 2, D], BF16, kind="ExternalOutput").ap()

    with ExitStack() as st:
        P = Prog(nc, st)
        sb = lambda n, s, d: st.enter_context(nc.sbuf_tensor("s_" + n, s, d))
        ps = [st.enter_context(nc.psum_tensor(f"ps{i}", [128, 512], F32)) for i in range(8)]
        psk = lambda i: ("ps", i)
        psb = lambda i: ps[i][:, :].bitcast(BF16)

        ra = sb("ra", [128, 4, 32], F32); rb = sb("rb", [128, 4, 32], F32); rtmp = sb("rtmp", [128, 256], BF16)
        cst = sb("cst", [128, 512], F32)
        ident_f = cst[:, 0:128]; U_f = cst[:, 128:256]; Ls_f = cst[:, 256:384]; ones_f = cst[:, 384:512]
        ident_b = sb("ident_b", [128, 128], BF16); ones_b = sb("ones_b", [128, 128], BF16)
        identrep = sb("identrep", [128, 4, 128], BF16)
        gn = sb("gn", [128, KC], F32); gdng = sb("gdng", [128, 128], F32)
        alog = sb("alog", [128, 8], F32); dtb = sb("dtb", [128, 8], F32); negA = sb("negA", [128, 8], F32)
        cw = sb("cw", [128, 24, 4], F32)
        flg = sb("flg", [128, 2], F32); cmask = sb("cmask", [128, 256], F32)
        invf = sb("invf", [128, 16], F32)
        cc = sb("cc", [128, NT, 32], F32); ns = sb("ns", [128, NT, 32], F32)
        cci = sb("cci", [128, NT, 16], F32); nsi = sb("nsi", [128, NT, 16], F32)
        cco = sb("cco", [128, NOWN, 32], F32); nso = sb("nso", [128, NOWN, 32], F32)
        ccio = sb("ccio", [128, NOWN, 16], F32); nsio = sb("nsio", [128, NOWN, 16], F32)
        kT_res = sb("kT_res", [128, 2, T], BF16)
        v1_res = sb("v1_res", [128, NT, 2, 130], BF16)
        kiT_res = sb("kiT_res", [128, T], BF16)
        halo = sb("halo", [128, 24, 4], F32)
        S_f = sb("S_f", [128, 8, 128], F32); S_b = sb("S_b", [128, 8, 128], BF16)
        xt = sb("xt", [128, D], F32); xs = sb("xs", [128, D], BF16)
        sml = sb("sml", [128, 64], F32)
        scrA = sb("scrA", [128, 8192], BF16)
        scrB = sb("scrB", [128, 4096], BF16)
        scrC = sb("scrC", [128, 4096], BF16)
        wbuf = [sb(f"wbuf{i}", [128, KC, 256], BF16) for i in range(3)]
        qT = sb("qT", [128, 8, TM], BF16); kT = sb("kT", [128, 8, TM], BF16); vT = sb("vT", [128, 8, TM], BF16)
        gzs = sb("gzs", [128, TPM, 1024], BF16)
        gat = sb("gat", [128, TPM, 8], F32); bet = sb("bet", [128, TPM, 8], F32)
        qTo = sb("qTo", [128, 2, 8, 128], BF16); azs = sb("azs", [128, 2, 1024], BF16)
        qiT = sb("qiT", [128, 2, 8, 128], BF16); wq = sb("wq", [128, 2, 16], F32)
        oa_sel = sb("oa_sel", [128, 2, 1024], BF16)
        otok = sb("otok", [128, D], BF16)
        mixT = sb("mixT", [128, KC, 256], BF16)
        gsm = sb("gsm", [128, 64], F32)

        hT = scrA[:, :].rearrange("p (k t) -> p k t", k=KC)
        hTo = scrB[:, :].rearrange("p (k t) -> p k t", k=KC)
        sc = scrA[:, :].bitcast(F32)
        mb = scrB[:, :]
        yres = scrA[:, :].bitcast(F32).rearrange("p (o d) -> p o d", o=2)
        scrCf = scrC[:, :].bitcast(F32)
        raw = [scrCf[:, 0:516], scrCf[:, 516:1032]]
        cacc = [xt[:, 0:512], xt[:, 512:1024]]
        rl = cacc
        silt = xt[:, 1024:1536]
        PT = [xs[:, 0:512], xs[:, 512:1024]]
        rtt = xt[:, 1536:2048]; sqt = sb("sqt", [128, 512], BF16)

        P.dma("sp", cst[:], cst_d, [], ["cst"])
        P.dma("sp", gn[:], gn_d, [], ["gn"])
        P.dma("sp", gdng[:], gdng_d.partition_broadcast(128), [], ["gdng"])
        P.dma("sp", alog[:], alog_d.partition_broadcast(128), [], ["alog"])
        P.dma("sp", dtb[:], dtb_d.partition_broadcast(128), [], ["dtb"])
        P.dma("sp", cw[:], cw_d.rearrange("p (c i) -> p c i", i=4), [], ["cw"])
        P.dma("sp", flg[:], flg_d, [], ["flg"])
        P.dma("sp", cmask[:], cmask_d, [], ["cmask"])
        P.dma("sp", invf[:], invf_d.partition_broadcast(128), [], ["invf"])
        P.copy("dve", ident_b[:], ident_f, ["cst"], ["ident_b"])
        P.copy("dve", ones_b[:], ones_f, ["cst"], ["ones_b"])
        P.copy("dve", identrep[:], ident_f.unsqueeze(1).broadcast_to([128, 4, 128]), ["cst"], ["identrep"])
        P.act(negA[:], alog[:], AF.Exp, ["alog"], ["negA"])
        P.ts("dve", negA[:], negA[:], -1.0, None, ALU.mult, ["negA"], ["negA"])
        P.memset("pool", halo[:], 0.0, ["halo"])
        P.memset("pool", S_f[:], 0.0, ["S_f"])
        P.memset("pool", S_b[:], 0.0, ["S_b"])
        P.memset("pool", v1_res[:], 1.0, ["v1_res"])

        A32s = scrA[:, :].bitcast(F32)

        def make_tables(pos_dram, ntl, cc_, ns_, cci_, nsi_, tag):
            pi_ = sb("pi" + tag, [128, ntl], I32); pf = sb("pf" + tag, [128, ntl], F32)
            ne = ntl * 32
            AR = A32s[:, 0:ne].rearrange("p (t a f) -> p t a f", a=2, f=16)
            KI = A32s[:, 1024:1024 + ne].bitcast(I32).rearrange("p (t a f) -> p t a f", a=2, f=16)
            KF = A32s[:, 2048:2048 + ne].rearrange("p (t a f) -> p t a f", a=2, f=16)
            tag = ""
            P.dma("sp", pi_[:], pos_dram, [], ["pi" + tag])
            P.copy("dve", pf[:], pi_[:], ["pi" + tag], ["pf" + tag])
            P.tt("dve", AR[:, :, 0, :], pf[:].unsqueeze(2).broadcast_to([128, ntl, 16]),
                 invf[:].unsqueeze(1).broadcast_to([128, ntl, 16]), ALU.mult, ["pf" + tag, "invf"], ["AR" + tag])
            P.ts("dve", AR[:, :, 1, :], AR[:, :, 0, :], math.pi / 2, None, ALU.add, ["AR" + tag], ["AR" + tag])
            P.ts("dve", KF[:], AR[:], 1.0 / TWO_PI, None, ALU.mult, ["AR" + tag], ["KF" + tag])
            P.copy("dve", KI[:], KF[:], ["KF" + tag], ["KI" + tag])
            P.copy("dve", KF[:], KI[:], ["KI" + tag], ["KF" + tag])
            P.stt(AR[:], KF[:], -TWO_PI, AR[:], ALU.mult, ALU.add, ["KF" + tag, "AR" + tag], ["AR" + tag])
            P.ts("dve", AR[:], AR[:], 3.14159, -3.14159, ALU.min, ["AR" + tag], ["AR" + tag], op1=ALU.max)
            P.act(AR[:], AR[:], AF.Sin, ["AR" + tag], ["AR" + tag])
            k = "AR" + tag
            P.copy("dve", cc_[:, :, 0:16], AR[:, :, 1, :], [k], [("cc", tag)])
            P.copy("dve", cc_[:, :, 16:32], AR[:, :, 1, :], [k], [("cc", tag)])
            P.ts("dve", ns_[:, :, 0:16], AR[:, :, 0, :], -1.0, None, ALU.mult, [k], [("ns", tag)])
            P.copy("dve", ns_[:, :, 16:32], AR[:, :, 0, :], [k], [("ns", tag)])
            P.copy("dve", cci_[:, :, 0:8], AR[:, :, 1, 0:16:2], [k], [("cci", tag)])
            P.copy("dve", cci_[:, :, 8:16], AR[:, :, 1, 0:16:2], [k], [("cci", tag)])
            P.ts("dve", nsi_[:, :, 0:8], AR[:, :, 0, 0:16:2], -1.0, None, ALU.mult, [k], [("nsi", tag)])
            P.copy("dve", nsi_[:, :, 8:16], AR[:, :, 0, 0:16:2], [k], [("nsi", tag)])

        make_tables(pos_d, NT, cc, ns, cci, nsi, "a")
        P.barrier()
        make_tables(poso_d, NOWN, cco, nso, ccio, nsio, "o")
        P.barrier()

        wstate = {"i": 0}

        def wload(src, c0, n):
            i = wstate["i"] % 3
            wstate["i"] += 1
            for h in range(2):
                P.dma("pool", wbuf[i][:, h * 8:(h + 1) * 8, 0:n], src[:, h * 8:(h + 1) * 8, c0:c0 + n], [], [("wb", i)])
            return i

        def stream(blocks, body, ahead=2):
            loaded = []
            nb = len(blocks)
            for j in range(min(ahead, nb)):
                loaded.append(wload(*blocks[j][:3]))
            for j in range(nb):
                if j + ahead < nb:
                    loaded.append(wload(*blocks[j + ahead][:3]))
                body(loaded[j], blocks[j][3])

        bank_rr = {"i": 0}

        def nbank(lo=0, hi=8):
            b = lo + bank_rr["i"] % (hi - lo)
            bank_rr["i"] += 1
            return b

        def norm_tile(src_rows, hview, hkey, col0):
            P.dma("sp", xt[:], src_rows, [], ["xt"])
            P.act(xs[:], xt[:], AF.Square, ["xt"], ["xs", "ssq"], accum_out=sml[:, 0:1])
            P.ts("dve", sml[:, 1:2], sml[:, 0:1], 1.0 / D, EPS, ALU.mult, ["ssq"], ["ssq1"], op1=ALU.add)
            P.act(sml[:, 2:3], sml[:, 1:2], AF.Sqrt, ["ssq1"], ["ssq2"])
            P.add("dve", lambda e: e.reciprocal(out=sml[:, 3:4], in_=sml[:, 2:3]), ["ssq2"], ["rstd"])
            P.ts("dve", xs[:], xt[:], sml[:, 3:4], None, ALU.mult, ["xt", "rstd"], ["xs"])
            for q in range(4):
                b = nbank()
                pb = psb(b)
                P.add("pe", [P.tr(pb[:, j * 128:(j + 1) * 128], xs[:, (4 * q + j) * 128:(4 * q + j + 1) * 128], ident_b[:]) for j in range(4)],
                      ["xs", "ident_b"], [psk(b)])
                P.tt("dve", hview[:, 4 * q:4 * q + 4, col0:col0 + 128], pb[:, 0:512].rearrange("p (j t) -> p j t", j=4),
                     gn[:, 4 * q:4 * q + 4].unsqueeze(2).broadcast_to([128, 4, 128]), ALU.mult, [psk(b), "gn"], [hkey])

        DBG = 99

        def rope(dst, src, H, half, cct, nst, rk, wk, dst2=None, src2=None):
            h2 = 2 * half
            W = dst2.shape[1]
            dh = W // H
            rf = xs[:, :].bitcast(F32)[:, 0:W]
            if DBG >= 1:
                P.copy("act", rf, src2, rk, ["rf"])
            if DBG >= 2:
                P.copy("pool", dst2, rf, ["rf"], [wk])
            fa = []
            for h in range(H):
                o = h * dh
                fa.append(lambda e, h=h, o=o: e.tensor_tensor(out=ra[:, h, 0:h2], in0=rf[:, o:o + h2], in1=cct, op=ALU.mult))
                fa.append(lambda e, h=h, o=o: e.tensor_tensor(out=rb[:, h, 0:half], in0=rf[:, o + half:o + h2], in1=nst[:, 0:half], op=ALU.mult))
                fa.append(lambda e, h=h, o=o: e.tensor_tensor(out=rb[:, h, half:h2], in0=rf[:, o:o + half], in1=nst[:, half:h2], op=ALU.mult))
            if DBG >= 3:
                P.add("dve", fa, ["rf", "tabs"], ["ra", "rb"])
            fb = [(lambda e, h=h, o=h * dh: e.tensor_tensor(out=dst2[:, o:o + h2], in0=ra[:, h, 0:h2], in1=rb[:, h, 0:h2], op=ALU.add)) for h in range(H)]
            if DBG >= 4:
                P.add("dve", fb, ["ra", "rb", wk], [wk])

        for m in range(n_macro if stop > 0 else 0):
            for tt_ in range(TPM):
                r0 = m * TM + tt_ * 128
                norm_tile(x_d[r0:r0 + 128, :], hT, "hT", tt_ * 128)
            for ot in range(2):
                r0 = (2 * m + ot) * 128
                norm_tile(xo_d[r0:r0 + 128, :], hTo, "hTo", ot * 128)
            P.barrier()
            if stop <= 1:
                break

            blocks = []
            for blk in range(12):
                blocks.append((wall_d, C_QKV + blk * 256, 256, ("qkv", blk)))
            for blk in range(4):
                blocks.append((wall_d, C_GZ + blk * 256, 256, ("gz", blk)))
            blocks.append((wall_d, C_KV, 256, ("ak", 0)))
            blocks.append((wall_d, C_KV + 256, 256, ("av", 0)))
            blocks.append((wall_d, C_SM, 80, ("sm", 0)))
            for blk in range(4):
                blocks.append((wall_d, C_AQ + blk * 256, 256, ("aq", blk)))
            for blk in range(4):
                blocks.append((wall_d, C_AZ + blk * 256, 256, ("az", blk)))
            for blk in range(4):
                blocks.append((wall_d, C_IQ + blk * 256, 256, ("iq", blk)))
            blocks.append((wall_d, C_IW, 16, ("iw", 0)))

            def tok_mm(b, hv, hk, col0, wi, n):
                P.add("pe", [P.mm(ps[b][:, 0:n], hv[:, kc, col0:col0 + 128], wbuf[wi][:, kc, 0:n], kc == 0, kc == KC - 1) for kc in range(KC)],
                      [hk, ("wb", wi)], [psk(b)])

            def body2(wi, info):
                kind, blk = info
                if kind == "qkv":
                    for c2 in range(2):
                        ct = blk * 2 + c2
                        b = nbank()
                        P.add("pe", [P.mm(ps[b][:, :], wbuf[wi][:, kc, c2 * 128:(c2 + 1) * 128], hT[:, kc, :], kc == 0, kc == KC - 1) for kc in range(KC)],
                              ["hT", ("wb", wi)], [psk(b)])
                        rw = raw[ct % 2]; rk = ("raw", ct % 2); ac = cacc[ct % 2]; ak_ = ("cacc", ct % 2)
                        P.copy("act", rw[:, 3:515], ps[b][:, :], [psk(b)], [rk])
                        P.copy("pool", rw[:, 0:3], halo[:, ct, 0:3], [("halo", ct)], [rk])
                        P.copy("pool", halo[:, ct, 0:3], rw[:, 512:515], [rk], [("halo", ct)])
                        P.ts("dve", ac, rw[:, 3:515], cw[:, ct, 3:4], None, ALU.mult, [rk, "cw"], [ak_])
                        for i in (2, 1, 0):
                            P.stt(ac, rw[:, i:i + 512], cw[:, ct, i:i + 1], ac, ALU.mult, ALU.add, [rk, "cw", ak_], [ak_])
                        if ct < 16:
                            isq = ct < 8; hd = ct % 8
                            P.act(silt, ac, AF.Silu, [ak_], ["silt"])
                            P.tt("pool", sqt[:], silt, silt, ALU.mult, ["silt"], ["sqt"])
                            b2 = nbank()
                            P.add("pe", [P.mm(ps[b2][:, :], ones_b[:], sqt[:], True, True)], ["sqt", "ones_b"], [psk(b2)])
                            P.act(rtt, ps[b2][:, :], AF.Sqrt, [psk(b2)], ["rtt"], scale=(128.0 if isq else 1.0), bias=(128.0 * EPS if isq else EPS))
                            P.add("dve", lambda e: e.reciprocal(out=rtt, in_=rtt), ["rtt"], ["rtt"])
                            dstT = qT if isq else kT
                            P.tt("dve", dstT[:, hd, :], silt, rtt, ALU.mult, ["silt", "rtt"], [("qT" if isq else "kT", hd)])
                        else:
                            P.act(vT[:, ct - 16, :], ac, AF.Silu, [ak_], [("vT", ct - 16)])
                elif kind == "gz":
                    for tt_ in range(TPM):
                        b = nbank()
                        tok_mm(b, hT, "hT", tt_ * 128, wi, 256)
                        P.act(gzs[:, tt_, blk * 256:(blk + 1) * 256], ps[b][:, 0:256], AF.Silu, [psk(b)], [("gzs", tt_)])
                elif kind == "ak":
                    for tt_ in range(TPM):
                        b = nbank(); gt = m * TPM + tt_
                        tok_mm(b, hT, "hT", tt_ * 128, wi, 256)
                        LV = 9 if DBG >= 6 else (3 if DBG >= 5 else 2)
                        kv_ = rtmp[:, :].rearrange("p (h d) -> p h d", h=2)
                        if LV >= 2:
                            rope(kv_, ps[b][:, 0:256].rearrange("p (h d) -> p h d", h=2), 2, 16, cc[:, gt, :], ns[:, gt, :], [psk(b)], "rtmp", rtmp[:, 0:256], ps[b][:, 0:256])
                        elif LV >= 1:
                            P.copy("act", rtmp[:, 0:256], ps[b][:, 0:256], [psk(b)], ["rtmp"])
                        b2 = nbank(); pb = psb(b2)
                        if LV >= 3:
                            P.add("pe", [P.tr(pb[:, g * 128:(g + 1) * 128], rtmp[:, g * 128:(g + 1) * 128], ident_b[:]) for g in range(2)], ["rtmp", "ident_b"], [psk(b2)])
                        if LV >= 4:
                            for g in range(2):
                                if DBG == 7 and g == 1:
                                    continue
                                if DBG == 8 and g == 0:
                                    continue
                                eng_ = "dve" if g == 0 else "act"
                                if DBG == 9:
                                    eng_ = "dve"
                                P.copy(eng_, kT_res[:, g, gt * 128:(gt + 1) * 128], pb[:, g * 128:(g + 1) * 128], [psk(b2)], [("kT_res", gt, g)])
                elif kind == "av":
                    for tt_ in range(TPM):
                        b = nbank(); gt = m * TPM + tt_
                        tok_mm(b, hT, "hT", tt_ * 128, wi, 256)
                        for g in range(2):
                            P.copy("dve" if g == 0 else "act", v1_res[:, gt, g, 0:128], ps[b][:, g * 128:(g + 1) * 128], [psk(b)], [("v1_res", gt, g)])
                elif kind == "sm":
                    for tt_ in range(TPM):
                        b = nbank(); gt = m * TPM + tt_
                        tok_mm(b, hT, "hT", tt_ * 128, wi, 80)
                        pk = [psk(b)]
                        P.tt("dve", gsm[:, 0:8], ps[b][:, 0:8], dtb[:], ALU.add, pk + ["dtb"], ["g0"])
                        P.act(gsm[:, 8:16], gsm[:, 0:8], AF.Abs, ["g0"], ["g1"])
                        P.act(gsm[:, 16:24], gsm[:, 8:16], AF.Exp, ["g1"], ["g2"], scale=-1.0)
                        P.act(gsm[:, 24:32], gsm[:, 16:24], AF.Ln, ["g2"], ["g3"], bias=1.0)
                        P.stt(gsm[:, 32:40], gsm[:, 0:8], 0.0, gsm[:, 24:32], ALU.max, ALU.add, ["g0", "g3"], ["g4"])
                        P.tt("dve", gat[:, tt_, :], gsm[:, 32:40], negA[:], ALU.mult, ["g4", "negA"], [("gat", tt_)])
                        P.act(bet[:, tt_, :], ps[b][:, 8:16], AF.Sigmoid, pk, [("bet", tt_)])
                        ikv = rtmp[:, 0:64].rearrange("p (h d) -> p h d", h=1)
                        rope(ikv, ps[b][:, 16:80].rearrange("p (h d) -> p h d", h=1), 1, 8, cci[:, gt, :], nsi[:, gt, :], pk, "rtmp", rtmp[:, 0:64], ps[b][:, 16:80])
                        P.copy("pool", rtmp[:, 64:128], rtmp[:, 0:64], ["rtmp"], ["rtmp"])
                        b2 = nbank(); pb = psb(b2)
                        P.add("pe", [P.tr(pb[:, 0:128], rtmp[:, 0:128], ident_b[:])], ["rtmp", "ident_b"], [psk(b2)])
                        P.copy("dve", kiT_res[:, gt * 128:(gt + 1) * 128], pb[:, 0:128], [psk(b2)], [("kiT_res", gt)])
                elif kind == "aq":
                    for ot in range(2):
                        b = nbank(); oi = 2 * m + ot
                        tok_mm(b, hTo, "hTo", ot * 128, wi, 256)
                        qv = rtmp[:, :].rearrange("p (h d) -> p h d", h=2)
                        rope(qv, ps[b][:, 0:256].rearrange("p (h d) -> p h d", h=2), 2, 16, cco[:, oi, :], nso[:, oi, :], [psk(b)], "rtmp", rtmp[:, 0:256], ps[b][:, 0:256])
                        b2 = nbank(); pb = psb(b2)
                        P.add("pe", [P.tr(pb[:, g * 128:(g + 1) * 128], rtmp[:, g * 128:(g + 1) * 128], ident_b[:]) for g in range(2)], ["rtmp", "ident_b"], [psk(b2)])
                        P.copy("dve", qTo[:, ot, 2 * blk:2 * blk + 2, :].rearrange("p g t -> p (g t)"), pb[:, 0:256], [psk(b2)], [("qTo", ot)])
                elif kind == "az":
                    for ot in range(2):
                        b = nbank()
                        tok_mm(b, hTo, "hTo", ot * 128, wi, 256)
                        P.act(azs[:, ot, blk * 256:(blk + 1) * 256], ps[b][:, 0:256], AF.Silu, [psk(b)], [("azs", ot)])
                elif kind == "iq":
                    for ot in range(2):
                        b = nbank(); oi = 2 * m + ot
                        tok_mm(b, hTo, "hTo", ot * 128, wi, 256)
                        qv = rtmp[:, :].rearrange("p (h d) -> p h d", h=4)
                        rope(qv, ps[b][:, 0:256].rearrange("p (h d) -> p h d", h=4), 4, 8, ccio[:, oi, :], nsio[:, oi, :], [psk(b)], "rtmp", rtmp[:, 0:256], ps[b][:, 0:256])
                        b2 = nbank(); pb = psb(b2)
                        P.add("pe", [P.tr(pb[:, g * 128:(g + 1) * 128], rtmp[:, g * 128:(g + 1) * 128], ident_b[:]) for g in range(2)], ["rtmp", "ident_b"], [psk(b2)])
                        P.copy("dve", qiT[:, ot, 2 * blk:2 * blk + 2, :].rearrange("p g t -> p (g t)"), pb[:, 0:256], [psk(b2)], [("qiT", ot)])
                elif kind == "iw":
                    for ot in range(2):
                        b = nbank()
                        tok_mm(b, hTo, "hTo", ot * 128, wi, 16)
                        P.ts("dve", wq[:, ot, :], ps[b][:, 0:16], 1.0 / 32.0, None, ALU.mult, [psk(b)], [("wq", ot)])

            stream(blocks[:nblk], body2)
            P.barrier()
            if stop <= 2:
                break

            A32 = scrA[:, :].bitcast(F32)
            B32 = scrB[:, :].bitcast(F32)

            def tmpl(hg):
                o = hg * 2048
                t = {}
                t["Z"] = A32[:, o:o + 512]; t["E1"] = A32[:, o + 512:o + 1024]; t["E2"] = A32[:, o + 1024:o + 1536]
                t["gU"] = A32[:, o + 1536:o + 2048]
                ob = hg * 2048
                for i, nme in enumerate(["Mc", "Nc", "Pc", "AT"]):
                    t[nme] = scrB[:, ob + i * 512: ob + (i + 1) * 512]
                return t
            gtmp = [scrC[:, :].rearrange("p (a b) -> p a b", a=8), mixT[:, :, :].rearrange("p k t -> p (k t)").rearrange("p (a b) -> p a b", a=8)]

            def gdn_chunk_prep(tt_):
                b = nbank()
                P.add("pe", [P.mm(ps[b][:, 0:8], U_f, gat[:, tt_, :], True, True), P.mm(ps[b][:, 8:16], ones_f, gat[:, tt_, :], True, True)],
                      [("gat", tt_), "cst"], [psk(b)])
                P.copy("dve", gsm[:, 0:16], ps[b][:, 0:16], [psk(b)], ["gc"])
                P.act(gsm[:, 16:32], gsm[:, 0:16], AF.Exp, ["gc"], ["eg"])
                P.tt("dve", gsm[:, 32:40], gsm[:, 8:16], gsm[:, 0:8], ALU.subtract, ["gc"], ["ekl"])
                P.act(gsm[:, 32:40], gsm[:, 32:40], AF.Exp, ["ekl"], ["ek"])
                P.tt("dve", gsm[:, 40:48], bet[:, tt_, :], gsm[:, 16:24], ALU.mult, [("bet", tt_), "eg"], ["bk"])
                P.ts("dve", gsm[:, 48:56], bet[:, tt_, :], -1.0, None, ALU.mult, [("bet", tt_)], ["nbet"])

            def bc(ap8, hg):
                return ap8[:, 4 * hg:4 * hg + 4].unsqueeze(2).broadcast_to([128, 4, 128])

            def v3(ap):
                return ap.rearrange("p (h d) -> p h d", h=4)

            def gdn_hg(tt_, hg):
                t = tmpl(hg); K = lambda n: (n, hg)
                G = gtmp[hg]
                ktok = G[:, 0, :]; vtok = G[:, 1, :]; qg = G[:, 2, :]; kbg = G[:, 3, :]; vb = G[:, 4, :]; wTn = G[:, 5, :]; vn = G[:, 6, :]; kt2 = G[:, 7, :]
                tok = slice(tt_ * 128, (tt_ + 1) * 128)
                hs = range(4 * hg, 4 * hg + 4)
                gc = gsm[:, 0:8]; eg = gsm[:, 16:24]; egl = gsm[:, 24:32]; ek = gsm[:, 32:40]; bk = gsm[:, 40:48]; nbet = gsm[:, 48:56]
                base = 4 * hg
                bA, bB, bC, bD = base, base + 1, base + 2, base + 3
                pb = psb(bA)
                P.add("pe", [P.tr(pb[:, j * 128:(j + 1) * 128], kT[:, 4 * hg + j, tok], ident_b[:]) for j in range(4)], [("kT", h) for h in hs] + ["ident_b"], [psk(bA)])
                P.copy("act", ktok, pb[:, 0:512], [psk(bA)], [K("ktok")])
                pb2 = psb(bB)
                P.add("pe", [P.tr(pb2[:, j * 128:(j + 1) * 128], vT[:, 4 * hg + j, tok], ident_b[:]) for j in range(4)], [("vT", h) for h in hs] + ["ident_b"], [psk(bB)])
                P.copy("dve", vtok, pb2[:, 0:512], [psk(bB)], [K("vtok")])
                yield
                for j in range(4):
                    P.ts("pool", t["gU"][:, j * 128:(j + 1) * 128], U_f, gat[:, tt_, 4 * hg + j:4 * hg + j + 1], None, ALU.mult, ["cst", ("gat", tt_)], [K("gU")])
                P.add("pe", [P.mm(ps[bC][:, :], ones_f, t["gU"], True, True)], [K("gU"), "cst"], [psk(bC)])
                P.tt("dve", v3(t["Z"]), v3(ps[bC][:, :]), bc(gc, hg), ALU.subtract, [psk(bC), "gc"], [K("Z")])
                P.act(t["gU"], ps[bC][:, :], AF.Exp, [psk(bC)], [K("gU")])
                yield
                P.ts("pool", t["E1"], t["Z"], 0.0, None, ALU.min, [K("Z")], [K("E1")])
                P.ts("dve", t["E2"], t["Z"], 0.0, None, ALU.max, [K("Z")], [K("E2")])
                P.act(t["E1"], t["E1"], AF.Exp, [K("E1")], [K("E1")])
                P.act(t["E2"], t["E2"], AF.Exp, [K("E2")], [K("E2")], scale=-1.0)
                for j in range(4):
                    P.tt("pool", t["E1"][:, j * 128:(j + 1) * 128], t["E1"][:, j * 128:(j + 1) * 128], U_f, ALU.mult, [K("E1"), "cst"], [K("E1")])
                for j in range(4):
                    P.tt("dve", t["E2"][:, j * 128:(j + 1) * 128], t["E2"][:, j * 128:(j + 1) * 128], Ls_f, ALU.mult, [K("E2"), "cst"], [K("E2")])
                P.tt("dve", v3(qg), qT[:, 4 * hg:4 * hg + 4, tok], v3(t["gU"]), ALU.mult, [("qT", h) for h in hs] + [K("gU")], [K("qg")])
                yield
                P.add("pe", [P.mm(ps[bA][:, j * 128:(j + 1) * 128], kT[:, 4 * hg + j, tok], kT[:, 4 * hg + j, tok], True, True) for j in range(4)], [("kT", h) for h in hs], [psk(bA)])
                P.add("pe", [P.mm(ps[bB][:, j * 128:(j + 1) * 128], kT[:, 4 * hg + j, tok], qT[:, 4 * hg + j, tok], True, True) for j in range(4)], [("kT", h) for h in hs] + [("qT", h) for h in hs], [psk(bB)])
                P.tt("dve", v3(t["Z"]), v3(ps[bA][:, :]), bc(nbet, hg), ALU.mult, [psk(bA), "nbet"], [K("Z")])
                P.tt("dve", t["Mc"], t["Z"], t["E2"], ALU.mult, [K("Z"), K("E2")], [K("Mc")])
                P.tt("dve", t["AT"], ps[bB][:, :], t["E1"], ALU.mult, [psk(bB), K("E1")], [K("AT")])
                yield
                pb = psb(bC)
                P.add("pe", [P.tr(pb[:, j * 128:(j + 1) * 128], t["Mc"][:, j * 128:(j + 1) * 128], ident_b[:]) for j in range(4)], [K("Mc"), "ident_b"], [psk(bC)])
                P.copy("act", t["Nc"], pb[:, 0:512], [psk(bC)], [K("Nc")])
                for j in range(4):
                    P.tt("dve", t["Pc"][:, j * 128:(j + 1) * 128], pb[:, j * 128:(j + 1) * 128], ident_f, ALU.add, [psk(bC), "cst"], [K("Pc")])
                yield
                Mn = G[:, 5, :]; Nn = G[:, 6, :]
                Mc, Nc, Pc = t["Mc"], t["Nc"], t["Pc"]
                kM, kN, kP = K("Mc"), K("Nc"), K("Pc")
                kMn, kNn = K("wTn"), K("vn")
                for lvl in range(1, 7):
                    P.add("pe", [P.mm(ps[bA][:, j * 128:(j + 1) * 128], Nc[:, j * 128:(j + 1) * 128], Mc[:, j * 128:(j + 1) * 128], True, True) for j in range(4)], [kM, kN], [psk(bA)])
                    if lvl < 6:
                        P.add("pe", [P.mm(ps[bB][:, j * 128:(j + 1) * 128], Mc[:, j * 128:(j + 1) * 128], Nc[:, j * 128:(j + 1) * 128], True, True) for j in range(4)], [kM, kN], [psk(bB)])
                    P.copy("act", Mn, ps[bA][:, :], [psk(bA)], [kMn])
                    if lvl < 6:
                        P.copy("dve", Nn, ps[bB][:, :], [psk(bB)], [kNn])
                    yield
                    fns = []
                    for j in range(4):
                        fns.append(P.mm(ps[bC][:, j * 128:(j + 1) * 128], ident_b[:], Pc[:, j * 128:(j + 1) * 128], True, False))
                        fns.append(P.mm(ps[bC][:, j * 128:(j + 1) * 128], Mn[:, j * 128:(j + 1) * 128], Pc[:, j * 128:(j + 1) * 128], False, True))
                    P.add("pe", fns, [kMn, kP, "ident_b"], [psk(bC)])
                    P.copy("pool" if False else "dve", Pc, ps[bC][:, :], [psk(bC)], [kP])
                    Mc, Mn = Mn, Mc; kM, kMn = kMn, kM
                    Nc, Nn = Nn, Nc; kN, kNn = kNn, kN
                    yield
                TT = Pc
                P.tt("dve", v3(kbg), v3(ktok), bc(bk, hg), ALU.mult, [K("ktok"), "bk"], [K("kbg")])
                P.tt("pool", v3(vb), v3(vtok), bc(bet[:, tt_, :], hg), ALU.mult, [K("vtok"), ("bet", tt_)], [K("vb")])
                P.tt("pool", v3(kt2), v3(ktok), bc(ek, hg), ALU.mult, [K("ktok"), "ek"], [K("kt2")])
                yield
                P.add("pe", [P.mm(ps[bA][:, j * 128:(j + 1) * 128], kbg[:, j * 128:(j + 1) * 128], TT[:, j * 128:(j + 1) * 128], True, True) for j in range(4)], [K("kbg"), kP], [psk(bA)])
                P.ts("dve", wTn, ps[bA][:, :], -1.0, None, ALU.mult, [psk(bA)], [K("wTn")])
                yield
                fns = []
                for j in range(4):
                    fns.append(P.mm(ps[bB][:, j * 128:(j + 1) * 128], TT[:, j * 128:(j + 1) * 128], vb[:, j * 128:(j + 1) * 128], True, False))
                    fns.append(P.mm(ps[bB][:, j * 128:(j + 1) * 128], wTn[:, j * 128:(j + 1) * 128], S_b[:, 4 * hg + j, :], False, True))
                P.add("pe", fns, [kP, K("vb"), K("wTn"), ("S_b", hg)], [psk(bB)])
                P.copy("act", vn, ps[bB][:, :], [psk(bB)], [K("vn")])
                yield
                fns = []
                for j in range(4):
                    fns.append(P.mm(ps[bC][:, j * 128:(j + 1) * 128], qg[:, j * 128:(j + 1) * 128], S_b[:, 4 * hg + j, :], True, False))
                    fns.append(P.mm(ps[bC][:, j * 128:(j + 1) * 128], t["AT"][:, j * 128:(j + 1) * 128], vn[:, j * 128:(j + 1) * 128], False, True))
                P.add("pe", fns, [K("qg"), K("AT"), K("vn"), ("S_b", hg)], [psk(bC)])
                P.add("pe", [P.mm(ps[bD][:, j * 128:(j + 1) * 128], kt2[:, j * 128:(j + 1) * 128], vn[:, j * 128:(j + 1) * 128], True, True) for j in range(4)], [K("kt2"), K("vn")], [psk(bD)])
                Sv = S_f[:, 4 * hg:4 * hg + 4, :]
                P.tt("pool", Sv, Sv, bc(egl, hg), ALU.mult, [("S_f", hg), "eg"], [("S_f", hg)])
                P.tt("dve", Sv, Sv, v3(ps[bD][:, :]), ALU.add, [("S_f", hg), psk(bD)], [("S_f", hg)])
                yield
                P.copy("act", S_b[:, 4 * hg:4 * hg + 4, :].rearrange("p h d -> p (h d)"), Sv.rearrange("p h d -> p (h d)"), [("S_f", hg)], [("S_b", hg)])
                o32 = t["Z"]
                P.copy("act", o32, ps[bC][:, :], [psk(bC)], [K("Z")])
                P.tt("pool", t["E1"], o32, o32, ALU.mult, [K("Z")], [K("E1")])
                sm = gsm[:, 56:60] if hg == 0 else gsm[:, 60:64]
                P.add("dve", lambda e: e.tensor_reduce(out=sm, in_=v3(t["E1"]), axis=AX.X, op=ALU.add), [K("E1")], [K("osq")])
                yield
                P.ts("dve", sm, sm, 1.0 / 128.0, EPS, ALU.mult, [K("osq")], [K("osq")], op1=ALU.add)
                P.act(sm, sm, AF.Sqrt, [K("osq")], [K("osq")])
                P.add("dve", lambda e: e.reciprocal(out=sm, in_=sm), [K("osq")], [K("osq")])
                P.tt("dve", v3(o32), v3(o32), sm.unsqueeze(2).broadcast_to([128, 4, 128]), ALU.mult, [K("Z"), K("osq")], [K("Z")])
                for j in range(4):
                    P.tt("pool", o32[:, j * 128:(j + 1) * 128], o32[:, j * 128:(j + 1) * 128], gdng[:], ALU.mult, [K("Z"), "gdng"], [K("Z")])
                P.tt("dve", o32, o32, gzs[:, tt_, hg * 512:(hg + 1) * 512], ALU.mult, [K("Z"), ("gzs", tt_)], [K("Z")])
                ot = tt_ // 2
                dsl = oa_sel[:, ot, hg * 512:(hg + 1) * 512]
                if tt_ % 2 == 0:
                    P.ts("dve", dsl, o32, flg[:, 0:1], None, ALU.mult, [K("Z"), "flg"], [("oa_sel", ot, hg)])
                else:
                    P.stt(dsl, o32, flg[:, 1:2], dsl, ALU.mult, ALU.add, [K("Z"), "flg", ("oa_sel", ot, hg)], [("oa_sel", ot, hg)])
                yield

            for tt_ in range(TPM):
                gdn_chunk_prep(tt_)
                gens = [gdn_hg(tt_, 0), gdn_hg(tt_, 1)]
                alive = [True, True]
                while any(alive):
                    for gi in range(2):
                        if alive[gi]:
                            try:
                                next(gens[gi])
                            except StopIteration:
                                alive[gi] = False
            P.barrier()
            if stop <= 3:
                break

            for ot in range(2):
                oi = 2 * m + ot
                S_i = 256 * (oi + 1)
                nkb = (S_i + 511) // 512
                for kb in range(nkb):
                    k0 = kb * 512; n = min(512, S_i - k0)
                    for pr in range(8):
                        b0 = 2 * (pr % 2); b1 = b0 + 1
                        P.add("pe", [P.mm(ps[b0][:, 0:n], qiT[0:64, ot, pr, :], kiT_res[0:64, k0:k0 + n], True, True)], [("qiT", ot), "kiT_all"], [psk(b0)])
                        P.add("pe", [P.mm(ps[b1][:, 0:n], qiT[64:128, ot, pr, :], kiT_res[64:128, k0:k0 + n], True, True)], [("qiT", ot), "kiT_all"], [psk(b1)])
                        for hh_, bb in ((2 * pr, b0), (2 * pr + 1, b1)):
                            r_ = rl[hh_ % 2]; rk = ("rl", hh_ % 2)
                            P.act(r_[:, 0:n], ps[bb][:, 0:n], AF.Relu, [psk(bb)], [rk])
                            if hh_ == 0:
                                P.ts("dve", sc[:, k0:k0 + n], r_[:, 0:n], wq[:, ot, 0:1], None, ALU.mult, [rk, ("wq", ot)], [("sc", kb)])
                            else:
                                P.stt(sc[:, k0:k0 + n], r_[:, 0:n], wq[:, ot, hh_:hh_ + 1], sc[:, k0:k0 + n], ALU.mult, ALU.add, [rk, ("wq", ot), ("sc", kb)], [("sc", kb)])
                sck = [("sc", kb) for kb in range(nkb)]
                am = sml[:, 8:9]; lo = sml[:, 9:10]; hi = sml[:, 10:11]; d0 = sml[:, 11:12]; mid = sml[:, 12:13]; cnt = sml[:, 13:14]; gg = sml[:, 14:15]
                P.add("dve", lambda e, S_i=S_i: e.tensor_reduce(out=am, in_=sc[:, 0:S_i], axis=AX.X, op=ALU.max, apply_absolute_value=True), sck, ["am"])
                P.ts("dve", lo, am, -1.0, -1.0, ALU.mult, ["am"], ["lo"], op1=ALU.add)
                P.tt("dve", sc[:, S_i - 256:S_i], sc[:, S_i - 256:S_i], cmask[:], ALU.add, sck + ["cmask"], sck)
                P.add("dve", lambda e, S_i=S_i: e.tensor_reduce(out=hi, in_=sc[:, 0:S_i], axis=AX.X, op=ALU.max), sck, ["hi"])
                P.tt("dve", d0, hi, lo, ALU.subtract, ["hi", "lo"], ["d0"])
                for it in range(NIT):
                    f = 2.0 ** (-(it + 1))
                    P.stt(mid, d0, f, lo, ALU.mult, ALU.add, ["d0", "lo"], ["mid"])
                    P.ts("dve", mb[:, 0:S_i], sc[:, 0:S_i], mid, None, ALU.is_gt, sck + ["mid"], ["mb", "cnt"], op1=ALU.add, accum_out=cnt)
                    P.ts("dve", gg, cnt, 255.5, d0, ALU.is_ge, ["cnt", "d0"], ["gg"], op1=ALU.mult)
                    P.stt(lo, gg, f, lo, ALU.mult, ALU.add, ["gg", "lo"], ["lo"])
                P.ts("dve", mb[:, 0:S_i], sc[:, 0:S_i], lo, -30000.0, ALU.is_le, sck + ["lo"], ["mb"], op1=ALU.mult)
                nsb = S_i // 128
                oacc = [ps[4], ps[5], ps[6]]
                hslot = {0: (0, 0), 1: (0, 1), 2: (0, 2), 3: (1, 0), 4: (1, 1), 5: (1, 2), 6: (2, 0), 7: (2, 1)}
                started = [False, False, False]
                scale = 128.0 ** -0.5
                it_ = 0
                for sbk in range(nsb):
                    for g in range(2):
                        bs = it_ % 2; it_ += 1
                        P.add("pe", [P.mm(ps[bs][:, :], kT_res[:, g, sbk * 128:(sbk + 1) * 128], qTo[:, ot, 4 * g:4 * g + 4, :].rearrange("p h q -> p (h q)"), True, False),
                                     P.mm(ps[bs][:, :], mb[:, sbk * 128:(sbk + 1) * 128], identrep[:].rearrange("p h q -> p (h q)"), False, True)],
                              ["kT_all", ("qTo", ot), "mb", "identrep"], [psk(bs)])
                        P.act(PT[bs], ps[bs][:, :], AF.Exp, [psk(bs)], [("PT", bs)], scale=scale)
                        fns = []
                        for j in range(4):
                            hd = 4 * g + j; bi, sl = hslot[hd]
                            stf = not started[bi]
                            started[bi] = True
                            fns.append(P.mm(oacc[bi][:, sl * 130:sl * 130 + 129], PT[bs][:, j * 128:(j + 1) * 128], v1_res[:, sbk, g, 0:129], stf, sbk == nsb - 1, skip_group_check=True))
                        P.add("pe", fns, [("PT", bs), "v1_all"], [psk(4), psk(5), psk(6)])
                den = sml[:, 16:24]
                for bi, nh in ((0, 3), (1, 3), (2, 2)):
                    ov = oacc[bi][:, 0:nh * 130].rearrange("p (h c) -> p h c", c=130)
                    P.copy("dve", den[:, 3 * bi:3 * bi + nh], ov[:, :, 128], [psk(4 + bi)], [("den", bi)])
                    P.add("dve", lambda e, bi=bi, nh=nh: e.reciprocal(out=den[:, 3 * bi:3 * bi + nh], in_=den[:, 3 * bi:3 * bi + nh]), [("den", bi)], [("den", bi)])
                    o32 = rl[0][:, 0:nh * 128].rearrange("p (h d) -> p h d", d=128)
                    P.tt("dve", o32, ov[:, :, 0:128], den[:, 3 * bi:3 * bi + nh].unsqueeze(2).broadcast_to([128, nh, 128]), ALU.mult, [psk(4 + bi), ("den", bi)], [("rl", 0)])
                    c0 = 3 * bi * 128
                    P.tt("dve", otok[:, 1024 + c0:1024 + c0 + nh * 128], rl[0][:, 0:nh * 128], azs[:, ot, c0:c0 + nh * 128], ALU.mult, [("rl", 0), ("azs", ot)], [("otok", ot)])
                P.copy("pool", otok[:, 0:1024], oa_sel[:, ot, :], [("oa_sel", ot, 0), ("oa_sel", ot, 1)], [("otok", ot)])
                if debug:
                    P.dma("sp", dbg_d[oi * 128:(oi + 1) * 128, :], otok[:], [("otok", ot)], [])
                for q in range(4):
                    b = 2 + (q % 2); pb = psb(b)
                    P.add("pe", [P.tr(pb[:, j * 128:(j + 1) * 128], otok[:, (4 * q + j) * 128:(4 * q + j + 1) * 128], ident_b[:]) for j in range(4)], [("otok", ot), "ident_b"], [psk(b)])
                    for j in range(4):
                        P.copy("dve" if j % 2 == 0 else "act", mixT[:, 4 * q + j, ot * 128:(ot + 1) * 128], pb[:, j * 128:(j + 1) * 128], [psk(b)], [("mixT", ot, 4 * q + j)])
                P.barrier()

            if stop <= 4:
                break
            for ot in range(2):
                oi = 2 * m + ot
                P.dma("sp", yres[:, ot, :], xo_d[oi * 128:(oi + 1) * 128, :], [], [("y", ot)])
            P.dma("sp", xt[:], fing_d.partition_broadcast(128), [], ["fing"])
            oblocks = [(wout_d, blk * 256, 256, ("o", blk)) for blk in range(8)]

            def body5(wi, info):
                blk = info[1]
                for ot in range(2):
                    b = nbank()
                    P.add("pe", [P.mm(ps[b][:, 0:256], mixT[:, kc, ot * 128:(ot + 1) * 128], wbuf[wi][:, kc, 0:256], kc == 0, kc == KC - 1) for kc in range(KC)],
                          [("mixT", ot, kc) for kc in range(KC)] + [("wb", wi)], [psk(b)])
                    ysl = yres[:, ot, blk * 256:(blk + 1) * 256]
                    P.tt("dve", ysl, ysl, ps[b][:, 0:256], ALU.add, [psk(b), ("y", ot)], [("y", ot)])
            stream(oblocks, body5)
            for ot in range(2):
                oi = 2 * m + ot
                yk = ("y", ot)
                P.act(xs[:], yres[:, ot, :], AF.Square, [yk], ["xs", "fs"], accum_out=sml[:, 24:25])
                P.ts("dve", sml[:, 25:26], sml[:, 24:25], 1.0 / D, EPS, ALU.mult, ["fs"], ["fs1"], op1=ALU.add)
                P.act(sml[:, 26:27], sml[:, 25:26], AF.Sqrt, ["fs1"], ["fs2"])
                P.add("dve", lambda e: e.reciprocal(out=sml[:, 27:28], in_=sml[:, 26:27]), ["fs2"], ["fs3"])
                P.stt(yres[:, ot, :], yres[:, ot, :], sml[:, 27:28], xt[:], ALU.mult, ALU.mult, [yk, "fs3", "fing"], [yk])
                P.dma("sp", out_d[oi * 128:(oi + 1) * 128, :], yres[:, ot, :], [yk], [])
            P.barrier()

        P.finish_waits("sp")
        P.emit()
    return nc


def _prep_shared(inputs):
    f32 = np.float32
    w_in = np.asarray(inputs["w_in"], f32)[0]
    o = dict(gq=0, gk=1024, gv=2048, gz=3072, ga=4096, gb=4104, aq=4112, ak=5136, av=5392, az=5648, iq=6672, ik=7696, iw=7760)
    order = [("gq", 1024), ("gk", 1024), ("gv", 1024), ("gz", 1024), ("ak", 256), ("av", 256), ("ga", 8), ("gb", 8), ("ik", 64),
             ("aq", 1024), ("az", 1024), ("iq", 1024), ("iw", 16)]
    cols = np.concatenate([np.arange(o[n], o[n] + s) for n, s in order])
    wall = np.ascontiguousarray(w_in[:, cols].reshape(KC, 128, NCOL).transpose(1, 0, 2))
    wout = np.ascontiguousarray(np.asarray(inputs["w_out"], f32)[0].reshape(KC, 128, D).transpose(1, 0, 2))
    ii = np.arange(128)
    ident = (ii[:, None] == ii[None, :]).astype(f32)
    U = (ii[:, None] <= ii[None, :]).astype(f32)
    Ls = (ii[:, None] > ii[None, :]).astype(f32)
    cst = np.ascontiguousarray(np.concatenate([ident, U, Ls, np.ones((128, 128), f32)], axis=1))
    gn = np.ascontiguousarray(np.asarray(inputs["attn_norm_g"], f32)[0].reshape(KC, 128).T)
    cw = np.ascontiguousarray(np.asarray(inputs["gdn_conv_w"], f32)[0].reshape(4, 24, 128).transpose(2, 1, 0).reshape(128, 96))
    invf = (np.float32(500000.0) ** (-(np.arange(16, dtype=f32) * f32(2.0) / f32(32.0)))).astype(f32)
    return dict(wall=wall, wout=wout, cst=cst, gn=gn, cw=cw, invf=invf,
                fing=np.ascontiguousarray(np.asarray(inputs["final_norm_g"], f32)),
                gdng=np.ascontiguousarray(np.asarray(inputs["gdn_norm_g"], f32)[0]),
                alog=np.ascontiguousarray(np.asarray(inputs["gdn_a_log"], f32)[0]),
                dtb=np.ascontiguousarray(np.asarray(inputs["gdn_dt_bias"], f32)[0]))


def _core_inputs(inputs, shared, b, hh):
    f32 = np.float32
    x = np.asarray(inputs["x"], f32)[b]
    pos = np.asarray(inputs["positions"], np.int32)[b]
    xo = np.ascontiguousarray(x.reshape(NT, 128, D)[hh::2].reshape(T // 2, D))
    pos_t = np.ascontiguousarray(pos.reshape(NT, 128).T)
    poso = np.ascontiguousarray(pos.reshape(NT, 128)[hh::2].T)
    ii = np.arange(128)
    tril = np.where(ii[None, :] <= ii[:, None], 0.0, -3.0e38).astype(f32)
    NEG = np.full((128, 128), -3.0e38, f32); Z = np.zeros((128, 128), f32)
    cmask = np.concatenate([tril, NEG], 1) if hh == 0 else np.concatenate([Z, tril], 1)
    flg = np.tile(np.array([[1.0 - hh, float(hh)]], f32), (128, 1))
    d = dict(shared)
    d.update(x=np.ascontiguousarray(x), xo=xo, pos=pos_t, poso=poso, cmask=np.ascontiguousarray(cmask), flg=np.ascontiguousarray(flg))
    return d


_NC_CACHE = {}


def kernel(**inputs):
    _nco = inputs.pop("_ncores", None)
    debug = bool(inputs.pop("_debug", False))
    n_macro = int(inputs.pop("_n_macro", NM))
    stop = int(inputs.pop("_stop", 99))
    nblk = int(inputs.pop("_nblk", 99))
    key = (debug, n_macro, stop, nblk)
    if key not in _NC_CACHE:
        _NC_CACHE[key] = build(debug=debug, n_macro=n_macro, stop=stop, nblk=nblk)
    nc = _NC_CACHE[key]
    shared = _prep_shared(inputs)
    ncores = int(_nco) if _nco is not None else 8
    in_maps = [_core_inputs(inputs, shared, c // 2, c % 2) for c in range(ncores)]
    res = run_bass_kernel_spmd(nc, in_maps, core_ids=list(range(ncores)))
    if ncores < 8:
        return [res.results[c] for c in range(ncores)]
    out = np.zeros((4, NT, 128, D), np.float32)
    for c in range(8):
        b, hh = c // 2, c % 2
        out[b, hh::2] = np.asarray(res.results[c]["out"], np.float32).reshape(NOWN, 128, D)
    out = out.reshape(4, T, D)
    if debug:
        dbg = [np.asarray(res.results[c]["dbg"]) for c in range(8)]
        return out, dbg
    return out
```

```python
from contextlib import ExitStack
import math
import numpy as np
import concourse.bass as bass
import concourse.mybir as mybir

F32 = mybir.dt.float32
BF16 = mybir.dt.bfloat16
I32 = mybir.dt.int32
ALU = mybir.AluOpType
AF = mybir.ActivationFunctionType
AX = mybir.AxisListType

ENGS = ("pe", "act", "dve", "pool", "sp")
SEM_EPOCH = 4000
DMA_SLOTS = 6
DMA_EPOCH = 1500


class Prog:
    def __init__(self, nc, stack):
        self.nc = nc
        self.stack = stack
        self.streams = {e: [] for e in ENGS}
        self.cur = {e: None for e in ENGS}
        self.waited = {e: {} for e in ENGS}
        self.lw = {}
        self.rd = {}
        self.nsem = 0
        self.slots = {e: [[None, 0, None] for _ in range(DMA_SLOTS)] for e in ENGS}
        self.slot_i = {e: 0 for e in ENGS}
        self.n_inst = 0

    def _newsem(self):
        self.nsem += 1
        return self.stack.enter_context(self.nc.semaphore(f"s{self.nsem}"))

    def add(self, eng, fns, r=(), w=(), dma=False):
        if not isinstance(fns, (list, tuple)):
            fns = [fns]
        r = list(r)
        w = list(w) + [k for k in r if isinstance(k, tuple) and k and k[0] == "ps" and k not in w]
        deps = []
        for k in r:
            t = self.lw.get(k)
            if t is not None:
                deps.append(t)
        for k in w:
            t = self.lw.get(k)
            if t is not None:
                deps.append(t)
            for s, v in self.rd.get(k, {}).items():
                deps.append((s, v))
        if dma:
            i = self.slot_i[eng]
            self.slot_i[eng] = (i + 1) % DMA_SLOTS
            slot = self.slots[eng][i]
            if slot[0] is None or slot[1] >= 16 * DMA_EPOCH:
                if slot[2] is not None:
                    deps.append(slot[2])
                slot[0] = self._newsem()
                slot[1] = 0
            elif slot[2] is not None:
                deps.append(slot[2])
            slot[1] += 16
            tok = (slot[0], slot[1])
            slot[2] = tok
            inc = 16
        else:
            c = self.cur[eng]
            if c is None or c[1] >= SEM_EPOCH:
                c = self.cur[eng] = [self._newsem(), 0]
            c[1] += 1
            tok = (c[0], c[1])
            inc = 1
        waits = []
        wd = self.waited[eng]
        own = self.cur[eng][0] if (self.cur[eng] is not None) else None
        for s, v in deps:
            if eng == "pe" and not dma and s is own:
                continue
            if wd.get(id(s), 0) < v:
                wd[id(s)] = v
                waits.append((s, v))
        ww = {}
        for s, v in waits:
            if id(s) not in ww or ww[id(s)][1] < v:
                ww[id(s)] = (s, v)
        self.streams[eng].append((list(ww.values()), fns, tok[0], inc))
        self.n_inst += len(fns) + len(ww)
        for k in w:
            self.lw[k] = tok
            self.rd[k] = {}
        for k in r:
            if k in w:
                continue
            d = self.rd.setdefault(k, {})
            d[tok[0]] = tok[1]
        return tok

    def barrier(self):
        toks = []
        for e in ENGS:
            if self.cur[e] is not None:
                toks.append((self.cur[e][0], self.cur[e][1]))
            for slot in self.slots[e]:
                if slot[2] is not None:
                    toks.append(slot[2])
        for e in ENGS:
            wd = self.waited[e]
            waits = []
            for s, v in toks:
                if v > 0 and wd.get(id(s), 0) < v:
                    wd[id(s)] = v
                    waits.append((s, v))
            self.streams[e].append((waits, [], None, 0))
            self.n_inst += len(waits)
        self.lw.clear()
        self.rd.clear()

    def finish_waits(self, eng="sp"):
        waits = []
        for e in ENGS:
            for slot in self.slots[e]:
                if slot[2] is not None:
                    waits.append(slot[2])
        self.streams[eng].append((waits, [], None, 0))

    def emit(self):
        nc = self.nc
        streams = self.streams

        def replay(name, eng):
            for waits, fns, sem, inc in streams[name]:
                for s, v in waits:
                    eng.wait_ge(s, v)
                for f in fns[:-1]:
                    f(eng)
                if fns:
                    fns[-1](eng).then_inc(sem, inc)

        with nc.Block() as block:
            @block.tensor
            def _(e):
                replay("pe", e)

            @block.scalar
            def _(e):
                replay("act", e)

            @block.vector
            def _(e):
                replay("dve", e)

            @block.gpsimd
            def _(e):
                replay("pool", e)

            @block.sync
            def _(e):
                replay("sp", e)

    def dma(self, eng, out, in_, r, w, **kw):
        return self.add(eng, lambda e: e.dma_start(out=out, in_=in_, **kw), r, w, dma=True)

    def act(self, out, in_, func, r, w, eng="act", **kw):
        return self.add(eng, lambda e: e.activation(out=out, in_=in_, func=func, **kw), r, w)

    def ts(self, eng, out, in0, s1, s2, op0, r, w, op1=None, **kw):
        if op1 is None:
            return self.add(eng, lambda e: e.tensor_scalar(out=out, in0=in0, scalar1=s1, scalar2=None, op0=op0, **kw), r, w)
        return self.add(eng, lambda e: e.tensor_scalar(out=out, in0=in0, scalar1=s1, scalar2=s2, op0=op0, op1=op1, **kw), r, w)

    def tt(self, eng, out, in0, in1, op, r, w):
        return self.add(eng, lambda e: e.tensor_tensor(out=out, in0=in0, in1=in1, op=op), r, w)

    def stt(self, out, in0, scalar, in1, op0, op1, r, w, **kw):
        return self.add("dve", lambda e: e.scalar_tensor_tensor(out=out, in0=in0, scalar=scalar, in1=in1, op0=op0, op1=op1, **kw), r, w)

    def copy(self, eng, out, in_, r, w):
        if eng == "act":
            return self.add(eng, lambda e: e.copy(out=out, in_=in_), r, w)
        return self.add(eng, lambda e: e.tensor_copy(out=out, in_=in_), r, w)

    def memset(self, eng, ap, val, w):
        return self.add(eng, lambda e: e.memset(ap, val), (), w)

    def mm(self, out, lhsT, rhs, start, stop, **kw):
        return lambda e: e.matmul(out, lhsT, rhs, start=start, stop=stop, **kw)

    def tr(self, out, in_, ident):
        return lambda e: e.transpose(out, in_, ident)
from concourse.bass_utils import run_bass_kernel_spmd


D = 2048; KC = 16; T = 4096; NT = 32; TM = 512; NM = 8; TPM = 4; NOWN = 16
EPS = 1e-6
C_QKV = 0; C_GZ = 3072; C_KV = 4096; C_SM = 4608; C_AQ = 4688; C_AZ = 5712; C_IQ = 6736; C_IW = 7760; NCOL = 7776
NIT = 24
TWO_PI = 2.0 * math.pi


def build(debug=False, n_macro=NM, stop=99, nblk=99, dma_only=False):
    nc = bass.Bass("TRN2", target_bir_lowering=False)
    dt_in = lambda n, s, d=F32: nc.dram_tensor(n, s, d, kind="ExternalInput").ap()
    x_d = dt_in("x", [T, D]); xo_d = dt_in("xo", [T // 2, D])
    wall_d = dt_in("wall", [128, KC * NCOL]); wout_d = dt_in("wout", [128, KC * D])
    cst_d = dt_in("cst", [128, 4 * 128])
    gn_d = dt_in("gn", [128, KC]); fing_d = dt_in("fing", [D]); gdng_d = dt_in("gdng", [128])
    alog_d = dt_in("alog", [8]); dtb_d = dt_in("dtb", [8]); cw_d = dt_in("cw", [128, 24 * 4])
    pos_d = dt_in("pos", [128, NT], I32); poso_d = dt_in("poso", [128, NOWN], I32)
    invf_d = dt_in("invf", [16]); flg_d = dt_in("flg", [128, 2]); cmask_d = dt_in("cmask", [128, 256])
    out_d = nc.dram_tensor("out", [T // 2, D], F32, kind="ExternalOutput").ap()
    if debug:
        dbg_d = nc.dram_tensor("dbg", [T // 2, D], BF16, kind="ExternalOutput").ap()

    with ExitStack() as st:
        P = Prog(nc, st)
        sb = lambda n, s, d: st.enter_context(nc.sbuf_tensor("s_" + n, s, d))
        ps = [st.enter_context(nc.psum_tensor(f"ps{i}", [128, 512], F32)) for i in range(8)]
        psk = lambda i: ("ps", i)
        psb = lambda i: ps[i][:, :].bitcast(BF16)

        ra = sb("ra", [128, 4, 32], F32); rb = sb("rb", [128, 4, 32], F32); rtmp = sb("rtmp", [128, 256], BF16)
        cst = sb("cst", [128, 512], F32)
        ident_f = cst[:, 0:128]; U_f = cst[:, 128:256]; Ls_f = cst[:, 256:384]; ones_f = cst[:, 384:512]
        ident_b = sb("ident_b", [128, 128], BF16); ones_b = sb("ones_b", [128, 128], BF16)
        identrep = sb("identrep", [128, 4, 128], BF16)
        gn = sb("gn", [128, KC], F32); gdng = sb("gdng", [128, 128], F32)
        alog = sb("alog", [128, 8], F32); dtb = sb("dtb", [128, 8], F32); negA = sb("negA", [128, 8], F32)
        cw = sb("cw", [128, 24, 4], F32)
        flg = sb("flg", [128, 2], F32); cmask = sb("cmask", [128, 256], F32)
        invf = sb("invf", [128, 16], F32)
        cc = sb("cc", [128, NT, 32], F32); ns = sb("ns", [128, NT, 32], F32)
        cci = sb("cci", [128, NT, 16], F32); nsi = sb("nsi", [128, NT, 16], F32)
        cco = sb("cco", [128, NOWN, 32], F32); nso = sb("nso", [128, NOWN, 32], F32)
        ccio = sb("ccio", [128, NOWN, 16], F32); nsio = sb("nsio", [128, NOWN, 16], F32)
        kT_res = sb("kT_res", [128, 2, T], BF16)
        v1_res = sb("v1_res", [128, NT, 2, 130], BF16)
        kiT_res = sb("kiT_res", [128, T], BF16)
        halo = sb("halo", [128, 24, 4], F32)
        S_f = sb("S_f", [128, 8, 128], F32); S_b = sb("S_b", [128, 8, 128], BF16)
        xt = sb("xt", [128, D], F32); xs = sb("xs", [128, D], BF16)
        sml = sb("sml", [128, 64], F32)
        scrA = sb("scrA", [128, 8192], BF16)
        scrB = sb("scrB", [128, 4096], BF16)
        scrC = sb("scrC", [128, 4096], BF16)
        wbuf = [sb(f"wbuf{i}", [128, KC, 256], BF16) for i in range(3)]
        qT = sb("qT", [128, 8, TM], BF16); kT = sb("kT", [128, 8, TM], BF16); vT = sb("vT", [128, 8, TM], BF16)
        gzs = sb("gzs", [128, TPM, 1024], BF16)
        gat = sb("gat", [128, TPM, 8], F32); bet = sb("bet", [128, TPM, 8], F32)
        qTo = sb("qTo", [128, 2, 8, 128], BF16); azs = sb("azs", [128, 2, 1024], BF16)
        qiT = sb("qiT", [128, 2, 8, 128], BF16); wq = sb("wq", [128, 2, 16], F32)
        oa_sel = sb("oa_sel", [128, 2, 1024], BF16)
        otok = sb("otok", [128, D], BF16)
        mixT = sb("mixT", [128, KC, 256], BF16)
        gsm = sb("gsm", [128, 64], F32)

        hT = scrA[:, :].rearrange("p (k t) -> p k t", k=KC)
        hTo = scrB[:, :].rearrange("p (k t) -> p k t", k=KC)
        sc = scrA[:, :].bitcast(F32)
        mb = scrB[:, :]
        yres = scrA[:, :].bitcast(F32).rearrange("p (o d) -> p o d", o=2)
        scrCf = scrC[:, :].bitcast(F32)
        raw = [scrCf[:, 0:516], scrCf[:, 516:1032]]
        cacc = [xt[:, 0:512], xt[:, 512:1024]]
        rl = cacc
        silt = xt[:, 1024:1536]
        PT = [xs[:, 0:512], xs[:, 512:1024]]
        rtt = xt[:, 1536:2048]; sqt = sb("sqt", [128, 512], BF16)
        silt_b = sb("silt_b", [128, 512], F32); sqt_b = sb("sqt_b", [128, 512], BF16)
        silt2 = [silt, silt_b[:, :]]; sqt2 = [sqt[:, :], sqt_b[:, :]]

        P.dma("sp", cst[:], cst_d, [], ["cst"])
        P.dma("sp", gn[:], gn_d, [], ["gn"])
        P.dma("sp", gdng[:], gdng_d.partition_broadcast(128), [], ["gdng"])
        P.dma("sp", alog[:], alog_d.partition_broadcast(128), [], ["alog"])
        P.dma("sp", dtb[:], dtb_d.partition_broadcast(128), [], ["dtb"])
        P.dma("sp", cw[:], cw_d.rearrange("p (c i) -> p c i", i=4), [], ["cw"])
        P.dma("sp", flg[:], flg_d, [], ["flg"])
        P.dma("sp", cmask[:], cmask_d, [], ["cmask"])
        P.dma("sp", invf[:], invf_d.partition_broadcast(128), [], ["invf"])
        P.copy("dve", ident_b[:], ident_f, ["cst"], ["ident_b"])
        P.copy("dve", ones_b[:], ones_f, ["cst"], ["ones_b"])
        P.copy("dve", identrep[:], ident_f.unsqueeze(1).broadcast_to([128, 4, 128]), ["cst"], ["identrep"])
        P.act(negA[:], alog[:], AF.Exp, ["alog"], ["negA"])
        P.ts("dve", negA[:], negA[:], -1.0, None, ALU.mult, ["negA"], ["negA"])
        Ub = sb("Ub", [128, 128], F32); Lp = sb("Lp", [128, 128], F32)
        P.ts("dve", Ub[:], U_f, 30000.0, -30000.0, ALU.mult, ["cst"], ["Ub"], op1=ALU.add)
        P.ts("dve", Lp[:], Ls_f, -30000.0, 30000.0, ALU.mult, ["cst"], ["Lp"], op1=ALU.add)
        P.memset("pool", halo[:], 0.0, ["halo"])
        P.memset("pool", S_f[:], 0.0, ["S_f"])
        P.memset("pool", S_b[:], 0.0, ["S_b"])
        P.memset("pool", v1_res[:], 1.0, ["v1_res"])

        A32s = scrA[:, :].bitcast(F32)

        def make_tables(pos_dram, ntl, cc_, ns_, cci_, nsi_, tag):
            pi_ = sb("pi" + tag, [128, ntl], I32); pf = sb("pf" + tag, [128, ntl], F32)
            ne = ntl * 32
            AR = A32s[:, 0:ne].rearrange("p (t a f) -> p t a f", a=2, f=16)
            KI = A32s[:, 1024:1024 + ne].bitcast(I32).rearrange("p (t a f) -> p t a f", a=2, f=16)
            KF = A32s[:, 2048:2048 + ne].rearrange("p (t a f) -> p t a f", a=2, f=16)
            tag = ""
            P.dma("sp", pi_[:], pos_dram, [], ["pi" + tag])
            P.copy("dve", pf[:], pi_[:], ["pi" + tag], ["pf" + tag])
            P.tt("dve", AR[:, :, 0, :], pf[:].unsqueeze(2).broadcast_to([128, ntl, 16]),
                 invf[:].unsqueeze(1).broadcast_to([128, ntl, 16]), ALU.mult, ["pf" + tag, "invf"], ["AR" + tag])
            P.ts("dve", AR[:, :, 1, :], AR[:, :, 0, :], math.pi / 2, None, ALU.add, ["AR" + tag], ["AR" + tag])
            P.ts("dve", KF[:], AR[:], 1.0 / TWO_PI, None, ALU.mult, ["AR" + tag], ["KF" + tag])
            P.copy("dve", KI[:], KF[:], ["KF" + tag], ["KI" + tag])
            P.copy("dve", KF[:], KI[:], ["KI" + tag], ["KF" + tag])
            P.stt(AR[:], KF[:], -TWO_PI, AR[:], ALU.mult, ALU.add, ["KF" + tag, "AR" + tag], ["AR" + tag])
            P.ts("dve", AR[:], AR[:], 3.14159, -3.14159, ALU.min, ["AR" + tag], ["AR" + tag], op1=ALU.max)
            P.act(AR[:], AR[:], AF.Sin, ["AR" + tag], ["AR" + tag])
            k = "AR" + tag
            P.copy("dve", cc_[:, :, 0:16], AR[:, :, 1, :], [k], [("cc", tag)])
            P.copy("dve", cc_[:, :, 16:32], AR[:, :, 1, :], [k], [("cc", tag)])
            P.ts("dve", ns_[:, :, 0:16], AR[:, :, 0, :], -1.0, None, ALU.mult, [k], [("ns", tag)])
            P.copy("dve", ns_[:, :, 16:32], AR[:, :, 0, :], [k], [("ns", tag)])
            P.copy("dve", cci_[:, :, 0:8], AR[:, :, 1, 0:16:2], [k], [("cci", tag)])
            P.copy("dve", cci_[:, :, 8:16], AR[:, :, 1, 0:16:2], [k], [("cci", tag)])
            P.ts("dve", nsi_[:, :, 0:8], AR[:, :, 0, 0:16:2], -1.0, None, ALU.mult, [k], [("nsi", tag)])
            P.copy("dve", nsi_[:, :, 8:16], AR[:, :, 0, 0:16:2], [k], [("nsi", tag)])

        make_tables(pos_d, NT, cc, ns, cci, nsi, "a")
        P.barrier()
        make_tables(poso_d, NOWN, cco, nso, ccio, nsio, "o")
        P.barrier()

        wstate = {"i": 0}

        def wload(src, c0, n):
            i = wstate["i"] % 3
            wstate["i"] += 1
            blk_ap = src[:, KC * c0:KC * (c0 + n)].rearrange("p (k c) -> p k c", k=KC)
            for h in range(2):
                P.dma("pool", wbuf[i][:, h * 8:(h + 1) * 8, 0:n], blk_ap[:, h * 8:(h + 1) * 8, :], [], [("wb", i)])
            return i

        def stream(blocks, body, ahead=2):
            loaded = []
            nb = len(blocks)
            for j in range(min(ahead, nb)):
                loaded.append(wload(*blocks[j][:3]))
            for j in range(nb):
                if j + ahead < nb:
                    loaded.append(wload(*blocks[j + ahead][:3]))
                body(loaded[j], blocks[j][3])

        bank_rr = {"i": 0}

        def nbank(lo=0, hi=8):
            b = lo + bank_rr["i"] % (hi - lo)
            bank_rr["i"] += 1
            return b

        def norm_tile(src_rows, hview, hkey, col0):
            P.dma("sp", xt[:], src_rows, [], ["xt"])
            P.act(xs[:], xt[:], AF.Square, ["xt"], ["xs", "ssq"], accum_out=sml[:, 0:1])
            P.ts("dve", sml[:, 1:2], sml[:, 0:1], 1.0 / D, EPS, ALU.mult, ["ssq"], ["ssq1"], op1=ALU.add)
            P.act(sml[:, 2:3], sml[:, 1:2], AF.Sqrt, ["ssq1"], ["ssq2"])
            P.add("dve", lambda e: e.reciprocal(out=sml[:, 3:4], in_=sml[:, 2:3]), ["ssq2"], ["rstd"])
            P.ts("dve", xs[:], xt[:], sml[:, 3:4], None, ALU.mult, ["xt", "rstd"], ["xs"])
            for q in range(4):
                b = nbank()
                pb = psb(b)
                P.add("pe", [P.tr(pb[:, j * 128:(j + 1) * 128], xs[:, (4 * q + j) * 128:(4 * q + j + 1) * 128], ident_b[:]) for j in range(4)],
                      ["xs", "ident_b"], [psk(b)])
                P.tt("dve", hview[:, 4 * q:4 * q + 4, col0:col0 + 128], pb[:, 0:512].rearrange("p (j t) -> p j t", j=4),
                     gn[:, 4 * q:4 * q + 4].unsqueeze(2).broadcast_to([128, 4, 128]), ALU.mult, [psk(b), "gn"], [hkey])

        DBG = 99

        def rope(dst, src, H, half, cct, nst, rk, wk, dst2=None, src2=None):
            h2 = 2 * half
            W = dst2.shape[1]
            dh = W // H
            rf = xs[:, :].bitcast(F32)[:, 0:W]
            if DBG >= 1:
                P.copy("act", rf, src2, rk, ["rf"])
            if DBG >= 2:
                P.copy("act", dst2, rf, ["rf"], [wk])
            fa = []
            for h in range(H):
                o = h * dh
                fa.append(lambda e, h=h, o=o: e.tensor_tensor(out=ra[:, h, 0:h2], in0=rf[:, o:o + h2], in1=cct, op=ALU.mult))
                fa.append(lambda e, h=h, o=o: e.tensor_tensor(out=rb[:, h, 0:half], in0=rf[:, o + half:o + h2], in1=nst[:, 0:half], op=ALU.mult))
                fa.append(lambda e, h=h, o=o: e.tensor_tensor(out=rb[:, h, half:h2], in0=rf[:, o:o + half], in1=nst[:, half:h2], op=ALU.mult))
            if DBG >= 3:
                P.add("dve", fa, ["rf", "tabs"], ["ra", "rb"])
            fb = [(lambda e, h=h, o=h * dh: e.tensor_tensor(out=dst2[:, o:o + h2], in0=ra[:, h, 0:h2], in1=rb[:, h, 0:h2], op=ALU.add)) for h in range(H)]
            if DBG >= 4:
                P.add("dve", fb, ["ra", "rb", wk], [wk])

        for m in range(n_macro if stop > 0 else 0):
            for tt_ in range(TPM):
                r0 = m * TM + tt_ * 128
                norm_tile(x_d[r0:r0 + 128, :], hT, "hT", tt_ * 128)
            for ot in range(2):
                r0 = (2 * m + ot) * 128
                norm_tile(xo_d[r0:r0 + 128, :], hTo, "hTo", ot * 128)
            P.barrier()
            if stop <= 1:
                break

            blocks = []
            for blk in range(12):
                blocks.append((wall_d, C_QKV + blk * 256, 256, ("qkv", blk)))
            for blk in range(4):
                blocks.append((wall_d, C_GZ + blk * 256, 256, ("gz", blk)))
            blocks.append((wall_d, C_KV, 256, ("ak", 0)))
            blocks.append((wall_d, C_KV + 256, 256, ("av", 0)))
            blocks.append((wall_d, C_SM, 80, ("sm", 0)))
            for blk in range(4):
                blocks.append((wall_d, C_AQ + blk * 256, 256, ("aq", blk)))
            for blk in range(4):
                blocks.append((wall_d, C_AZ + blk * 256, 256, ("az", blk)))
            for blk in range(4):
                blocks.append((wall_d, C_IQ + blk * 256, 256, ("iq", blk)))
            blocks.append((wall_d, C_IW, 16, ("iw", 0)))

            def tok_mm(b, hv, hk, col0, wi, n):
                P.add("pe", [P.mm(ps[b][:, 0:n], hv[:, kc, col0:col0 + 128], wbuf[wi][:, kc, 0:n], kc == 0, kc == KC - 1) for kc in range(KC)],
                      [hk, ("wb", wi)], [psk(b)])

            pending = []

            def flush_pending():
                for f_ in pending:
                    f_()
                del pending[:]

            def body2(wi, info):
                kind, blk = info
                if dma_only:
                    return
                if kind != "qkv":
                    flush_pending()
                if kind == "qkv":
                    for c2 in range(2):
                        ct = blk * 2 + c2
                        b = nbank()
                        P.add("pe", [P.mm(ps[b][:, :], wbuf[wi][:, kc, c2 * 128:(c2 + 1) * 128], hT[:, kc, :], kc == 0, kc == KC - 1) for kc in range(KC)],
                              ["hT", ("wb", wi)], [psk(b)])
                        flush_pending()
                        rw = raw[ct % 2]; rk = ("raw", ct % 2); ac = cacc[ct % 2]; ak_ = ("cacc", ct % 2)
                        P.copy("act", rw[:, 3:515], ps[b][:, :], [psk(b)], [rk])
                        P.copy("pool", rw[:, 0:3], halo[:, ct, 0:3], [("halo", ct)], [rk])
                        P.copy("pool", halo[:, ct, 0:3], rw[:, 512:515], [rk], [("halo", ct)])
                        P.ts("dve", ac, rw[:, 3:515], cw[:, ct, 3:4], None, ALU.mult, [rk, "cw"], [ak_])
                        for i in (2, 1, 0):
                            P.stt(ac, rw[:, i:i + 512], cw[:, ct, i:i + 1], ac, ALU.mult, ALU.add, [rk, "cw", ak_], [ak_])
                        if ct < 16:
                            isq = ct < 8; hd = ct % 8
                            sl_ = silt2[ct % 2]; sk_ = ("silt", ct % 2); sq_ = sqt2[ct % 2]; qk_ = ("sqt", ct % 2)
                            P.act(sl_, ac, AF.Silu, [ak_], [sk_])
                            P.act(sq_, sl_, AF.Square, [sk_], [qk_])

                            def tail(isq=isq, hd=hd, sl_=sl_, sk_=sk_, sq_=sq_, qk_=qk_):
                                b2 = nbank()
                                P.add("pe", [P.mm(ps[b2][:, :], ones_b[:], sq_, True, True)], [qk_, "ones_b"], [psk(b2)])
                                P.act(rtt, ps[b2][:, :], AF.Sqrt, [psk(b2)], ["rtt"], scale=(128.0 if isq else 1.0), bias=(128.0 * EPS if isq else EPS))
                                P.add("dve", lambda e: e.reciprocal(out=rtt, in_=rtt), ["rtt"], ["rtt"])
                                dstT = qT if isq else kT
                                P.tt("dve", dstT[:, hd, :], sl_, rtt, ALU.mult, [sk_, "rtt"], [("qT" if isq else "kT", hd)])
                            pending.append(tail)
                        else:
                            P.act(vT[:, ct - 16, :], ac, AF.Silu, [ak_], [("vT", ct - 16)])
                elif kind == "gz":
                    for tt_ in range(TPM):
                        b = nbank()
                        tok_mm(b, hT, "hT", tt_ * 128, wi, 256)
                        P.act(gzs[:, tt_, blk * 256:(blk + 1) * 256], ps[b][:, 0:256], AF.Silu, [psk(b)], [("gzs", tt_)])
                elif kind == "ak":
                    for tt_ in range(TPM):
                        b = nbank(); gt = m * TPM + tt_
                        tok_mm(b, hT, "hT", tt_ * 128, wi, 256)
                        LV = 9 if DBG >= 6 else (3 if DBG >= 5 else 2)
                        kv_ = rtmp[:, :].rearrange("p (h d) -> p h d", h=2)
                        if LV >= 2:
                            rope(kv_, ps[b][:, 0:256].rearrange("p (h d) -> p h d", h=2), 2, 16, cc[:, gt, :], ns[:, gt, :], [psk(b)], "rtmp", rtmp[:, 0:256], ps[b][:, 0:256])
                        elif LV >= 1:
                            P.copy("act", rtmp[:, 0:256], ps[b][:, 0:256], [psk(b)], ["rtmp"])
                        b2 = nbank(); pb = psb(b2)
                        if LV >= 3:
                            P.add("pe", [P.tr(pb[:, g * 128:(g + 1) * 128], rtmp[:, g * 128:(g + 1) * 128], ident_b[:]) for g in range(2)], ["rtmp", "ident_b"], [psk(b2)])
                        if LV >= 4:
                            for g in range(2):
                                if DBG == 7 and g == 1:
                                    continue
                                if DBG == 8 and g == 0:
                                    continue
                                eng_ = "dve" if g == 0 else "act"
                                if DBG == 9:
                                    eng_ = "dve"
                                P.copy(eng_, kT_res[:, g, gt * 128:(gt + 1) * 128], pb[:, g * 128:(g + 1) * 128], [psk(b2)], [("kT_res", gt, g)])
                elif kind == "av":
                    for tt_ in range(TPM):
                        b = nbank(); gt = m * TPM + tt_
                        tok_mm(b, hT, "hT", tt_ * 128, wi, 256)
                        for g in range(2):
                            P.copy("dve" if g == 0 else "act", v1_res[:, gt, g, 0:128], ps[b][:, g * 128:(g + 1) * 128], [psk(b)], [("v1_res", gt, g)])
                elif kind == "sm":
                    for tt_ in range(TPM):
                        b = nbank(); gt = m * TPM + tt_
                        tok_mm(b, hT, "hT", tt_ * 128, wi, 80)
                        pk = [psk(b)]
                        P.tt("dve", gsm[:, 0:8], ps[b][:, 0:8], dtb[:], ALU.add, pk + ["dtb"], ["g0"])
                        P.act(gsm[:, 8:16], gsm[:, 0:8], AF.Abs, ["g0"], ["g1"])
                        P.act(gsm[:, 16:24], gsm[:, 8:16], AF.Exp, ["g1"], ["g2"], scale=-1.0)
                        P.act(gsm[:, 24:32], gsm[:, 16:24], AF.Ln, ["g2"], ["g3"], bias=1.0)
                        P.stt(gsm[:, 32:40], gsm[:, 0:8], 0.0, gsm[:, 24:32], ALU.max, ALU.add, ["g0", "g3"], ["g4"])
                        P.tt("dve", gat[:, tt_, :], gsm[:, 32:40], negA[:], ALU.mult, ["g4", "negA"], [("gat", tt_)])
                        P.act(bet[:, tt_, :], ps[b][:, 8:16], AF.Sigmoid, pk, [("bet", tt_)])
                        ikv = rtmp[:, 0:64].rearrange("p (h d) -> p h d", h=1)
                        rope(ikv, ps[b][:, 16:80].rearrange("p (h d) -> p h d", h=1), 1, 8, cci[:, gt, :], nsi[:, gt, :], pk, "rtmp", rtmp[:, 0:64], ps[b][:, 16:80])
                        P.copy("pool", rtmp[:, 64:128], rtmp[:, 0:64], ["rtmp"], ["rtmp"])
                        b2 = nbank(); pb = psb(b2)
                        P.add("pe", [P.tr(pb[:, 0:128], rtmp[:, 0:128], ident_b[:])], ["rtmp", "ident_b"], [psk(b2)])
                        P.copy("dve", kiT_res[:, gt * 128:(gt + 1) * 128], pb[:, 0:128], [psk(b2)], [("kiT_res", gt)])
                elif kind == "aq":
                    for ot in range(2):
                        b = nbank(); oi = 2 * m + ot
                        tok_mm(b, hTo, "hTo", ot * 128, wi, 256)
                        qv = rtmp[:, :].rearrange("p (h d) -> p h d", h=2)
                        rope(qv, ps[b][:, 0:256].rearrange("p (h d) -> p h d", h=2), 2, 16, cco[:, oi, :], nso[:, oi, :], [psk(b)], "rtmp", rtmp[:, 0:256], ps[b][:, 0:256])
                        b2 = nbank(); pb = psb(b2)
                        P.add("pe", [P.tr(pb[:, g * 128:(g + 1) * 128], rtmp[:, g * 128:(g + 1) * 128], ident_b[:]) for g in range(2)], ["rtmp", "ident_b"], [psk(b2)])
                        P.copy("dve", qTo[:, ot, 2 * blk:2 * blk + 2, :].rearrange("p g t -> p (g t)"), pb[:, 0:256], [psk(b2)], [("qTo", ot)])
                elif kind == "az":
                    for ot in range(2):
                        b = nbank()
                        tok_mm(b, hTo, "hTo", ot * 128, wi, 256)
                        P.act(azs[:, ot, blk * 256:(blk + 1) * 256], ps[b][:, 0:256], AF.Silu, [psk(b)], [("azs", ot)])
                elif kind == "iq":
                    for ot in range(2):
                        b = nbank(); oi = 2 * m + ot
                        tok_mm(b, hTo, "hTo", ot * 128, wi, 256)
                        qv = rtmp[:, :].rearrange("p (h d) -> p h d", h=4)
                        rope(qv, ps[b][:, 0:256].rearrange("p (h d) -> p h d", h=4), 4, 8, ccio[:, oi, :], nsio[:, oi, :], [psk(b)], "rtmp", rtmp[:, 0:256], ps[b][:, 0:256])
                        b2 = nbank(); pb = psb(b2)
                        P.add("pe", [P.tr(pb[:, g * 128:(g + 1) * 128], rtmp[:, g * 128:(g + 1) * 128], ident_b[:]) for g in range(2)], ["rtmp", "ident_b"], [psk(b2)])
                        P.copy("dve", qiT[:, ot, 2 * blk:2 * blk + 2, :].rearrange("p g t -> p (g t)"), pb[:, 0:256], [psk(b2)], [("qiT", ot)])
                elif kind == "iw":
                    for ot in range(2):
                        b = nbank()
                        tok_mm(b, hTo, "hTo", ot * 128, wi, 16)
                        P.ts("dve", wq[:, ot, :], ps[b][:, 0:16], 1.0 / 32.0, None, ALU.mult, [psk(b)], [("wq", ot)])

            stream(blocks[:nblk], body2)
            P.barrier()
            if stop <= 2:
                break

            A32 = scrA[:, :].bitcast(F32)
            B32 = scrB[:, :].bitcast(F32)

            def tmpl(hg):
                o = hg * 2048
                t = {}
                t["Z"] = A32[:, o:o + 512]; t["E1"] = A32[:, o + 512:o + 1024]; t["E2"] = A32[:, o + 1024:o + 1536]
                t["gU"] = A32[:, o + 1536:o + 2048]
                ob = hg * 2048
                for i, nme in enumerate(["Mc", "Nc", "Pc", "AT"]):
                    t[nme] = scrB[:, ob + i * 512: ob + (i + 1) * 512]
                return t
            gtmp = [scrC[:, :].rearrange("p (a b) -> p a b", a=8), mixT[:, :, :].rearrange("p k t -> p (k t)").rearrange("p (a b) -> p a b", a=8)]

            def gdn_chunk_prep(tt_):
                b = nbank()
                P.add("pe", [P.mm(ps[b][:, 0:8], U_f, gat[:, tt_, :], True, True), P.mm(ps[b][:, 8:16], ones_f, gat[:, tt_, :], True, True)],
                      [("gat", tt_), "cst"], [psk(b)])
                P.copy("dve", gsm[:, 0:16], ps[b][:, 0:16], [psk(b)], ["gc"])
                P.act(gsm[:, 16:32], gsm[:, 0:16], AF.Exp, ["gc"], ["eg"])
                P.tt("dve", gsm[:, 32:40], gsm[:, 8:16], gsm[:, 0:8], ALU.subtract, ["gc"], ["ekl"])
                P.act(gsm[:, 32:40], gsm[:, 32:40], AF.Exp, ["ekl"], ["ek"])
                P.tt("dve", gsm[:, 40:48], bet[:, tt_, :], gsm[:, 16:24], ALU.mult, [("bet", tt_), "eg"], ["bk"])
                P.ts("dve", gsm[:, 48:56], bet[:, tt_, :], -1.0, None, ALU.mult, [("bet", tt_)], ["nbet"])

            def bc(ap8, hg):
                return ap8[:, 4 * hg:4 * hg + 4].unsqueeze(2).broadcast_to([128, 4, 128])

            def v3(ap):
                return ap.rearrange("p (h d) -> p h d", h=4)

            def gdn_hg(tt_, hg):
                t = tmpl(hg); K = lambda n: (n, hg)
                G = gtmp[hg]
                ktok = G[:, 0, :]; vtok = G[:, 1, :]; qg = G[:, 2, :]; kbg = G[:, 3, :]; vb = G[:, 4, :]; wTn = G[:, 5, :]; vn = G[:, 6, :]; kt2 = G[:, 7, :]
                tok = slice(tt_ * 128, (tt_ + 1) * 128)
                hs = range(4 * hg, 4 * hg + 4)
                gc = gsm[:, 0:8]; eg = gsm[:, 16:24]; egl = gsm[:, 24:32]; ek = gsm[:, 32:40]; bk = gsm[:, 40:48]; nbet = gsm[:, 48:56]
                base = 4 * hg
                bA, bB, bC, bD = base, base + 1, base + 2, base + 3
                pb = psb(bA)
                P.add("pe", [P.tr(pb[:, j * 128:(j + 1) * 128], kT[:, 4 * hg + j, tok], ident_b[:]) for j in range(4)], [("kT", h) for h in hs] + ["ident_b"], [psk(bA)])
                P.copy("act", ktok, pb[:, 0:512], [psk(bA)], [K("ktok")])
                pb2 = psb(bB)
                P.add("pe", [P.tr(pb2[:, j * 128:(j + 1) * 128], vT[:, 4 * hg + j, tok], ident_b[:]) for j in range(4)], [("vT", h) for h in hs] + ["ident_b"], [psk(bB)])
                P.copy("dve", vtok, pb2[:, 0:512], [psk(bB)], [K("vtok")])
                yield
                for j in range(4):
                    P.act(t["gU"][:, j * 128:(j + 1) * 128], U_f, AF.Copy, ["cst", ("gat", tt_)], [K("gU")], scale=gat[:, tt_, 4 * hg + j:4 * hg + j + 1])
                P.add("pe", [P.mm(ps[bC][:, :], ones_f, t["gU"], True, True)], [K("gU"), "cst"], [psk(bC)])
                P.tt("dve", v3(t["Z"]), v3(ps[bC][:, :]), bc(gc, hg), ALU.subtract, [psk(bC), "gc"], [K("Z")])
                P.act(t["gU"], ps[bC][:, :], AF.Exp, [psk(bC)], [K("gU")])
                yield
                P.stt(v3(t["E1"]), v3(t["Z"]), 0.0, Ub[:].unsqueeze(1).broadcast_to([128, 4, 128]), ALU.min, ALU.add, [K("Z"), "Ub"], [K("E1")])
                P.stt(v3(t["E2"]), v3(t["Z"]), 0.0, Lp[:].unsqueeze(1).broadcast_to([128, 4, 128]), ALU.max, ALU.add, [K("Z"), "Lp"], [K("E2")])
                P.act(t["E1"], t["E1"], AF.Exp, [K("E1")], [K("E1")])
                P.act(t["E2"], t["E2"], AF.Exp, [K("E2")], [K("E2")], scale=-1.0)
                P.tt("dve", v3(qg), qT[:, 4 * hg:4 * hg + 4, tok], v3(t["gU"]), ALU.mult, [("qT", h) for h in hs] + [K("gU")], [K("qg")])
                yield
                P.add("pe", [P.mm(ps[bA][:, j * 128:(j + 1) * 128], kT[:, 4 * hg + j, tok], kT[:, 4 * hg + j, tok], True, True) for j in range(4)], [("kT", h) for h in hs], [psk(bA)])
                P.add("pe", [P.mm(ps[bB][:, j * 128:(j + 1) * 128], kT[:, 4 * hg + j, tok], qT[:, 4 * hg + j, tok], True, True) for j in range(4)], [("kT", h) for h in hs] + [("qT", h) for h in hs], [psk(bB)])
                P.tt("dve", v3(t["Z"]), v3(ps[bA][:, :]), bc(nbet, hg), ALU.mult, [psk(bA), "nbet"], [K("Z")])
                P.tt("dve", t["Mc"], t["Z"], t["E2"], ALU.mult, [K("Z"), K("E2")], [K("Mc")])
                P.tt("dve", t["AT"], ps[bB][:, :], t["E1"], ALU.mult, [psk(bB), K("E1")], [K("AT")])
                yield
                pb = psb(bC)
                P.add("pe", [P.tr(pb[:, j * 128:(j + 1) * 128], t["Mc"][:, j * 128:(j + 1) * 128], ident_b[:]) for j in range(4)], [K("Mc"), "ident_b"], [psk(bC)])
                P.copy("act", t["Nc"], pb[:, 0:512], [psk(bC)], [K("Nc")])
                for j in range(4):
                    P.tt("dve", t["Pc"][:, j * 128:(j + 1) * 128], pb[:, j * 128:(j + 1) * 128], ident_f, ALU.add, [psk(bC), "cst"], [K("Pc")])
                yield
                Mn = G[:, 5, :]; Nn = G[:, 6, :]
                Mc, Nc, Pc = t["Mc"], t["Nc"], t["Pc"]
                kM, kN, kP = K("Mc"), K("Nc"), K("Pc")
                kMn, kNn = K("wTn"), K("vn")
                for lvl in range(1, 7):
                    P.add("pe", [P.mm(ps[bA][:, j * 128:(j + 1) * 128], Nc[:, j * 128:(j + 1) * 128], Mc[:, j * 128:(j + 1) * 128], True, True) for j in range(4)], [kM, kN], [psk(bA)])
                    if lvl < 6:
                        P.add("pe", [P.mm(ps[bB][:, j * 128:(j + 1) * 128], Mc[:, j * 128:(j + 1) * 128], Nc[:, j * 128:(j + 1) * 128], True, True) for j in range(4)], [kM, kN], [psk(bB)])
                    P.copy("act", Mn, ps[bA][:, :], [psk(bA)], [kMn])
                    if lvl < 6:
                        P.copy("dve", Nn, ps[bB][:, :], [psk(bB)], [kNn])
                    yield
                    fns = []
                    for j in range(4):
                        fns.append(P.mm(ps[bC][:, j * 128:(j + 1) * 128], ident_b[:], Pc[:, j * 128:(j + 1) * 128], True, False))
                        fns.append(P.mm(ps[bC][:, j * 128:(j + 1) * 128], Mn[:, j * 128:(j + 1) * 128], Pc[:, j * 128:(j + 1) * 128], False, True))
                    P.add("pe", fns, [kMn, kP, "ident_b"], [psk(bC)])
                    P.copy("pool" if False else "dve", Pc, ps[bC][:, :], [psk(bC)], [kP])
                    Mc, Mn = Mn, Mc; kM, kMn = kMn, kM
                    Nc, Nn = Nn, Nc; kN, kNn = kNn, kN
                    yield
                TT = Pc
                P.tt("dve", v3(kbg), v3(ktok), bc(bk, hg), ALU.mult, [K("ktok"), "bk"], [K("kbg")])
                for j in range(4):
                    hh2 = 4 * hg + j
                    P.act(vb[:, j * 128:(j + 1) * 128], vtok[:, j * 128:(j + 1) * 128], AF.Copy, [K("vtok"), ("bet", tt_)], [K("vb")], scale=bet[:, tt_, hh2:hh2 + 1])
                    P.act(kt2[:, j * 128:(j + 1) * 128], ktok[:, j * 128:(j + 1) * 128], AF.Copy, [K("ktok"), "ek"], [K("kt2")], scale=ek[:, hh2:hh2 + 1])
                yield
                P.add("pe", [P.mm(ps[bA][:, j * 128:(j + 1) * 128], kbg[:, j * 128:(j + 1) * 128], TT[:, j * 128:(j + 1) * 128], True, True) for j in range(4)], [K("kbg"), kP], [psk(bA)])
                P.ts("dve", wTn, ps[bA][:, :], -1.0, None, ALU.mult, [psk(bA)], [K("wTn")])
                yield
                fns = []
                for j in range(4):
                    fns.append(P.mm(ps[bB][:, j * 128:(j + 1) * 128], TT[:, j * 128:(j + 1) * 128], vb[:, j * 128:(j + 1) * 128], True, False))
                    fns.append(P.mm(ps[bB][:, j * 128:(j + 1) * 128], wTn[:, j * 128:(j + 1) * 128], S_b[:, 4 * hg + j, :], False, True))
                P.add("pe", fns, [kP, K("vb"), K("wTn"), ("S_b", hg)], [psk(bB)])
                P.copy("act", vn, ps[bB][:, :], [psk(bB)], [K("vn")])
                yield
                fns = []
                for j in range(4):
                    fns.append(P.mm(ps[bC][:, j * 128:(j + 1) * 128], qg[:, j * 128:(j + 1) * 128], S_b[:, 4 * hg + j, :], True, False))
                    fns.append(P.mm(ps[bC][:, j * 128:(j + 1) * 128], t["AT"][:, j * 128:(j + 1) * 128], vn[:, j * 128:(j + 1) * 128], False, True))
                P.add("pe", fns, [K("qg"), K("AT"), K("vn"), ("S_b", hg)], [psk(bC)])
                P.add("pe", [P.mm(ps[bD][:, j * 128:(j + 1) * 128], kt2[:, j * 128:(j + 1) * 128], vn[:, j * 128:(j + 1) * 128], True, True) for j in range(4)], [K("kt2"), K("vn")], [psk(bD)])
                Sv = S_f[:, 4 * hg:4 * hg + 4, :]
                for j in range(4):
                    hh2 = 4 * hg + j
                    P.stt(S_f[:, hh2, :], S_f[:, hh2, :], egl[:, hh2:hh2 + 1], ps[bD][:, j * 128:(j + 1) * 128], ALU.mult, ALU.add, [("S_f", hg), "eg", psk(bD)], [("S_f", hg)])
                yield
                P.copy("act", S_b[:, 4 * hg:4 * hg + 4, :].rearrange("p h d -> p (h d)"), Sv.rearrange("p h d -> p (h d)"), [("S_f", hg)], [("S_b", hg)])
                o32 = t["Z"]
                P.copy("act", o32, ps[bC][:, :], [psk(bC)], [K("Z")])
                sm = gsm[:, 56:60] if hg == 0 else gsm[:, 60:64]
                for j in range(4):
                    P.act(t["E1"][:, j * 128:(j + 1) * 128], o32[:, j * 128:(j + 1) * 128], AF.Square, [K("Z")], [K("E1"), K("osq")], accum_out=sm[:, j:j + 1])
                yield
                P.ts("dve", sm, sm, 1.0 / 128.0, EPS, ALU.mult, [K("osq")], [K("osq")], op1=ALU.add)
                P.act(sm, sm, AF.Sqrt, [K("osq")], [K("osq")])
                P.add("dve", lambda e: e.reciprocal(out=sm, in_=sm), [K("osq")], [K("osq")])
                P.tt("dve", v3(o32), v3(o32), sm.unsqueeze(2).broadcast_to([128, 4, 128]), ALU.mult, [K("Z"), K("osq")], [K("Z")])
                P.tt("dve", v3(o32), v3(o32), gdng[:].unsqueeze(1).broadcast_to([128, 4, 128]), ALU.mult, [K("Z"), "gdng"], [K("Z")])
                P.tt("dve", o32, o32, gzs[:, tt_, hg * 512:(hg + 1) * 512], ALU.mult, [K("Z"), ("gzs", tt_)], [K("Z")])
                ot = tt_ // 2
                dsl = oa_sel[:, ot, hg * 512:(hg + 1) * 512]
                if tt_ % 2 == 0:
                    P.ts("dve", dsl, o32, flg[:, 0:1], None, ALU.mult, [K("Z"), "flg"], [("oa_sel", ot, hg)])
                else:
                    P.stt(dsl, o32, flg[:, 1:2], dsl, ALU.mult, ALU.add, [K("Z"), "flg", ("oa_sel", ot, hg)], [("oa_sel", ot, hg)])
                yield

            for tt_ in range(TPM):
                gdn_chunk_prep(tt_)
                gens = [gdn_hg(tt_, 0), gdn_hg(tt_, 1)]
                alive = [True, True]
                while any(alive):
                    for gi in range(2):
                        if alive[gi]:
                            try:
                                next(gens[gi])
                            except StopIteration:
                                alive[gi] = False
            P.barrier()
            if stop <= 3:
                break

            for ot in range(2):
                oi = 2 * m + ot
                S_i = 256 * (oi + 1)
                nkb = (S_i + 511) // 512
                for kb in range(nkb):
                    k0 = kb * 512; n = min(512, S_i - k0)
                    for pr in range(8):
                        b0 = 2 * (pr % 2); b1 = b0 + 1
                        P.add("pe", [P.mm(ps[b0][:, 0:n], qiT[0:64, ot, pr, :], kiT_res[0:64, k0:k0 + n], True, True)], [("qiT", ot), "kiT_all"], [psk(b0)])
                        P.add("pe", [P.mm(ps[b1][:, 0:n], qiT[64:128, ot, pr, :], kiT_res[64:128, k0:k0 + n], True, True)], [("qiT", ot), "kiT_all"], [psk(b1)])
                        for hh_, bb in ((2 * pr, b0), (2 * pr + 1, b1)):
                            r_ = rl[hh_ % 2]; rk = ("rl", hh_ % 2)
                            P.act(r_[:, 0:n], ps[bb][:, 0:n], AF.Relu, [psk(bb)], [rk])
                            if hh_ == 0:
                                P.ts("dve", sc[:, k0:k0 + n], r_[:, 0:n], wq[:, ot, 0:1], None, ALU.mult, [rk, ("wq", ot)], [("sc", kb)])
                            else:
                                P.stt(sc[:, k0:k0 + n], r_[:, 0:n], wq[:, ot, hh_:hh_ + 1], sc[:, k0:k0 + n], ALU.mult, ALU.add, [rk, ("wq", ot), ("sc", kb)], [("sc", kb)])
                sck = [("sc", kb) for kb in range(nkb)]
                am = sml[:, 8:9]; lo = sml[:, 9:10]; hi = sml[:, 10:11]; d0 = sml[:, 11:12]; mid = sml[:, 12:13]; cnt = sml[:, 13:14]; gg = sml[:, 14:15]
                P.add("dve", lambda e, S_i=S_i: e.tensor_reduce(out=am, in_=sc[:, 0:S_i], axis=AX.X, op=ALU.max, apply_absolute_value=True), sck, ["am"])
                P.ts("dve", lo, am, -1.0, -1.0, ALU.mult, ["am"], ["lo"], op1=ALU.add)
                P.tt("dve", sc[:, S_i - 256:S_i], sc[:, S_i - 256:S_i], cmask[:], ALU.add, sck + ["cmask"], sck)
                P.add("dve", lambda e, S_i=S_i: e.tensor_reduce(out=hi, in_=sc[:, 0:S_i], axis=AX.X, op=ALU.max), sck, ["hi"])
                P.tt("dve", d0, hi, lo, ALU.subtract, ["hi", "lo"], ["d0"])
                for it in range(NIT):
                    f = 2.0 ** (-(it + 1))
                    P.stt(mid, d0, f, lo, ALU.mult, ALU.add, ["d0", "lo"], ["mid"])
                    P.ts("dve", mb[:, 0:S_i], sc[:, 0:S_i], mid, None, ALU.is_gt, sck + ["mid"], ["mb", "cnt"], op1=ALU.add, accum_out=cnt)
                    P.ts("dve", gg, cnt, 255.5, d0, ALU.is_ge, ["cnt", "d0"], ["gg"], op1=ALU.mult)
                    P.stt(lo, gg, f, lo, ALU.mult, ALU.add, ["gg", "lo"], ["lo"])
                P.ts("dve", mb[:, 0:S_i], sc[:, 0:S_i], lo, -30000.0, ALU.is_le, sck + ["lo"], ["mb"], op1=ALU.mult)
                nsb = S_i // 128
                oacc = [ps[4], ps[5], ps[6]]
                hslot = {0: (0, 0), 1: (0, 1), 2: (0, 2), 3: (1, 0), 4: (1, 1), 5: (1, 2), 6: (2, 0), 7: (2, 1)}
                started = [False, False, False]
                scale = 128.0 ** -0.5
                it_ = 0
                for sbk in range(nsb):
                    for g in range(2):
                        bs = it_ % 2; it_ += 1
                        P.add("pe", [P.mm(ps[bs][:, :], kT_res[:, g, sbk * 128:(sbk + 1) * 128], qTo[:, ot, 4 * g:4 * g + 4, :].rearrange("p h q -> p (h q)"), True, False),
                                     P.mm(ps[bs][:, :], mb[:, sbk * 128:(sbk + 1) * 128], identrep[:].rearrange("p h q -> p (h q)"), False, True)],
                              ["kT_all", ("qTo", ot), "mb", "identrep"], [psk(bs)])
                        P.act(PT[bs], ps[bs][:, :], AF.Exp, [psk(bs)], [("PT", bs)], scale=scale)
                        fns = []
                        for j in range(4):
                            hd = 4 * g + j; bi, sl = hslot[hd]
                            stf = not started[bi]
                            started[bi] = True
                            fns.append(P.mm(oacc[bi][:, sl * 130:sl * 130 + 129], PT[bs][:, j * 128:(j + 1) * 128], v1_res[:, sbk, g, 0:129], stf, sbk == nsb - 1, skip_group_check=True))
                        P.add("pe", fns, [("PT", bs), "v1_all"], [psk(4), psk(5), psk(6)])
                den = sml[:, 16:24]
                for bi, nh in ((0, 3), (1, 3), (2, 2)):
                    ov = oacc[bi][:, 0:nh * 130].rearrange("p (h c) -> p h c", c=130)
                    P.copy("dve", den[:, 3 * bi:3 * bi + nh], ov[:, :, 128], [psk(4 + bi)], [("den", bi)])
                    P.add("dve", lambda e, bi=bi, nh=nh: e.reciprocal(out=den[:, 3 * bi:3 * bi + nh], in_=den[:, 3 * bi:3 * bi + nh]), [("den", bi)], [("den", bi)])
                    o32 = rl[0][:, 0:nh * 128].rearrange("p (h d) -> p h d", d=128)
                    P.tt("dve", o32, ov[:, :, 0:128], den[:, 3 * bi:3 * bi + nh].unsqueeze(2).broadcast_to([128, nh, 128]), ALU.mult, [psk(4 + bi), ("den", bi)], [("rl", 0)])
                    c0 = 3 * bi * 128
                    P.tt("dve", otok[:, 1024 + c0:1024 + c0 + nh * 128], rl[0][:, 0:nh * 128], azs[:, ot, c0:c0 + nh * 128], ALU.mult, [("rl", 0), ("azs", ot)], [("otok", ot)])
                P.copy("pool", otok[:, 0:1024], oa_sel[:, ot, :], [("oa_sel", ot, 0), ("oa_sel", ot, 1)], [("otok", ot)])
                if debug:
                    P.dma("sp", dbg_d[oi * 128:(oi + 1) * 128, :], otok[:], [("otok", ot)], [])
                for q in range(4):
                    b = 2 + (q % 2); pb = psb(b)
                    P.add("pe", [P.tr(pb[:, j * 128:(j + 1) * 128], otok[:, (4 * q + j) * 128:(4 * q + j + 1) * 128], ident_b[:]) for j in range(4)], [("otok", ot), "ident_b"], [psk(b)])
                    for j in range(4):
                        P.copy("dve" if j % 2 == 0 else "act", mixT[:, 4 * q + j, ot * 128:(ot + 1) * 128], pb[:, j * 128:(j + 1) * 128], [psk(b)], [("mixT", ot, 4 * q + j)])
                P.barrier()

            if stop <= 4:
                break
            for ot in range(2):
                oi = 2 * m + ot
                P.dma("sp", yres[:, ot, :], xo_d[oi * 128:(oi + 1) * 128, :], [], [("y", ot)])
            P.dma("sp", xt[:], fing_d.partition_broadcast(128), [], ["fing"])
            oblocks = [(wout_d, blk * 256, 256, ("o", blk)) for blk in range(8)]

            def body5(wi, info):
                blk = info[1]
                for ot in range(2):
                    b = nbank()
                    P.add("pe", [P.mm(ps[b][:, 0:256], mixT[:, kc, ot * 128:(ot + 1) * 128], wbuf[wi][:, kc, 0:256], kc == 0, kc == KC - 1) for kc in range(KC)],
                          [("mixT", ot, kc) for kc in range(KC)] + [("wb", wi)], [psk(b)])
                    ysl = yres[:, ot, blk * 256:(blk + 1) * 256]
                    P.tt("dve", ysl, ysl, ps[b][:, 0:256], ALU.add, [psk(b), ("y", ot)], [("y", ot)])
            stream(oblocks, body5)
            for ot in range(2):
                oi = 2 * m + ot
                yk = ("y", ot)
                P.act(xs[:], yres[:, ot, :], AF.Square, [yk], ["xs", "fs"], accum_out=sml[:, 24:25])
                P.ts("dve", sml[:, 25:26], sml[:, 24:25], 1.0 / D, EPS, ALU.mult, ["fs"], ["fs1"], op1=ALU.add)
                P.act(sml[:, 26:27], sml[:, 25:26], AF.Sqrt, ["fs1"], ["fs2"])
                P.add("dve", lambda e: e.reciprocal(out=sml[:, 27:28], in_=sml[:, 26:27]), ["fs2"], ["fs3"])
                P.stt(yres[:, ot, :], yres[:, ot, :], sml[:, 27:28], xt[:], ALU.mult, ALU.mult, [yk, "fs3", "fing"], [yk])
                P.dma("sp", out_d[oi * 128:(oi + 1) * 128, :], yres[:, ot, :], [yk], [])
            P.barrier()

        P.finish_waits("sp")
        P.emit()
    return nc


def _prep_shared(inputs):
    f32 = np.float32
    w_in = np.asarray(inputs["w_in"], f32)[0]
    o = dict(gq=0, gk=1024, gv=2048, gz=3072, ga=4096, gb=4104, aq=4112, ak=5136, av=5392, az=5648, iq=6672, ik=7696, iw=7760)
    order = [("gq", 1024), ("gk", 1024), ("gv", 1024), ("gz", 1024), ("ak", 256), ("av", 256), ("ga", 8), ("gb", 8), ("ik", 64),
             ("aq", 1024), ("az", 1024), ("iq", 1024), ("iw", 16)]
    cols = np.concatenate([np.arange(o[n], o[n] + s) for n, s in order])
    wall3 = w_in[:, cols].reshape(KC, 128, NCOL).transpose(1, 0, 2)
    bounds = [(b * 256, 256) for b in range(18)] + [(C_SM, 80)] + [(C_AQ + b * 256, 256) for b in range(12)] + [(C_IW, 16)]
    wall = np.ascontiguousarray(np.concatenate([wall3[:, :, c0:c0 + n].reshape(128, KC * n) for c0, n in bounds], axis=1))
    wout3 = np.asarray(inputs["w_out"], f32)[0].reshape(KC, 128, D).transpose(1, 0, 2)
    wout = np.ascontiguousarray(np.concatenate([wout3[:, :, b * 256:(b + 1) * 256].reshape(128, KC * 256) for b in range(8)], axis=1))
    ii = np.arange(128)
    ident = (ii[:, None] == ii[None, :]).astype(f32)
    U = (ii[:, None] <= ii[None, :]).astype(f32)
    Ls = (ii[:, None] > ii[None, :]).astype(f32)
    cst = np.ascontiguousarray(np.concatenate([ident, U, Ls, np.ones((128, 128), f32)], axis=1))
    gn = np.ascontiguousarray(np.asarray(inputs["attn_norm_g"], f32)[0].reshape(KC, 128).T)
    cw = np.ascontiguousarray(np.asarray(inputs["gdn_conv_w"], f32)[0].reshape(4, 24, 128).transpose(2, 1, 0).reshape(128, 96))
    invf = (np.float32(500000.0) ** (-(np.arange(16, dtype=f32) * f32(2.0) / f32(32.0)))).astype(f32)
    return dict(wall=wall, wout=wout, cst=cst, gn=gn, cw=cw, invf=invf,
                fing=np.ascontiguousarray(np.asarray(inputs["final_norm_g"], f32)),
                gdng=np.ascontiguousarray(np.asarray(inputs["gdn_norm_g"], f32)[0]),
                alog=np.ascontiguousarray(np.asarray(inputs["gdn_a_log"], f32)[0]),
                dtb=np.ascontiguousarray(np.asarray(inputs["gdn_dt_bias"], f32)[0]))


def _core_inputs(inputs, shared, b, hh):
    f32 = np.float32
    x = np.asarray(inputs["x"], f32)[b]
    pos = np.asarray(inputs["positions"], np.int32)[b]
    xo = np.ascontiguousarray(x.reshape(NT, 128, D)[hh::2].reshape(T // 2, D))
    pos_t = np.ascontiguousarray(pos.reshape(NT, 128).T)
    poso = np.ascontiguousarray(pos.reshape(NT, 128)[hh::2].T)
    ii = np.arange(128)
    tril = np.where(ii[None, :] <= ii[:, None], 0.0, -3.0e38).astype(f32)
    NEG = np.full((128, 128), -3.0e38, f32); Z = np.zeros((128, 128), f32)
    cmask = np.concatenate([tril, NEG], 1) if hh == 0 else np.concatenate([Z, tril], 1)
    flg = np.tile(np.array([[1.0 - hh, float(hh)]], f32), (128, 1))
    d = dict(shared)
    d.update(x=np.ascontiguousarray(x), xo=xo, pos=pos_t, poso=poso, cmask=np.ascontiguousarray(cmask), flg=np.ascontiguousarray(flg))
    return d


_NC_CACHE = {}


def kernel(**inputs):
    _nco = inputs.pop("_ncores", None)
    debug = bool(inputs.pop("_debug", False))
    n_macro = int(inputs.pop("_n_macro", NM))
    stop = int(inputs.pop("_stop", 99))
    nblk = int(inputs.pop("_nblk", 99))
    key = (debug, n_macro, stop, nblk)
    if key not in _NC_CACHE:
        _NC_CACHE[key] = build(debug=debug, n_macro=n_macro, stop=stop, nblk=nblk)
    nc = _NC_CACHE[key]
    shared = _prep_shared(inputs)
    ncores = int(_nco) if _nco is not None else 8
    in_maps = [_core_inputs(inputs, shared, c // 2, c % 2) for c in range(ncores)]
    if ncores < 8:
        res = run_bass_kernel_spmd(nc, in_maps, core_ids=list(range(ncores)), trace=True)
        print("EXEC_NS", res.exec_time_ns)
        return [res.results[c] for c in range(ncores)]
    res = run_bass_kernel_spmd(nc, in_maps, core_ids=list(range(ncores)))
    out = np.zeros((4, NT, 128, D), np.float32)
    for c in range(8):
        b, hh = c // 2, c % 2
        out[b, hh::2] = np.asarray(res.results[c]["out"], np.float32).reshape(NOWN, 128, D)
    out = out.reshape(4, T, D)
    if debug:
        dbg = [np.asarray(res.results[c]["dbg"]) for c in range(8)]
        return out, dbg
    return out
```

```python
from contextlib import ExitStack
import math
import numpy as np
import concourse.bass as bass
import concourse.mybir as mybir

F32 = mybir.dt.float32
BF16 = mybir.dt.bfloat16
I32 = mybir.dt.int32
ALU = mybir.AluOpType
AF = mybir.ActivationFunctionType
AX = mybir.AxisListType

ENGS = ("pe", "act", "dve", "pool", "sp")
SEM_EPOCH = 4000
DMA_SLOTS = 6
DMA_EPOCH = 1500


class Prog:
    def __init__(self, nc, stack):
        self.nc = nc
        self.stack = stack
        self.streams = {e: [] for e in ENGS}
        self.cur = {e: None for e in ENGS}
        self.waited = {e: {} for e in ENGS}
        self.lw = {}
        self.rd = {}
        self.nsem = 0
        self.slots = {e: [[None, 0, None] for _ in range(DMA_SLOTS)] for e in ENGS}
        self.slot_i = {e: 0 for e in ENGS}
        self.n_inst = 0

    def _newsem(self):
        self.nsem += 1
        return self.stack.enter_context(self.nc.semaphore(f"s{self.nsem}"))

    def add(self, eng, fns, r=(), w=(), dma=False):
        if not isinstance(fns, (list, tuple)):
            fns = [fns]
        r = list(r)
        w = list(w) + [k for k in r if isinstance(k, tuple) and k and k[0] == "ps" and k not in w]
        deps = []
        for k in r:
            t = self.lw.get(k)
            if t is not None:
                deps.append(t)
        for k in w:
            t = self.lw.get(k)
            if t is not None:
                deps.append(t)
            for s, v in self.rd.get(k, {}).items():
                deps.append((s, v))
        if dma:
            i = self.slot_i[eng]
            self.slot_i[eng] = (i + 1) % DMA_SLOTS
            slot = self.slots[eng][i]
            if slot[0] is None or slot[1] >= 16 * DMA_EPOCH:
                if slot[2] is not None:
                    deps.append(slot[2])
                slot[0] = self._newsem()
                slot[1] = 0
            elif slot[2] is not None:
                deps.append(slot[2])
            slot[1] += 16
            tok = (slot[0], slot[1])
            slot[2] = tok
            inc = 16
        else:
            c = self.cur[eng]
            if c is None or c[1] >= SEM_EPOCH:
                c = self.cur[eng] = [self._newsem(), 0]
            c[1] += 1
            tok = (c[0], c[1])
            inc = 1
        waits = []
        wd = self.waited[eng]
        own = self.cur[eng][0] if (self.cur[eng] is not None) else None
        for s, v in deps:
            if eng == "pe" and not dma and s is own:
                continue
            if wd.get(id(s), 0) < v:
                wd[id(s)] = v
                waits.append((s, v))
        ww = {}
        for s, v in waits:
            if id(s) not in ww or ww[id(s)][1] < v:
                ww[id(s)] = (s, v)
        self.streams[eng].append((list(ww.values()), fns, tok[0], inc))
        self.n_inst += len(fns) + len(ww)
        for k in w:
            self.lw[k] = tok
            self.rd[k] = {}
        for k in r:
            if k in w:
                continue
            d = self.rd.setdefault(k, {})
            d[tok[0]] = tok[1]
        return tok

    def barrier(self):
        toks = []
        for e in ENGS:
            if self.cur[e] is not None:
                toks.append((self.cur[e][0], self.cur[e][1]))
            for slot in self.slots[e]:
                if slot[2] is not None:
                    toks.append(slot[2])
        for e in ENGS:
            wd = self.waited[e]
            waits = []
            for s, v in toks:
                if v > 0 and wd.get(id(s), 0) < v:
                    wd[id(s)] = v
                    waits.append((s, v))
            self.streams[e].append((waits, [], None, 0))
            self.n_inst += len(waits)
        self.lw.clear()
        self.rd.clear()

    def finish_waits(self, eng="sp"):
        waits = []
        for e in ENGS:
            for slot in self.slots[e]:
                if slot[2] is not None:
                    waits.append(slot[2])
        self.streams[eng].append((waits, [], None, 0))

    def emit(self):
        nc = self.nc
        streams = self.streams

        def replay(name, eng):
            for waits, fns, sem, inc in streams[name]:
                for s, v in waits:
                    eng.wait_ge(s, v)
                for f in fns[:-1]:
                    f(eng)
                if fns:
                    fns[-1](eng).then_inc(sem, inc)

        with nc.Block() as block:
            @block.tensor
            def _(e):
                replay("pe", e)

            @block.scalar
            def _(e):
                replay("act", e)

            @block.vector
            def _(e):
                replay("dve", e)

            @block.gpsimd
            def _(e):
                replay("pool", e)

            @block.sync
            def _(e):
                replay("sp", e)

    def dma(self, eng, out, in_, r, w, **kw):
        return self.add(eng, lambda e: e.dma_start(out=out, in_=in_, **kw), r, w, dma=True)

    def act(self, out, in_, func, r, w, eng="act", **kw):
        return self.add(eng, lambda e: e.activation(out=out, in_=in_, func=func, **kw), r, w)

    def ts(self, eng, out, in0, s1, s2, op0, r, w, op1=None, **kw):
        if op1 is None:
            return self.add(eng, lambda e: e.tensor_scalar(out=out, in0=in0, scalar1=s1, scalar2=None, op0=op0, **kw), r, w)
        return self.add(eng, lambda e: e.tensor_scalar(out=out, in0=in0, scalar1=s1, scalar2=s2, op0=op0, op1=op1, **kw), r, w)

    def tt(self, eng, out, in0, in1, op, r, w):
        return self.add(eng, lambda e: e.tensor_tensor(out=out, in0=in0, in1=in1, op=op), r, w)

    def stt(self, out, in0, scalar, in1, op0, op1, r, w, **kw):
        return self.add("dve", lambda e: e.scalar_tensor_tensor(out=out, in0=in0, scalar=scalar, in1=in1, op0=op0, op1=op1, **kw), r, w)

    def copy(self, eng, out, in_, r, w):
        if eng == "act":
            return self.add(eng, lambda e: e.copy(out=out, in_=in_), r, w)
        return self.add(eng, lambda e: e.tensor_copy(out=out, in_=in_), r, w)

    def memset(self, eng, ap, val, w):
        return self.add(eng, lambda e: e.memset(ap, val), (), w)

    def mm(self, out, lhsT, rhs, start, stop, **kw):
        return lambda e: e.matmul(out, lhsT, rhs, start=start, stop=stop, **kw)

    def tr(self, out, in_, ident):
        return lambda e: e.transpose(out, in_, ident)
from concourse.bass_utils import run_bass_kernel_spmd


D = 2048; KC = 16; T = 4096; NT = 32; TM = 512; NM = 8; TPM = 4; NOWN = 16
EPS = 1e-6
C_QKV = 0; C_GZ = 3072; C_KV = 4096; C_SM = 4608; C_AQ = 4688; C_AZ = 5712; C_IQ = 6736; C_IW = 7760; NCOL = 7776
NIT = 24
TWO_PI = 2.0 * math.pi


def build(debug=False, n_macro=NM, stop=99, nblk=99, dma_only=False):
    nc = bass.Bass("TRN2", target_bir_lowering=False)
    dt_in = lambda n, s, d=F32: nc.dram_tensor(n, s, d, kind="ExternalInput").ap()
    x_d = dt_in("x", [T, D]); xo_d = dt_in("xo", [T // 2, D])
    wall_d = dt_in("wall", [128, KC * NCOL]); wout_d = dt_in("wout", [128, KC * D])
    cst_d = dt_in("cst", [128, 4 * 128])
    gn_d = dt_in("gn", [128, KC]); fing_d = dt_in("fing", [D]); gdng_d = dt_in("gdng", [128])
    alog_d = dt_in("alog", [8]); dtb_d = dt_in("dtb", [8]); cw_d = dt_in("cw", [128, 24 * 4])
    pos_d = dt_in("pos", [128, NT], I32); poso_d = dt_in("poso", [128, NOWN], I32)
    invf_d = dt_in("invf", [16]); flg_d = dt_in("flg", [128, 2]); cmask_d = dt_in("cmask", [128, 256])
    out_d = nc.dram_tensor("out", [T // 2, D], F32, kind="ExternalOutput").ap()
    if debug:
        dbg_d = nc.dram_tensor("dbg", [T // 2, D], BF16, kind="ExternalOutput").ap()

    with ExitStack() as st:
        P = Prog(nc, st)
        sb = lambda n, s, d: st.enter_context(nc.sbuf_tensor("s_" + n, s, d))
        ps = [st.enter_context(nc.psum_tensor(f"ps{i}", [128, 512], F32)) for i in range(8)]
        psk = lambda i: ("ps", i)
        psb = lambda i: ps[i][:, :].bitcast(BF16)

        ra = sb("ra", [128, 4, 32], F32); rb = sb("rb", [128, 4, 32], F32); rtmp = sb("rtmp", [128, 256], BF16)
        cst = sb("cst", [128, 512], F32)
        ident_f = cst[:, 0:128]; U_f = cst[:, 128:256]; Ls_f = cst[:, 256:384]; ones_f = cst[:, 384:512]
        ident_b = sb("ident_b", [128, 128], BF16); ones_b = sb("ones_b", [128, 128], BF16)
        identrep = sb("identrep", [128, 4, 128], BF16)
        gn = sb("gn", [128, KC], F32); gdng = sb("gdng", [128, 128], F32)
        alog = sb("alog", [128, 8], F32); dtb = sb("dtb", [128, 8], F32); negA = sb("negA", [128, 8], F32)
        cw = sb("cw", [128, 24, 4], F32)
        flg = sb("flg", [128, 2], F32); cmask = sb("cmask", [128, 256], F32)
        invf = sb("invf", [128, 16], F32)
        cc = sb("cc", [128, NT, 32], F32); ns = sb("ns", [128, NT, 32], F32)
        cci = sb("cci", [128, NT, 16], F32); nsi = sb("nsi", [128, NT, 16], F32)
        cco = sb("cco", [128, NOWN, 32], F32); nso = sb("nso", [128, NOWN, 32], F32)
        ccio = sb("ccio", [128, NOWN, 16], F32); nsio = sb("nsio", [128, NOWN, 16], F32)
        kT_res = sb("kT_res", [128, 2, T], BF16)
        v1_res = sb("v1_res", [128, NT, 2, 130], BF16)
        kiT_res = sb("kiT_res", [128, T], BF16)
        halo = sb("halo", [128, 24, 4], F32)
        S_f = sb("S_f", [128, 8, 128], F32); S_b = sb("S_b", [128, 8, 128], BF16)
        xt = sb("xt", [128, D], F32); xs = sb("xs", [128, D], BF16)
        sml = sb("sml", [128, 64], F32)
        scrA = sb("scrA", [128, 8192], BF16)
        scrB = sb("scrB", [128, 4096], BF16)
        scrC = sb("scrC", [128, 4096], BF16)
        wbuf = [sb(f"wbuf{i}", [128, KC, 256], BF16) for i in range(3)]
        qT = sb("qT", [128, 8, TM], BF16); kT = sb("kT", [128, 8, TM], BF16); vT = sb("vT", [128, 8, TM], BF16)
        gzs = sb("gzs", [128, TPM, 1024], BF16)
        gat = sb("gat", [128, TPM, 8], F32); bet = sb("bet", [128, TPM, 8], F32)
        qTo = sb("qTo", [128, 2, 8, 128], BF16); azs = sb("azs", [128, 2, 1024], BF16)
        qiT = sb("qiT", [128, 2, 8, 128], BF16); wq = sb("wq", [128, 2, 16], F32)
        oa_sel = sb("oa_sel", [128, 2, 1024], BF16)
        otok = sb("otok", [128, D], BF16)
        mixT = sb("mixT", [128, KC, 256], BF16)
        gsm = sb("gsm", [128, 64], F32)

        hT = scrA[:, :].rearrange("p (k t) -> p k t", k=KC)
        hTo = scrB[:, :].rearrange("p (k t) -> p k t", k=KC)
        sc = scrA[:, :].bitcast(F32)
        mb = scrB[:, :]
        yres = scrA[:, :].bitcast(F32).rearrange("p (o d) -> p o d", o=2)
        scrCf = scrC[:, :].bitcast(F32)
        raw = [scrCf[:, 0:516], scrCf[:, 516:1032]]
        cacc = [xt[:, 0:512], xt[:, 512:1024]]
        rl = cacc
        silt = xt[:, 1024:1536]
        PT = [xs[:, 0:512], xs[:, 512:1024]]
        rtt = xt[:, 1536:2048]; sqt = sb("sqt", [128, 512], BF16)
        silt_b = sb("silt_b", [128, 512], F32); sqt_b = sb("sqt_b", [128, 512], BF16)
        silt2 = [silt, silt_b[:, :]]; sqt2 = [sqt[:, :], sqt_b[:, :]]

        P.dma("sp", cst[:], cst_d, [], ["cst"])
        P.dma("sp", gn[:], gn_d, [], ["gn"])
        P.dma("sp", gdng[:], gdng_d.partition_broadcast(128), [], ["gdng"])
        P.dma("sp", alog[:], alog_d.partition_broadcast(128), [], ["alog"])
        P.dma("sp", dtb[:], dtb_d.partition_broadcast(128), [], ["dtb"])
        P.dma("sp", cw[:], cw_d.rearrange("p (c i) -> p c i", i=4), [], ["cw"])
        P.dma("sp", flg[:], flg_d, [], ["flg"])
        P.dma("sp", cmask[:], cmask_d, [], ["cmask"])
        P.dma("sp", invf[:], invf_d.partition_broadcast(128), [], ["invf"])
        P.copy("dve", ident_b[:], ident_f, ["cst"], ["ident_b"])
        P.copy("dve", ones_b[:], ones_f, ["cst"], ["ones_b"])
        P.copy("dve", identrep[:], ident_f.unsqueeze(1).broadcast_to([128, 4, 128]), ["cst"], ["identrep"])
        P.act(negA[:], alog[:], AF.Exp, ["alog"], ["negA"])
        P.ts("dve", negA[:], negA[:], -1.0, None, ALU.mult, ["negA"], ["negA"])
        Ub = sb("Ub", [128, 128], F32); Lp = sb("Lp", [128, 128], F32)
        P.ts("dve", Ub[:], U_f, 30000.0, -30000.0, ALU.mult, ["cst"], ["Ub"], op1=ALU.add)
        P.ts("dve", Lp[:], Ls_f, -30000.0, 30000.0, ALU.mult, ["cst"], ["Lp"], op1=ALU.add)
        P.memset("pool", halo[:], 0.0, ["halo"])
        P.memset("pool", S_f[:], 0.0, ["S_f"])
        P.memset("pool", S_b[:], 0.0, ["S_b"])
        P.memset("pool", v1_res[:], 1.0, ["v1_res"])

        A32s = scrA[:, :].bitcast(F32)

        def make_tables(pos_dram, ntl, cc_, ns_, cci_, nsi_, tag):
            pi_ = sb("pi" + tag, [128, ntl], I32); pf = sb("pf" + tag, [128, ntl], F32)
            ne = ntl * 32
            AR = A32s[:, 0:ne].rearrange("p (t a f) -> p t a f", a=2, f=16)
            KI = A32s[:, 1024:1024 + ne].bitcast(I32).rearrange("p (t a f) -> p t a f", a=2, f=16)
            KF = A32s[:, 2048:2048 + ne].rearrange("p (t a f) -> p t a f", a=2, f=16)
            tag = ""
            P.dma("sp", pi_[:], pos_dram, [], ["pi" + tag])
            P.copy("dve", pf[:], pi_[:], ["pi" + tag], ["pf" + tag])
            P.tt("dve", AR[:, :, 0, :], pf[:].unsqueeze(2).broadcast_to([128, ntl, 16]),
                 invf[:].unsqueeze(1).broadcast_to([128, ntl, 16]), ALU.mult, ["pf" + tag, "invf"], ["AR" + tag])
            P.ts("dve", AR[:, :, 1, :], AR[:, :, 0, :], math.pi / 2, None, ALU.add, ["AR" + tag], ["AR" + tag])
            P.ts("dve", KF[:], AR[:], 1.0 / TWO_PI, None, ALU.mult, ["AR" + tag], ["KF" + tag])
            P.copy("dve", KI[:], KF[:], ["KF" + tag], ["KI" + tag])
            P.copy("dve", KF[:], KI[:], ["KI" + tag], ["KF" + tag])
            P.stt(AR[:], KF[:], -TWO_PI, AR[:], ALU.mult, ALU.add, ["KF" + tag, "AR" + tag], ["AR" + tag])
            P.ts("dve", AR[:], AR[:], 3.14159, -3.14159, ALU.min, ["AR" + tag], ["AR" + tag], op1=ALU.max)
            P.act(AR[:], AR[:], AF.Sin, ["AR" + tag], ["AR" + tag])
            k = "AR" + tag
            P.copy("dve", cc_[:, :, 0:16], AR[:, :, 1, :], [k], [("cc", tag)])
            P.copy("dve", cc_[:, :, 16:32], AR[:, :, 1, :], [k], [("cc", tag)])
            P.ts("dve", ns_[:, :, 0:16], AR[:, :, 0, :], -1.0, None, ALU.mult, [k], [("ns", tag)])
            P.copy("dve", ns_[:, :, 16:32], AR[:, :, 0, :], [k], [("ns", tag)])
            P.copy("dve", cci_[:, :, 0:8], AR[:, :, 1, 0:16:2], [k], [("cci", tag)])
            P.copy("dve", cci_[:, :, 8:16], AR[:, :, 1, 0:16:2], [k], [("cci", tag)])
            P.ts("dve", nsi_[:, :, 0:8], AR[:, :, 0, 0:16:2], -1.0, None, ALU.mult, [k], [("nsi", tag)])
            P.copy("dve", nsi_[:, :, 8:16], AR[:, :, 0, 0:16:2], [k], [("nsi", tag)])

        make_tables(pos_d, NT, cc, ns, cci, nsi, "a")
        P.barrier()
        make_tables(poso_d, NOWN, cco, nso, ccio, nsio, "o")
        P.barrier()

        wstate = {"i": 0}

        def wload(src, c0, n):
            i = wstate["i"] % 3
            wstate["i"] += 1
            blk_ap = src[:, KC * c0:KC * (c0 + n)].rearrange("p (k c) -> p k c", k=KC)
            for h in range(2):
                P.dma("pool", wbuf[i][:, h * 8:(h + 1) * 8, 0:n], blk_ap[:, h * 8:(h + 1) * 8, :], [], [("wb", i)])
            return i

        def prefetch(blocks, k=3):
            return [wload(*blocks[j][:3]) for j in range(min(k, len(blocks)))]

        def stream(blocks, body, ahead=2, pre=None):
            nb = len(blocks)
            if pre:
                loaded = list(pre)
                for j in range(nb):
                    body(loaded[j], blocks[j][3])
                    if len(loaded) < nb:
                        loaded.append(wload(*blocks[len(loaded)][:3]))
                return
            loaded = []
            for j in range(min(ahead, nb)):
                loaded.append(wload(*blocks[j][:3]))
            for j in range(nb):
                if j + ahead < nb:
                    loaded.append(wload(*blocks[j + ahead][:3]))
                body(loaded[j], blocks[j][3])

        bank_rr = {"i": 0}

        def nbank(lo=0, hi=8):
            b = lo + bank_rr["i"] % (hi - lo)
            bank_rr["i"] += 1
            return b

        def norm_tile(src_rows, hview, hkey, col0):
            P.dma("sp", xt[:], src_rows, [], ["xt"])
            P.act(xs[:], xt[:], AF.Square, ["xt"], ["xs", "ssq"], accum_out=sml[:, 0:1])
            P.ts("dve", sml[:, 1:2], sml[:, 0:1], 1.0 / D, EPS, ALU.mult, ["ssq"], ["ssq1"], op1=ALU.add)
            P.act(sml[:, 2:3], sml[:, 1:2], AF.Sqrt, ["ssq1"], ["ssq2"])
            P.add("dve", lambda e: e.reciprocal(out=sml[:, 3:4], in_=sml[:, 2:3]), ["ssq2"], ["rstd"])
            P.ts("dve", xs[:], xt[:], sml[:, 3:4], None, ALU.mult, ["xt", "rstd"], ["xs"])
            for q in range(4):
                b = nbank()
                pb = psb(b)
                P.add("pe", [P.tr(pb[:, j * 128:(j + 1) * 128], xs[:, (4 * q + j) * 128:(4 * q + j + 1) * 128], ident_b[:]) for j in range(4)],
                      ["xs", "ident_b"], [psk(b)])
                P.tt("dve", hview[:, 4 * q:4 * q + 4, col0:col0 + 128], pb[:, 0:512].rearrange("p (j t) -> p j t", j=4),
                     gn[:, 4 * q:4 * q + 4].unsqueeze(2).broadcast_to([128, 4, 128]), ALU.mult, [psk(b), "gn"], [hkey])

        DBG = 99

        def rope(dst, src, H, half, cct, nst, rk, wk, dst2=None, src2=None):
            h2 = 2 * half
            W = dst2.shape[1]
            dh = W // H
            rf = xs[:, :].bitcast(F32)[:, 0:W]
            if DBG >= 1:
                P.copy("act", rf, src2, rk, ["rf"])
            if DBG >= 2:
                P.copy("act", dst2, rf, ["rf"], [wk])
            fa = []
            for h in range(H):
                o = h * dh
                fa.append(lambda e, h=h, o=o: e.tensor_tensor(out=ra[:, h, 0:h2], in0=rf[:, o:o + h2], in1=cct, op=ALU.mult))
                fa.append(lambda e, h=h, o=o: e.tensor_tensor(out=rb[:, h, 0:half], in0=rf[:, o + half:o + h2], in1=nst[:, 0:half], op=ALU.mult))
                fa.append(lambda e, h=h, o=o: e.tensor_tensor(out=rb[:, h, half:h2], in0=rf[:, o:o + half], in1=nst[:, half:h2], op=ALU.mult))
            if DBG >= 3:
                P.add("dve", fa, ["rf", "tabs"], ["ra", "rb"])
            fb = [(lambda e, h=h, o=h * dh: e.tensor_tensor(out=dst2[:, o:o + h2], in0=ra[:, h, 0:h2], in1=rb[:, h, 0:h2], op=ALU.add)) for h in range(H)]
            if DBG >= 4:
                P.add("dve", fb, ["ra", "rb", wk], [wk])

        def in_blocks():
            bl = []
            for blk in range(12):
                bl.append((wall_d, C_QKV + blk * 256, 256, ("qkv", blk)))
            for blk in range(4):
                bl.append((wall_d, C_GZ + blk * 256, 256, ("gz", blk)))
            bl.append((wall_d, C_KV, 256, ("ak", 0)))
            bl.append((wall_d, C_KV + 256, 256, ("av", 0)))
            bl.append((wall_d, C_SM, 80, ("sm", 0)))
            for blk in range(4):
                bl.append((wall_d, C_AQ + blk * 256, 256, ("aq", blk)))
            for blk in range(4):
                bl.append((wall_d, C_AZ + blk * 256, 256, ("az", blk)))
            for blk in range(4):
                bl.append((wall_d, C_IQ + blk * 256, 256, ("iq", blk)))
            bl.append((wall_d, C_IW, 16, ("iw", 0)))
            return bl

        oblocks_all = [(wout_d, blk * 256, 256, ("o", blk)) for blk in range(8)]
        pre_in = prefetch(in_blocks()) if (stop >= 2 and nblk >= 3 and not dma_only) else None

        for m in range(n_macro if stop > 0 else 0):
            for tt_ in range(TPM):
                r0 = m * TM + tt_ * 128
                norm_tile(x_d[r0:r0 + 128, :], hT, "hT", tt_ * 128)
            for ot in range(2):
                r0 = (2 * m + ot) * 128
                norm_tile(xo_d[r0:r0 + 128, :], hTo, "hTo", ot * 128)
            P.barrier()
            if stop <= 1:
                break

            blocks = []
            for blk in range(12):
                blocks.append((wall_d, C_QKV + blk * 256, 256, ("qkv", blk)))
            for blk in range(4):
                blocks.append((wall_d, C_GZ + blk * 256, 256, ("gz", blk)))
            blocks.append((wall_d, C_KV, 256, ("ak", 0)))
            blocks.append((wall_d, C_KV + 256, 256, ("av", 0)))
            blocks.append((wall_d, C_SM, 80, ("sm", 0)))
            for blk in range(4):
                blocks.append((wall_d, C_AQ + blk * 256, 256, ("aq", blk)))
            for blk in range(4):
                blocks.append((wall_d, C_AZ + blk * 256, 256, ("az", blk)))
            for blk in range(4):
                blocks.append((wall_d, C_IQ + blk * 256, 256, ("iq", blk)))
            blocks.append((wall_d, C_IW, 16, ("iw", 0)))

            def tok_mm(b, hv, hk, col0, wi, n):
                P.add("pe", [P.mm(ps[b][:, 0:n], hv[:, kc, col0:col0 + 128], wbuf[wi][:, kc, 0:n], kc == 0, kc == KC - 1) for kc in range(KC)],
                      [hk, ("wb", wi)], [psk(b)])

            pending = []

            def flush_pending():
                for f_ in pending:
                    f_()
                del pending[:]

            def body2(wi, info):
                kind, blk = info
                if dma_only:
                    return
                if kind != "qkv":
                    flush_pending()
                if kind == "qkv":
                    for c2 in range(2):
                        ct = blk * 2 + c2
                        b = nbank()
                        P.add("pe", [P.mm(ps[b][:, :], wbuf[wi][:, kc, c2 * 128:(c2 + 1) * 128], hT[:, kc, :], kc == 0, kc == KC - 1) for kc in range(KC)],
                              ["hT", ("wb", wi)], [psk(b)])
                        flush_pending()
                        rw = raw[ct % 2]; rk = ("raw", ct % 2); ac = cacc[ct % 2]; ak_ = ("cacc", ct % 2)
                        P.copy("act", rw[:, 3:515], ps[b][:, :], [psk(b)], [rk])
                        P.copy("pool", rw[:, 0:3], halo[:, ct, 0:3], [("halo", ct)], [rk])
                        P.copy("pool", halo[:, ct, 0:3], rw[:, 512:515], [rk], [("halo", ct)])
                        P.ts("dve", ac, rw[:, 3:515], cw[:, ct, 3:4], None, ALU.mult, [rk, "cw"], [ak_])
                        for i in (2, 1, 0):
                            P.stt(ac, rw[:, i:i + 512], cw[:, ct, i:i + 1], ac, ALU.mult, ALU.add, [rk, "cw", ak_], [ak_])
                        if ct < 16:
                            isq = ct < 8; hd = ct % 8
                            sl_ = silt2[ct % 2]; sk_ = ("silt", ct % 2); sq_ = sqt2[ct % 2]; qk_ = ("sqt", ct % 2)
                            P.act(sl_, ac, AF.Silu, [ak_], [sk_])
                            P.act(sq_, sl_, AF.Square, [sk_], [qk_])

                            def tail(isq=isq, hd=hd, sl_=sl_, sk_=sk_, sq_=sq_, qk_=qk_):
                                b2 = nbank()
                                P.add("pe", [P.mm(ps[b2][:, :], ones_b[:], sq_, True, True)], [qk_, "ones_b"], [psk(b2)])
                                P.act(rtt, ps[b2][:, :], AF.Sqrt, [psk(b2)], ["rtt"], scale=(128.0 if isq else 1.0), bias=(128.0 * EPS if isq else EPS))
                                P.add("dve", lambda e: e.reciprocal(out=rtt, in_=rtt), ["rtt"], ["rtt"])
                                dstT = qT if isq else kT
                                P.tt("dve", dstT[:, hd, :], sl_, rtt, ALU.mult, [sk_, "rtt"], [("qT" if isq else "kT", hd)])
                            pending.append(tail)
                        else:
                            P.act(vT[:, ct - 16, :], ac, AF.Silu, [ak_], [("vT", ct - 16)])
                elif kind == "gz":
                    for tt_ in range(TPM):
                        b = nbank()
                        tok_mm(b, hT, "hT", tt_ * 128, wi, 256)
                        P.act(gzs[:, tt_, blk * 256:(blk + 1) * 256], ps[b][:, 0:256], AF.Silu, [psk(b)], [("gzs", tt_)])
                elif kind == "ak":
                    for tt_ in range(TPM):
                        b = nbank(); gt = m * TPM + tt_
                        tok_mm(b, hT, "hT", tt_ * 128, wi, 256)
                        LV = 9 if DBG >= 6 else (3 if DBG >= 5 else 2)
                        kv_ = rtmp[:, :].rearrange("p (h d) -> p h d", h=2)
                        if LV >= 2:
                            rope(kv_, ps[b][:, 0:256].rearrange("p (h d) -> p h d", h=2), 2, 16, cc[:, gt, :], ns[:, gt, :], [psk(b)], "rtmp", rtmp[:, 0:256], ps[b][:, 0:256])
                        elif LV >= 1:
                            P.copy("act", rtmp[:, 0:256], ps[b][:, 0:256], [psk(b)], ["rtmp"])
                        b2 = nbank(); pb = psb(b2)
                        if LV >= 3:
                            P.add("pe", [P.tr(pb[:, g * 128:(g + 1) * 128], rtmp[:, g * 128:(g + 1) * 128], ident_b[:]) for g in range(2)], ["rtmp", "ident_b"], [psk(b2)])
                        if LV >= 4:
                            for g in range(2):
                                if DBG == 7 and g == 1:
                                    continue
                                if DBG == 8 and g == 0:
                                    continue
                                eng_ = "dve" if g == 0 else "act"
                                if DBG == 9:
                                    eng_ = "dve"
                                P.copy(eng_, kT_res[:, g, gt * 128:(gt + 1) * 128], pb[:, g * 128:(g + 1) * 128], [psk(b2)], [("kT_res", gt, g)])
                elif kind == "av":
                    for tt_ in range(TPM):
                        b = nbank(); gt = m * TPM + tt_
                        tok_mm(b, hT, "hT", tt_ * 128, wi, 256)
                        for g in range(2):
                            P.copy("dve" if g == 0 else "act", v1_res[:, gt, g, 0:128], ps[b][:, g * 128:(g + 1) * 128], [psk(b)], [("v1_res", gt, g)])
                elif kind == "sm":
                    for tt_ in range(TPM):
                        b = nbank(); gt = m * TPM + tt_
                        tok_mm(b, hT, "hT", tt_ * 128, wi, 80)
                        pk = [psk(b)]
                        P.tt("dve", gsm[:, 0:8], ps[b][:, 0:8], dtb[:], ALU.add, pk + ["dtb"], ["g0"])
                        P.act(gsm[:, 8:16], gsm[:, 0:8], AF.Abs, ["g0"], ["g1"])
                        P.act(gsm[:, 16:24], gsm[:, 8:16], AF.Exp, ["g1"], ["g2"], scale=-1.0)
                        P.act(gsm[:, 24:32], gsm[:, 16:24], AF.Ln, ["g2"], ["g3"], bias=1.0)
                        P.stt(gsm[:, 32:40], gsm[:, 0:8], 0.0, gsm[:, 24:32], ALU.max, ALU.add, ["g0", "g3"], ["g4"])
                        P.tt("dve", gat[:, tt_, :], gsm[:, 32:40], negA[:], ALU.mult, ["g4", "negA"], [("gat", tt_)])
                        P.act(bet[:, tt_, :], ps[b][:, 8:16], AF.Sigmoid, pk, [("bet", tt_)])
                        ikv = rtmp[:, 0:64].rearrange("p (h d) -> p h d", h=1)
                        rope(ikv, ps[b][:, 16:80].rearrange("p (h d) -> p h d", h=1), 1, 8, cci[:, gt, :], nsi[:, gt, :], pk, "rtmp", rtmp[:, 0:64], ps[b][:, 16:80])
                        P.copy("pool", rtmp[:, 64:128], rtmp[:, 0:64], ["rtmp"], ["rtmp"])
                        b2 = nbank(); pb = psb(b2)
                        P.add("pe", [P.tr(pb[:, 0:128], rtmp[:, 0:128], ident_b[:])], ["rtmp", "ident_b"], [psk(b2)])
                        P.copy("dve", kiT_res[:, gt * 128:(gt + 1) * 128], pb[:, 0:128], [psk(b2)], [("kiT_res", gt)])
                elif kind == "aq":
                    for ot in range(2):
                        b = nbank(); oi = 2 * m + ot
                        tok_mm(b, hTo, "hTo", ot * 128, wi, 256)
                        qv = rtmp[:, :].rearrange("p (h d) -> p h d", h=2)
                        rope(qv, ps[b][:, 0:256].rearrange("p (h d) -> p h d", h=2), 2, 16, cco[:, oi, :], nso[:, oi, :], [psk(b)], "rtmp", rtmp[:, 0:256], ps[b][:, 0:256])
                        b2 = nbank(); pb = psb(b2)
                        P.add("pe", [P.tr(pb[:, g * 128:(g + 1) * 128], rtmp[:, g * 128:(g + 1) * 128], ident_b[:]) for g in range(2)], ["rtmp", "ident_b"], [psk(b2)])
                        P.copy("dve", qTo[:, ot, 2 * blk:2 * blk + 2, :].rearrange("p g t -> p (g t)"), pb[:, 0:256], [psk(b2)], [("qTo", ot)])
                elif kind == "az":
                    for ot in range(2):
                        b = nbank()
                        tok_mm(b, hTo, "hTo", ot * 128, wi, 256)
                        P.act(azs[:, ot, blk * 256:(blk + 1) * 256], ps[b][:, 0:256], AF.Silu, [psk(b)], [("azs", ot)])
                elif kind == "iq":
                    for ot in range(2):
                        b = nbank(); oi = 2 * m + ot
                        tok_mm(b, hTo, "hTo", ot * 128, wi, 256)
                        qv = rtmp[:, :].rearrange("p (h d) -> p h d", h=4)
                        rope(qv, ps[b][:, 0:256].rearrange("p (h d) -> p h d", h=4), 4, 8, ccio[:, oi, :], nsio[:, oi, :], [psk(b)], "rtmp", rtmp[:, 0:256], ps[b][:, 0:256])
                        b2 = nbank(); pb = psb(b2)
                        P.add("pe", [P.tr(pb[:, g * 128:(g + 1) * 128], rtmp[:, g * 128:(g + 1) * 128], ident_b[:]) for g in range(2)], ["rtmp", "ident_b"], [psk(b2)])
                        P.copy("dve", qiT[:, ot, 2 * blk:2 * blk + 2, :].rearrange("p g t -> p (g t)"), pb[:, 0:256], [psk(b2)], [("qiT", ot)])
                elif kind == "iw":
                    for ot in range(2):
                        b = nbank()
                        tok_mm(b, hTo, "hTo", ot * 128, wi, 16)
                        P.ts("dve", wq[:, ot, :], ps[b][:, 0:16], 1.0 / 32.0, None, ALU.mult, [psk(b)], [("wq", ot)])

            stream(blocks[:nblk], body2, pre=pre_in)
            pre_in = None
            P.barrier()
            if stop <= 2:
                break

            A32 = scrA[:, :].bitcast(F32)
            B32 = scrB[:, :].bitcast(F32)

            def tmpl(hg):
                o = hg * 2048
                t = {}
                t["Z"] = A32[:, o:o + 512]; t["E1"] = A32[:, o + 512:o + 1024]; t["E2"] = A32[:, o + 1024:o + 1536]
                t["gU"] = A32[:, o + 1536:o + 2048]
                ob = hg * 2048
                for i, nme in enumerate(["Mc", "Nc", "Pc", "AT"]):
                    t[nme] = scrB[:, ob + i * 512: ob + (i + 1) * 512]
                return t
            gtmp = [scrC[:, :].rearrange("p (a b) -> p a b", a=8), mixT[:, :, :].rearrange("p k t -> p (k t)").rearrange("p (a b) -> p a b", a=8)]

            def gdn_chunk_prep(tt_):
                b = nbank()
                P.add("pe", [P.mm(ps[b][:, 0:8], U_f, gat[:, tt_, :], True, True), P.mm(ps[b][:, 8:16], ones_f, gat[:, tt_, :], True, True)],
                      [("gat", tt_), "cst"], [psk(b)])
                P.copy("dve", gsm[:, 0:16], ps[b][:, 0:16], [psk(b)], ["gc"])
                P.act(gsm[:, 16:32], gsm[:, 0:16], AF.Exp, ["gc"], ["eg"])
                P.tt("dve", gsm[:, 32:40], gsm[:, 8:16], gsm[:, 0:8], ALU.subtract, ["gc"], ["ekl"])
                P.act(gsm[:, 32:40], gsm[:, 32:40], AF.Exp, ["ekl"], ["ek"])
                P.tt("dve", gsm[:, 40:48], bet[:, tt_, :], gsm[:, 16:24], ALU.mult, [("bet", tt_), "eg"], ["bk"])
                P.ts("dve", gsm[:, 48:56], bet[:, tt_, :], -1.0, None, ALU.mult, [("bet", tt_)], ["nbet"])

            def bc(ap8, hg):
                return ap8[:, 4 * hg:4 * hg + 4].unsqueeze(2).broadcast_to([128, 4, 128])

            def v3(ap):
                return ap.rearrange("p (h d) -> p h d", h=4)

            def gdn_hg(tt_, hg):
                t = tmpl(hg); K = lambda n: (n, hg)
                G = gtmp[hg]
                ktok = G[:, 0, :]; vtok = G[:, 1, :]; qg = G[:, 2, :]; kbg = G[:, 3, :]; vb = G[:, 4, :]; wTn = G[:, 5, :]; vn = G[:, 6, :]; kt2 = G[:, 7, :]
                tok = slice(tt_ * 128, (tt_ + 1) * 128)
                hs = range(4 * hg, 4 * hg + 4)
                gc = gsm[:, 0:8]; eg = gsm[:, 16:24]; egl = gsm[:, 24:32]; ek = gsm[:, 32:40]; bk = gsm[:, 40:48]; nbet = gsm[:, 48:56]
                base = 4 * hg
                bA, bB, bC, bD = base, base + 1, base + 2, base + 3
                pb = psb(bA)
                P.add("pe", [P.tr(pb[:, j * 128:(j + 1) * 128], kT[:, 4 * hg + j, tok], ident_b[:]) for j in range(4)], [("kT", h) for h in hs] + ["ident_b"], [psk(bA)])
                P.copy("act", ktok, pb[:, 0:512], [psk(bA)], [K("ktok")])
                pb2 = psb(bB)
                P.add("pe", [P.tr(pb2[:, j * 128:(j + 1) * 128], vT[:, 4 * hg + j, tok], ident_b[:]) for j in range(4)], [("vT", h) for h in hs] + ["ident_b"], [psk(bB)])
                P.copy("dve", vtok, pb2[:, 0:512], [psk(bB)], [K("vtok")])
                yield
                for j in range(4):
                    P.act(t["gU"][:, j * 128:(j + 1) * 128], U_f, AF.Copy, ["cst", ("gat", tt_)], [K("gU")], scale=gat[:, tt_, 4 * hg + j:4 * hg + j + 1])
                P.add("pe", [P.mm(ps[bC][:, :], ones_f, t["gU"], True, True)], [K("gU"), "cst"], [psk(bC)])
                P.tt("dve", v3(t["Z"]), v3(ps[bC][:, :]), bc(gc, hg), ALU.subtract, [psk(bC), "gc"], [K("Z")])
                P.act(t["gU"], ps[bC][:, :], AF.Exp, [psk(bC)], [K("gU")])
                yield
                P.stt(v3(t["E1"]), v3(t["Z"]), 0.0, Ub[:].unsqueeze(1).broadcast_to([128, 4, 128]), ALU.min, ALU.add, [K("Z"), "Ub"], [K("E1")])
                P.stt(v3(t["E2"]), v3(t["Z"]), 0.0, Lp[:].unsqueeze(1).broadcast_to([128, 4, 128]), ALU.max, ALU.add, [K("Z"), "Lp"], [K("E2")])
                P.act(t["E1"], t["E1"], AF.Exp, [K("E1")], [K("E1")])
                P.act(t["E2"], t["E2"], AF.Exp, [K("E2")], [K("E2")], scale=-1.0)
                P.tt("dve", v3(qg), qT[:, 4 * hg:4 * hg + 4, tok], v3(t["gU"]), ALU.mult, [("qT", h) for h in hs] + [K("gU")], [K("qg")])
                yield
                P.add("pe", [P.mm(ps[bA][:, j * 128:(j + 1) * 128], kT[:, 4 * hg + j, tok], kT[:, 4 * hg + j, tok], True, True) for j in range(4)], [("kT", h) for h in hs], [psk(bA)])
                P.add("pe", [P.mm(ps[bB][:, j * 128:(j + 1) * 128], kT[:, 4 * hg + j, tok], qT[:, 4 * hg + j, tok], True, True) for j in range(4)], [("kT", h) for h in hs] + [("qT", h) for h in hs], [psk(bB)])
                P.tt("dve", v3(t["Z"]), v3(ps[bA][:, :]), bc(nbet, hg), ALU.mult, [psk(bA), "nbet"], [K("Z")])
                P.tt("dve", t["Mc"], t["Z"], t["E2"], ALU.mult, [K("Z"), K("E2")], [K("Mc")])
                P.tt("dve", t["AT"], ps[bB][:, :], t["E1"], ALU.mult, [psk(bB), K("E1")], [K("AT")])
                yield
                pb = psb(bC)
                P.add("pe", [P.tr(pb[:, j * 128:(j + 1) * 128], t["Mc"][:, j * 128:(j + 1) * 128], ident_b[:]) for j in range(4)], [K("Mc"), "ident_b"], [psk(bC)])
                P.copy("act", t["Nc"], pb[:, 0:512], [psk(bC)], [K("Nc")])
                for j in range(4):
                    P.tt("dve", t["Pc"][:, j * 128:(j + 1) * 128], pb[:, j * 128:(j + 1) * 128], ident_f, ALU.add, [psk(bC), "cst"], [K("Pc")])
                yield
                Mn = G[:, 5, :]; Nn = G[:, 6, :]
                Mc, Nc, Pc = t["Mc"], t["Nc"], t["Pc"]
                kM, kN, kP = K("Mc"), K("Nc"), K("Pc")
                kMn, kNn = K("wTn"), K("vn")
                for lvl in range(1, 7):
                    P.add("pe", [P.mm(ps[bA][:, j * 128:(j + 1) * 128], Nc[:, j * 128:(j + 1) * 128], Mc[:, j * 128:(j + 1) * 128], True, True) for j in range(4)], [kM, kN], [psk(bA)])
                    if lvl < 6:
                        P.add("pe", [P.mm(ps[bB][:, j * 128:(j + 1) * 128], Mc[:, j * 128:(j + 1) * 128], Nc[:, j * 128:(j + 1) * 128], True, True) for j in range(4)], [kM, kN], [psk(bB)])
                    P.copy("act", Mn, ps[bA][:, :], [psk(bA)], [kMn])
                    if lvl < 6:
                        P.copy("dve", Nn, ps[bB][:, :], [psk(bB)], [kNn])
                    yield
                    fns = []
                    for j in range(4):
                        fns.append(P.mm(ps[bC][:, j * 128:(j + 1) * 128], ident_b[:], Pc[:, j * 128:(j + 1) * 128], True, False))
                        fns.append(P.mm(ps[bC][:, j * 128:(j + 1) * 128], Mn[:, j * 128:(j + 1) * 128], Pc[:, j * 128:(j + 1) * 128], False, True))
                    P.add("pe", fns, [kMn, kP, "ident_b"], [psk(bC)])
                    P.copy("pool" if False else "dve", Pc, ps[bC][:, :], [psk(bC)], [kP])
                    Mc, Mn = Mn, Mc; kM, kMn = kMn, kM
                    Nc, Nn = Nn, Nc; kN, kNn = kNn, kN
                    yield
                TT = Pc
                P.tt("dve", v3(kbg), v3(ktok), bc(bk, hg), ALU.mult, [K("ktok"), "bk"], [K("kbg")])
                for j in range(4):
                    hh2 = 4 * hg + j
                    P.act(vb[:, j * 128:(j + 1) * 128], vtok[:, j * 128:(j + 1) * 128], AF.Copy, [K("vtok"), ("bet", tt_)], [K("vb")], scale=bet[:, tt_, hh2:hh2 + 1])
                    P.act(kt2[:, j * 128:(j + 1) * 128], ktok[:, j * 128:(j + 1) * 128], AF.Copy, [K("ktok"), "ek"], [K("kt2")], scale=ek[:, hh2:hh2 + 1])
                yield
                P.add("pe", [P.mm(ps[bA][:, j * 128:(j + 1) * 128], kbg[:, j * 128:(j + 1) * 128], TT[:, j * 128:(j + 1) * 128], True, True) for j in range(4)], [K("kbg"), kP], [psk(bA)])
                P.ts("dve", wTn, ps[bA][:, :], -1.0, None, ALU.mult, [psk(bA)], [K("wTn")])
                yield
                fns = []
                for j in range(4):
                    fns.append(P.mm(ps[bB][:, j * 128:(j + 1) * 128], TT[:, j * 128:(j + 1) * 128], vb[:, j * 128:(j + 1) * 128], True, False))
                    fns.append(P.mm(ps[bB][:, j * 128:(j + 1) * 128], wTn[:, j * 128:(j + 1) * 128], S_b[:, 4 * hg + j, :], False, True))
                P.add("pe", fns, [kP, K("vb"), K("wTn"), ("S_b", hg)], [psk(bB)])
                P.copy("act", vn, ps[bB][:, :], [psk(bB)], [K("vn")])
                yield
                fns = []
                for j in range(4):
                    fns.append(P.mm(ps[bC][:, j * 128:(j + 1) * 128], qg[:, j * 128:(j + 1) * 128], S_b[:, 4 * hg + j, :], True, False))
                    fns.append(P.mm(ps[bC][:, j * 128:(j + 1) * 128], t["AT"][:, j * 128:(j + 1) * 128], vn[:, j * 128:(j + 1) * 128], False, True))
                P.add("pe", fns, [K("qg"), K("AT"), K("vn"), ("S_b", hg)], [psk(bC)])
                P.add("pe", [P.mm(ps[bD][:, j * 128:(j + 1) * 128], kt2[:, j * 128:(j + 1) * 128], vn[:, j * 128:(j + 1) * 128], True, True) for j in range(4)], [K("kt2"), K("vn")], [psk(bD)])
                Sv = S_f[:, 4 * hg:4 * hg + 4, :]
                for j in range(4):
                    hh2 = 4 * hg + j
                    P.stt(S_f[:, hh2, :], S_f[:, hh2, :], egl[:, hh2:hh2 + 1], ps[bD][:, j * 128:(j + 1) * 128], ALU.mult, ALU.add, [("S_f", hg), "eg", psk(bD)], [("S_f", hg)])
                yield
                P.copy("act", S_b[:, 4 * hg:4 * hg + 4, :].rearrange("p h d -> p (h d)"), Sv.rearrange("p h d -> p (h d)"), [("S_f", hg)], [("S_b", hg)])
                o32 = t["Z"]
                P.copy("act", o32, ps[bC][:, :], [psk(bC)], [K("Z")])
                sm = gsm[:, 56:60] if hg == 0 else gsm[:, 60:64]
                for j in range(4):
                    P.act(t["E1"][:, j * 128:(j + 1) * 128], o32[:, j * 128:(j + 1) * 128], AF.Square, [K("Z")], [K("E1"), K("osq")], accum_out=sm[:, j:j + 1])
                yield
                P.ts("dve", sm, sm, 1.0 / 128.0, EPS, ALU.mult, [K("osq")], [K("osq")], op1=ALU.add)
                P.act(sm, sm, AF.Sqrt, [K("osq")], [K("osq")])
                P.add("dve", lambda e: e.reciprocal(out=sm, in_=sm), [K("osq")], [K("osq")])
                P.tt("dve", v3(o32), v3(o32), sm.unsqueeze(2).broadcast_to([128, 4, 128]), ALU.mult, [K("Z"), K("osq")], [K("Z")])
                P.tt("dve", v3(o32), v3(o32), gdng[:].unsqueeze(1).broadcast_to([128, 4, 128]), ALU.mult, [K("Z"), "gdng"], [K("Z")])
                P.tt("dve", o32, o32, gzs[:, tt_, hg * 512:(hg + 1) * 512], ALU.mult, [K("Z"), ("gzs", tt_)], [K("Z")])
                ot = tt_ // 2
                dsl = oa_sel[:, ot, hg * 512:(hg + 1) * 512]
                if tt_ % 2 == 0:
                    P.ts("dve", dsl, o32, flg[:, 0:1], None, ALU.mult, [K("Z"), "flg"], [("oa_sel", ot, hg)])
                else:
                    P.stt(dsl, o32, flg[:, 1:2], dsl, ALU.mult, ALU.add, [K("Z"), "flg", ("oa_sel", ot, hg)], [("oa_sel", ot, hg)])
                yield

            for tt_ in range(TPM):
                gdn_chunk_prep(tt_)
                gens = [gdn_hg(tt_, 0), gdn_hg(tt_, 1)]
                alive = [True, True]
                while any(alive):
                    for gi in range(2):
                        if alive[gi]:
                            try:
                                next(gens[gi])
                            except StopIteration:
                                alive[gi] = False
            P.barrier()
            if stop <= 3:
                break

            pre_out = prefetch(oblocks_all) if stop > 4 else None
            for ot in range(2):
                oi = 2 * m + ot
                S_i = 256 * (oi + 1)
                nkb = (S_i + 511) // 512
                for kb in range(nkb):
                    k0 = kb * 512; n = min(512, S_i - k0)
                    for pr in range(8):
                        b0 = 2 * (pr % 2); b1 = b0 + 1
                        P.add("pe", [P.mm(ps[b0][:, 0:n], qiT[0:64, ot, pr, :], kiT_res[0:64, k0:k0 + n], True, True)], [("qiT", ot), "kiT_all"], [psk(b0)])
                        P.add("pe", [P.mm(ps[b1][:, 0:n], qiT[64:128, ot, pr, :], kiT_res[64:128, k0:k0 + n], True, True)], [("qiT", ot), "kiT_all"], [psk(b1)])
                        for hh_, bb in ((2 * pr, b0), (2 * pr + 1, b1)):
                            r_ = rl[hh_ % 2]; rk = ("rl", hh_ % 2)
                            P.act(r_[:, 0:n], ps[bb][:, 0:n], AF.Relu, [psk(bb)], [rk])
                            if hh_ == 0:
                                P.ts("dve", sc[:, k0:k0 + n], r_[:, 0:n], wq[:, ot, 0:1], None, ALU.mult, [rk, ("wq", ot)], [("sc", kb)])
                            else:
                                P.stt(sc[:, k0:k0 + n], r_[:, 0:n], wq[:, ot, hh_:hh_ + 1], sc[:, k0:k0 + n], ALU.mult, ALU.add, [rk, ("wq", ot), ("sc", kb)], [("sc", kb)])
                sck = [("sc", kb) for kb in range(nkb)]
                am = sml[:, 8:9]; lo = sml[:, 9:10]; hi = sml[:, 10:11]; d0 = sml[:, 11:12]; mid = sml[:, 12:13]; cnt = sml[:, 13:14]; gg = sml[:, 14:15]
                P.add("dve", lambda e, S_i=S_i: e.tensor_reduce(out=am, in_=sc[:, 0:S_i], axis=AX.X, op=ALU.max, apply_absolute_value=True), sck, ["am"])
                P.ts("dve", lo, am, -1.0, -1.0, ALU.mult, ["am"], ["lo"], op1=ALU.add)
                P.tt("dve", sc[:, S_i - 256:S_i], sc[:, S_i - 256:S_i], cmask[:], ALU.add, sck + ["cmask"], sck)
                P.add("dve", lambda e, S_i=S_i: e.tensor_reduce(out=hi, in_=sc[:, 0:S_i], axis=AX.X, op=ALU.max), sck, ["hi"])
                P.tt("dve", d0, hi, lo, ALU.subtract, ["hi", "lo"], ["d0"])
                for it in range(NIT):
                    f = 2.0 ** (-(it + 1))
                    P.stt(mid, d0, f, lo, ALU.mult, ALU.add, ["d0", "lo"], ["mid"])
                    P.ts("dve", mb[:, 0:S_i], sc[:, 0:S_i], mid, None, ALU.is_gt, sck + ["mid"], ["mb", "cnt"], op1=ALU.add, accum_out=cnt)
                    P.ts("dve", gg, cnt, 255.5, d0, ALU.is_ge, ["cnt", "d0"], ["gg"], op1=ALU.mult)
                    P.stt(lo, gg, f, lo, ALU.mult, ALU.add, ["gg", "lo"], ["lo"])
                P.ts("dve", mb[:, 0:S_i], sc[:, 0:S_i], lo, -30000.0, ALU.is_le, sck + ["lo"], ["mb"], op1=ALU.mult)
                nsb = S_i // 128
                oacc = [ps[4], ps[5], ps[6]]
                hslot = {0: (0, 0), 1: (0, 1), 2: (0, 2), 3: (1, 0), 4: (1, 1), 5: (1, 2), 6: (2, 0), 7: (2, 1)}
                started = [False, False, False]
                scale = 128.0 ** -0.5
                it_ = 0
                for sbk in range(nsb):
                    for g in range(2):
                        bs = it_ % 2; it_ += 1
                        P.add("pe", [P.mm(ps[bs][:, :], kT_res[:, g, sbk * 128:(sbk + 1) * 128], qTo[:, ot, 4 * g:4 * g + 4, :].rearrange("p h q -> p (h q)"), True, False),
                                     P.mm(ps[bs][:, :], mb[:, sbk * 128:(sbk + 1) * 128], identrep[:].rearrange("p h q -> p (h q)"), False, True)],
                              ["kT_all", ("qTo", ot), "mb", "identrep"], [psk(bs)])
                        P.act(PT[bs], ps[bs][:, :], AF.Exp, [psk(bs)], [("PT", bs)], scale=scale)
                        fns = []
                        for j in range(4):
                            hd = 4 * g + j; bi, sl = hslot[hd]
                            stf = not started[bi]
                            started[bi] = True
                            fns.append(P.mm(oacc[bi][:, sl * 130:sl * 130 + 129], PT[bs][:, j * 128:(j + 1) * 128], v1_res[:, sbk, g, 0:129], stf, sbk == nsb - 1, skip_group_check=True))
                        P.add("pe", fns, [("PT", bs), "v1_all"], [psk(4), psk(5), psk(6)])
                den = sml[:, 16:24]
                for bi, nh in ((0, 3), (1, 3), (2, 2)):
                    ov = oacc[bi][:, 0:nh * 130].rearrange("p (h c) -> p h c", c=130)
                    P.copy("dve", den[:, 3 * bi:3 * bi + nh], ov[:, :, 128], [psk(4 + bi)], [("den", bi)])
                    P.add("dve", lambda e, bi=bi, nh=nh: e.reciprocal(out=den[:, 3 * bi:3 * bi + nh], in_=den[:, 3 * bi:3 * bi + nh]), [("den", bi)], [("den", bi)])
                    o32 = rl[0][:, 0:nh * 128].rearrange("p (h d) -> p h d", d=128)
                    P.tt("dve", o32, ov[:, :, 0:128], den[:, 3 * bi:3 * bi + nh].unsqueeze(2).broadcast_to([128, nh, 128]), ALU.mult, [psk(4 + bi), ("den", bi)], [("rl", 0)])
                    c0 = 3 * bi * 128
                    P.tt("dve", otok[:, 1024 + c0:1024 + c0 + nh * 128], rl[0][:, 0:nh * 128], azs[:, ot, c0:c0 + nh * 128], ALU.mult, [("rl", 0), ("azs", ot)], [("otok", ot)])
                P.copy("pool", otok[:, 0:1024], oa_sel[:, ot, :], [("oa_sel", ot, 0), ("oa_sel", ot, 1)], [("otok", ot)])
                if debug:
                    P.dma("sp", dbg_d[oi * 128:(oi + 1) * 128, :], otok[:], [("otok", ot)], [])
                for q in range(4):
                    b = 2 + (q % 2); pb = psb(b)
                    P.add("pe", [P.tr(pb[:, j * 128:(j + 1) * 128], otok[:, (4 * q + j) * 128:(4 * q + j + 1) * 128], ident_b[:]) for j in range(4)], [("otok", ot), "ident_b"], [psk(b)])
                    for j in range(4):
                        P.copy("dve" if j % 2 == 0 else "act", mixT[:, 4 * q + j, ot * 128:(ot + 1) * 128], pb[:, j * 128:(j + 1) * 128], [psk(b)], [("mixT", ot, 4 * q + j)])
                P.barrier()

            if stop <= 4:
                break
            for ot in range(2):
                oi = 2 * m + ot
                P.dma("sp", yres[:, ot, :], xo_d[oi * 128:(oi + 1) * 128, :], [], [("y", ot)])
            P.dma("sp", xt[:], fing_d.partition_broadcast(128), [], ["fing"])
            oblocks = oblocks_all

            def body5(wi, info):
                blk = info[1]
                for ot in range(2):
                    b = nbank()
                    P.add("pe", [P.mm(ps[b][:, 0:256], mixT[:, kc, ot * 128:(ot + 1) * 128], wbuf[wi][:, kc, 0:256], kc == 0, kc == KC - 1) for kc in range(KC)],
                          [("mixT", ot, kc) for kc in range(KC)] + [("wb", wi)], [psk(b)])
                    ysl = yres[:, ot, blk * 256:(blk + 1) * 256]
                    P.tt("dve", ysl, ysl, ps[b][:, 0:256], ALU.add, [psk(b), ("y", ot)], [("y", ot)])
            stream(oblocks, body5, pre=pre_out)
            for ot in range(2):
                oi = 2 * m + ot
                yk = ("y", ot)
                P.act(xs[:], yres[:, ot, :], AF.Square, [yk], ["xs", "fs"], accum_out=sml[:, 24:25])
                P.ts("dve", sml[:, 25:26], sml[:, 24:25], 1.0 / D, EPS, ALU.mult, ["fs"], ["fs1"], op1=ALU.add)
                P.act(sml[:, 26:27], sml[:, 25:26], AF.Sqrt, ["fs1"], ["fs2"])
                P.add("dve", lambda e: e.reciprocal(out=sml[:, 27:28], in_=sml[:, 26:27]), ["fs2"], ["fs3"])
                P.stt(yres[:, ot, :], yres[:, ot, :], sml[:, 27:28], xt[:], ALU.mult, ALU.mult, [yk, "fs3", "fing"], [yk])
                P.dma("sp", out_d[oi * 128:(oi + 1) * 128, :], yres[:, ot, :], [yk], [])
            P.barrier()
            if m + 1 < n_macro:
                pre_in = prefetch(in_blocks())

        P.finish_waits("sp")
        P.emit()
    return nc


def _prep_shared(inputs):
    f32 = np.float32
    w_in = np.asarray(inputs["w_in"], f32)[0]
    o = dict(gq=0, gk=1024, gv=2048, gz=3072, ga=4096, gb=4104, aq=4112, ak=5136, av=5392, az=5648, iq=6672, ik=7696, iw=7760)
    order = [("gq", 1024), ("gk", 1024), ("gv", 1024), ("gz", 1024), ("ak", 256), ("av", 256), ("ga", 8), ("gb", 8), ("ik", 64),
             ("aq", 1024), ("az", 1024), ("iq", 1024), ("iw", 16)]
    cols = np.concatenate([np.arange(o[n], o[n] + s) for n, s in order])
    wall3 = w_in[:, cols].reshape(KC, 128, NCOL).transpose(1, 0, 2)
    bounds = [(b * 256, 256) for b in range(18)] + [(C_SM, 80)] + [(C_AQ + b * 256, 256) for b in range(12)] + [(C_IW, 16)]
    wall = np.ascontiguousarray(np.concatenate([wall3[:, :, c0:c0 + n].reshape(128, KC * n) for c0, n in bounds], axis=1))
    wout3 = np.asarray(inputs["w_out"], f32)[0].reshape(KC, 128, D).transpose(1, 0, 2)
    wout = np.ascontiguousarray(np.concatenate([wout3[:, :, b * 256:(b + 1) * 256].reshape(128, KC * 256) for b in range(8)], axis=1))
    ii = np.arange(128)
    ident = (ii[:, None] == ii[None, :]).astype(f32)
    U = (ii[:, None] <= ii[None, :]).astype(f32)
    Ls = (ii[:, None] > ii[None, :]).astype(f32)
    cst = np.ascontiguousarray(np.concatenate([ident, U, Ls, np.ones((128, 128), f32)], axis=1))
    gn = np.ascontiguousarray(np.asarray(inputs["attn_norm_g"], f32)[0].reshape(KC, 128).T)
    cw = np.ascontiguousarray(np.asarray(inputs["gdn_conv_w"], f32)[0].reshape(4, 24, 128).transpose(2, 1, 0).reshape(128, 96))
    invf = (np.float32(500000.0) ** (-(np.arange(16, dtype=f32) * f32(2.0) / f32(32.0)))).astype(f32)
    return dict(wall=wall, wout=wout, cst=cst, gn=gn, cw=cw, invf=invf,
                fing=np.ascontiguousarray(np.asarray(inputs["final_norm_g"], f32)),
                gdng=np.ascontiguousarray(np.asarray(inputs["gdn_norm_g"], f32)[0]),
                alog=np.ascontiguousarray(np.asarray(inputs["gdn_a_log"], f32)[0]),
                dtb=np.ascontiguousarray(np.asarray(inputs["gdn_dt_bias"], f32)[0]))


def _core_inputs(inputs, shared, b, hh):
    f32 = np.float32
    x = np.asarray(inputs["x"], f32)[b]
    pos = np.asarray(inputs["positions"], np.int32)[b]
    xo = np.ascontiguousarray(x.reshape(NT, 128, D)[hh::2].reshape(T // 2, D))
    pos_t = np.ascontiguousarray(pos.reshape(NT, 128).T)
    poso = np.ascontiguousarray(pos.reshape(NT, 128)[hh::2].T)
    ii = np.arange(128)
    tril = np.where(ii[None, :] <= ii[:, None], 0.0, -3.0e38).astype(f32)
    NEG = np.full((128, 128), -3.0e38, f32); Z = np.zeros((128, 128), f32)
    cmask = np.concatenate([tril, NEG], 1) if hh == 0 else np.concatenate([Z, tril], 1)
    flg = np.tile(np.array([[1.0 - hh, float(hh)]], f32), (128, 1))
    d = dict(shared)
    d.update(x=np.ascontiguousarray(x), xo=xo, pos=pos_t, poso=poso, cmask=np.ascontiguousarray(cmask), flg=np.ascontiguousarray(flg))
    return d


_NC_CACHE = {}


def kernel(**inputs):
    _nco = inputs.pop("_ncores", None)
    debug = bool(inputs.pop("_debug", False))
    n_macro = int(inputs.pop("_n_macro", NM))
    stop = int(inputs.pop("_stop", 99))
    nblk = int(inputs.pop("_nblk", 99))
    key = (debug, n_macro, stop, nblk)
    if key not in _NC_CACHE:
        _NC_CACHE[key] = build(debug=debug, n_macro=n_macro, stop=stop, nblk=nblk)
    nc = _NC_CACHE[key]
    shared = _prep_shared(inputs)
    ncores = int(_nco) if _nco is not None else 8
    in_maps = [_core_inputs(inputs, shared, c // 2, c % 2) for c in range(ncores)]
    if ncores < 8:
        res = run_bass_kernel_spmd(nc, in_maps, core_ids=list(range(ncores)), trace=True)
        print("EXEC_NS", res.exec_time_ns)
        return [res.results[c] for c in range(ncores)]
    res = run_bass_kernel_spmd(nc, in_maps, core_ids=list(range(ncores)))
    out = np.zeros((4, NT, 128, D), np.float32)
    for c in range(8):
        b, hh = c // 2, c % 2
        out[b, hh::2] = np.asarray(res.results[c]["out"], np.float32).reshape(NOWN, 128, D)
    out = out.reshape(4, T, D)
    if debug:
        dbg = [np.asarray(res.results[c]["dbg"]) for c in range(8)]
        return out, dbg
    return out
```

```python
from contextlib import ExitStack
import math
import numpy as np
import concourse.bass as bass
import concourse.mybir as mybir

F32 = mybir.dt.float32
BF16 = mybir.dt.bfloat16
I32 = mybir.dt.int32
ALU = mybir.AluOpType
AF = mybir.ActivationFunctionType
AX = mybir.AxisListType

ENGS = ("pe", "act", "dve", "pool", "sp")
SEM_EPOCH = 4000
DMA_SLOTS = 6
DMA_EPOCH = 1500


class Prog:
    def __init__(self, nc, stack):
        self.nc = nc
        self.stack = stack
        self.streams = {e: [] for e in ENGS}
        self.cur = {e: None for e in ENGS}
        self.waited = {e: {} for e in ENGS}
        self.lw = {}
        self.rd = {}
        self.nsem = 0
        self.slots = {e: [[None, 0, None] for _ in range(DMA_SLOTS)] for e in ENGS}
        self.slot_i = {e: 0 for e in ENGS}
        self.n_inst = 0

    def _newsem(self):
        self.nsem += 1
        return self.stack.enter_context(self.nc.semaphore(f"s{self.nsem}"))

    def add(self, eng, fns, r=(), w=(), dma=False):
        if not isinstance(fns, (list, tuple)):
            fns = [fns]
        r = list(r)
        w = list(w) + [k for k in r if isinstance(k, tuple) and k and k[0] == "ps" and k not in w]
        deps = []
        for k in r:
            t = self.lw.get(k)
            if t is not None:
                deps.append(t)
        for k in w:
            t = self.lw.get(k)
            if t is not None:
                deps.append(t)
            for s, v in self.rd.get(k, {}).items():
                deps.append((s, v))
        if dma:
            i = self.slot_i[eng]
            self.slot_i[eng] = (i + 1) % DMA_SLOTS
            slot = self.slots[eng][i]
            if slot[0] is None or slot[1] >= 16 * DMA_EPOCH:
                if slot[2] is not None:
                    deps.append(slot[2])
                slot[0] = self._newsem()
                slot[1] = 0
            elif slot[2] is not None:
                deps.append(slot[2])
            slot[1] += 16
            tok = (slot[0], slot[1])
            slot[2] = tok
            inc = 16
        else:
            c = self.cur[eng]
            if c is None or c[1] >= SEM_EPOCH:
                c = self.cur[eng] = [self._newsem(), 0]
            c[1] += 1
            tok = (c[0], c[1])
            inc = 1
        waits = []
        wd = self.waited[eng]
        own = self.cur[eng][0] if (self.cur[eng] is not None) else None
        for s, v in deps:
            if eng == "pe" and not dma and s is own:
                continue
            if wd.get(id(s), 0) < v:
                wd[id(s)] = v
                waits.append((s, v))
        ww = {}
        for s, v in waits:
            if id(s) not in ww or ww[id(s)][1] < v:
                ww[id(s)] = (s, v)
        self.streams[eng].append((list(ww.values()), fns, tok[0], inc))
        self.n_inst += len(fns) + len(ww)
        for k in w:
            self.lw[k] = tok
            self.rd[k] = {}
        for k in r:
            if k in w:
                continue
            d = self.rd.setdefault(k, {})
            d[tok[0]] = tok[1]
        return tok

    def barrier(self):
        toks = []
        for e in ENGS:
            if self.cur[e] is not None:
                toks.append((self.cur[e][0], self.cur[e][1]))
            for slot in self.slots[e]:
                if slot[2] is not None:
                    toks.append(slot[2])
        for e in ENGS:
            wd = self.waited[e]
            waits = []
            for s, v in toks:
                if v > 0 and wd.get(id(s), 0) < v:
                    wd[id(s)] = v
                    waits.append((s, v))
            self.streams[e].append((waits, [], None, 0))
            self.n_inst += len(waits)
        self.lw.clear()
        self.rd.clear()

    def finish_waits(self, eng="sp"):
        waits = []
        for e in ENGS:
            for slot in self.slots[e]:
                if slot[2] is not None:
                    waits.append(slot[2])
        self.streams[eng].append((waits, [], None, 0))

    def emit(self):
        nc = self.nc
        streams = self.streams

        def replay(name, eng):
            for waits, fns, sem, inc in streams[name]:
                for s, v in waits:
                    eng.wait_ge(s, v)
                for f in fns[:-1]:
                    f(eng)
                if fns:
                    fns[-1](eng).then_inc(sem, inc)

        with nc.Block() as block:
            @block.tensor
            def _(e):
                replay("pe", e)

            @block.scalar
            def _(e):
                replay("act", e)

            @block.vector
            def _(e):
                replay("dve", e)

            @block.gpsimd
            def _(e):
                replay("pool", e)

            @block.sync
            def _(e):
                replay("sp", e)

    def dma(self, eng, out, in_, r, w, **kw):
        return self.add(eng, lambda e: e.dma_start(out=out, in_=in_, **kw), r, w, dma=True)

    def act(self, out, in_, func, r, w, eng="act", **kw):
        return self.add(eng, lambda e: e.activation(out=out, in_=in_, func=func, **kw), r, w)

    def ts(self, eng, out, in0, s1, s2, op0, r, w, op1=None, **kw):
        if op1 is None:
            return self.add(eng, lambda e: e.tensor_scalar(out=out, in0=in0, scalar1=s1, scalar2=None, op0=op0, **kw), r, w)
        return self.add(eng, lambda e: e.tensor_scalar(out=out, in0=in0, scalar1=s1, scalar2=s2, op0=op0, op1=op1, **kw), r, w)

    def tt(self, eng, out, in0, in1, op, r, w):
        return self.add(eng, lambda e: e.tensor_tensor(out=out, in0=in0, in1=in1, op=op), r, w)

    def stt(self, out, in0, scalar, in1, op0, op1, r, w, **kw):
        return self.add("dve", lambda e: e.scalar_tensor_tensor(out=out, in0=in0, scalar=scalar, in1=in1, op0=op0, op1=op1, **kw), r, w)

    def copy(self, eng, out, in_, r, w):
        if eng == "act":
            return self.add(eng, lambda e: e.copy(out=out, in_=in_), r, w)
        return self.add(eng, lambda e: e.tensor_copy(out=out, in_=in_), r, w)

    def memset(self, eng, ap, val, w):
        return self.add(eng, lambda e: e.memset(ap, val), (), w)

    def mm(self, out, lhsT, rhs, start, stop, **kw):
        return lambda e: e.matmul(out, lhsT, rhs, start=start, stop=stop, **kw)

    def tr(self, out, in_, ident):
        return lambda e: e.transpose(out, in_, ident)
from concourse.bass_utils import run_bass_kernel_spmd


D = 2048; KC = 16; T = 4096; NT = 32; TM = 512; NM = 8; TPM = 4; NOWN = 16
EPS = 1e-6
C_QKV = 0; C_GZ = 3072; C_KV = 4096; C_SM = 4608; C_AQ = 4688; C_AZ = 5712; C_IQ = 6736; C_IW = 7760; NCOL = 7776
NIT = 24
TWO_PI = 2.0 * math.pi


def build(debug=False, n_macro=NM, stop=99, nblk=99, dma_only=False):
    nc = bass.Bass("TRN2", target_bir_lowering=False)
    dt_in = lambda n, s, d=F32: nc.dram_tensor(n, s, d, kind="ExternalInput").ap()
    x_d = dt_in("x", [T, D]); xo_d = dt_in("xo", [T // 2, D])
    wall_d = dt_in("wall", [128, KC * NCOL]); wout_d = dt_in("wout", [128, KC * D])
    cst_d = dt_in("cst", [128, 4 * 128])
    gn_d = dt_in("gn", [128, KC]); fing_d = dt_in("fing", [D]); gdng_d = dt_in("gdng", [128])
    alog_d = dt_in("alog", [8]); dtb_d = dt_in("dtb", [8]); cw_d = dt_in("cw", [128, 24 * 4])
    pos_d = dt_in("pos", [128, NT], I32); poso_d = dt_in("poso", [128, NOWN], I32)
    invf_d = dt_in("invf", [16]); flg_d = dt_in("flg", [128, 2]); cmask_d = dt_in("cmask", [128, 256])
    out_d = nc.dram_tensor("out", [T // 2, D], F32, kind="ExternalOutput").ap()
    if debug:
        dbg_d = nc.dram_tensor("dbg", [T // 2, D], BF16, kind="ExternalOutput").ap()

    with ExitStack() as st:
        P = Prog(nc, st)
        sb = lambda n, s, d: st.enter_context(nc.sbuf_tensor("s_" + n, s, d))
        ps = [st.enter_context(nc.psum_tensor(f"ps{i}", [128, 512], F32)) for i in range(8)]
        psk = lambda i: ("ps", i)
        psb = lambda i: ps[i][:, :].bitcast(BF16)

        ra = sb("ra", [128, 4, 32], F32); rb = sb("rb", [128, 4, 32], F32); rtmp = sb("rtmp", [128, 256], BF16)
        cst = sb("cst", [128, 512], F32)
        ident_f = cst[:, 0:128]; U_f = cst[:, 128:256]; Ls_f = cst[:, 256:384]; ones_f = cst[:, 384:512]
        ident_b = sb("ident_b", [128, 128], BF16); ones_b = sb("ones_b", [128, 128], BF16)
        identrep = sb("identrep", [128, 4, 128], BF16)
        gn = sb("gn", [128, KC], F32); gdng = sb("gdng", [128, 128], F32)
        alog = sb("alog", [128, 8], F32); dtb = sb("dtb", [128, 8], F32); negA = sb("negA", [128, 8], F32)
        cw = sb("cw", [128, 24, 4], F32)
        flg = sb("flg", [128, 2], F32); cmask = sb("cmask", [128, 256], F32)
        invf = sb("invf", [128, 16], F32)
        cc = sb("cc", [128, NT, 32], F32); ns = sb("ns", [128, NT, 32], F32)
        cci = sb("cci", [128, NT, 16], F32); nsi = sb("nsi", [128, NT, 16], F32)
        cco = sb("cco", [128, NOWN, 32], F32); nso = sb("nso", [128, NOWN, 32], F32)
        ccio = sb("ccio", [128, NOWN, 16], F32); nsio = sb("nsio", [128, NOWN, 16], F32)
        kT_res = sb("kT_res", [128, 2, T], BF16)
        v1_res = sb("v1_res", [128, NT, 2, 130], BF16)
        kiT_res = sb("kiT_res", [128, T], BF16)
        halo = sb("halo", [128, 24, 4], F32)
        S_f = sb("S_f", [128, 8, 128], F32); S_b = sb("S_b", [128, 8, 128], BF16)
        xt = sb("xt", [128, D], F32); xs = sb("xs", [128, D], BF16)
        sml = sb("sml", [128, 64], F32)
        scrA = sb("scrA", [128, 8192], BF16)
        scrB = sb("scrB", [128, 4096], BF16)
        scrC = sb("scrC", [128, 4096], BF16)
        wbuf = [sb(f"wbuf{i}", [128, KC, 256], BF16) for i in range(3)]
        qT = sb("qT", [128, 8, TM], BF16); kT = sb("kT", [128, 8, TM], BF16); vT = sb("vT", [128, 8, TM], BF16)
        gzs = sb("gzs", [128, TPM, 1024], BF16)
        gat = sb("gat", [128, TPM, 8], F32); bet = sb("bet", [128, TPM, 8], F32)
        qTo = sb("qTo", [128, 2, 8, 128], BF16); azs = sb("azs", [128, 2, 1024], BF16)
        qiT = sb("qiT", [128, 2, 8, 128], BF16); wq = sb("wq", [128, 2, 16], F32)
        oa_sel = sb("oa_sel", [128, 2, 1024], BF16)
        otok = sb("otok", [128, D], BF16)
        mixT = sb("mixT", [128, KC, 256], BF16)
        gsm = sb("gsm", [128, 64], F32)

        junkb = scrC[:, :]
        hT = scrA[:, :].rearrange("p (k t) -> p k t", k=KC)
        hTo = scrB[:, :].rearrange("p (k t) -> p k t", k=KC)
        sc = scrA[:, :].bitcast(F32)
        mb = scrB[:, :]
        yres = scrA[:, :].bitcast(F32).rearrange("p (o d) -> p o d", o=2)
        scrCf = scrC[:, :].bitcast(F32)
        raw = [scrCf[:, 0:516], scrCf[:, 516:1032]]
        cacc = [xt[:, 0:512], xt[:, 512:1024]]
        rl = cacc
        silt = xt[:, 1024:1536]
        PT = [xs[:, 0:512], xs[:, 512:1024]]
        rtt = xt[:, 1536:2048]; sqt = sb("sqt", [128, 512], BF16)
        silt_b = sb("silt_b", [128, 512], F32); sqt_b = sb("sqt_b", [128, 512], BF16)
        silt2 = [silt, silt_b[:, :]]; sqt2 = [sqt[:, :], sqt_b[:, :]]

        P.dma("sp", cst[:], cst_d, [], ["cst"])
        P.dma("sp", gn[:], gn_d, [], ["gn"])
        P.dma("sp", gdng[:], gdng_d.partition_broadcast(128), [], ["gdng"])
        P.dma("sp", alog[:], alog_d.partition_broadcast(128), [], ["alog"])
        P.dma("sp", dtb[:], dtb_d.partition_broadcast(128), [], ["dtb"])
        P.dma("sp", cw[:], cw_d.rearrange("p (c i) -> p c i", i=4), [], ["cw"])
        P.dma("sp", flg[:], flg_d, [], ["flg"])
        P.dma("sp", cmask[:], cmask_d, [], ["cmask"])
        P.dma("sp", invf[:], invf_d.partition_broadcast(128), [], ["invf"])
        P.copy("dve", ident_b[:], ident_f, ["cst"], ["ident_b"])
        P.copy("dve", ones_b[:], ones_f, ["cst"], ["ones_b"])
        P.copy("dve", identrep[:], ident_f.unsqueeze(1).broadcast_to([128, 4, 128]), ["cst"], ["identrep"])
        P.act(negA[:], alog[:], AF.Exp, ["alog"], ["negA"])
        P.ts("dve", negA[:], negA[:], -1.0, None, ALU.mult, ["negA"], ["negA"])
        Ub = sb("Ub", [128, 128], F32); Lp = sb("Lp", [128, 128], F32)
        P.ts("dve", Ub[:], U_f, 30000.0, -30000.0, ALU.mult, ["cst"], ["Ub"], op1=ALU.add)
        P.ts("dve", Lp[:], Ls_f, -30000.0, 30000.0, ALU.mult, ["cst"], ["Lp"], op1=ALU.add)
        pw2 = sb("pw2", [128, 32], F32); Dtab = sb("Dtab", [128, 32], F32)
        for k_ in range(NIT):
            P.memset("pool", pw2[:, k_:k_ + 1], 2.0 ** (-(k_ + 1)), [("pw2", k_)])
        P.memset("pool", halo[:], 0.0, ["halo"])
        P.memset("pool", S_f[:], 0.0, ["S_f"])
        P.memset("pool", S_b[:], 0.0, ["S_b"])
        P.memset("pool", v1_res[:], 1.0, ["v1_res"])

        A32s = scrA[:, :].bitcast(F32)

        def make_tables(pos_dram, ntl, cc_, ns_, cci_, nsi_, tag):
            pi_ = sb("pi" + tag, [128, ntl], I32); pf = sb("pf" + tag, [128, ntl], F32)
            ne = ntl * 32
            AR = A32s[:, 0:ne].rearrange("p (t a f) -> p t a f", a=2, f=16)
            KI = A32s[:, 1024:1024 + ne].bitcast(I32).rearrange("p (t a f) -> p t a f", a=2, f=16)
            KF = A32s[:, 2048:2048 + ne].rearrange("p (t a f) -> p t a f", a=2, f=16)
            tag = ""
            P.dma("sp", pi_[:], pos_dram, [], ["pi" + tag])
            P.copy("dve", pf[:], pi_[:], ["pi" + tag], ["pf" + tag])
            P.tt("dve", AR[:, :, 0, :], pf[:].unsqueeze(2).broadcast_to([128, ntl, 16]),
                 invf[:].unsqueeze(1).broadcast_to([128, ntl, 16]), ALU.mult, ["pf" + tag, "invf"], ["AR" + tag])
            P.ts("dve", AR[:, :, 1, :], AR[:, :, 0, :], math.pi / 2, None, ALU.add, ["AR" + tag], ["AR" + tag])
            P.ts("dve", KF[:], AR[:], 1.0 / TWO_PI, None, ALU.mult, ["AR" + tag], ["KF" + tag])
            P.copy("dve", KI[:], KF[:], ["KF" + tag], ["KI" + tag])
            P.copy("dve", KF[:], KI[:], ["KI" + tag], ["KF" + tag])
            P.stt(AR[:], KF[:], -TWO_PI, AR[:], ALU.mult, ALU.add, ["KF" + tag, "AR" + tag], ["AR" + tag])
            P.ts("dve", AR[:], AR[:], 3.14159, -3.14159, ALU.min, ["AR" + tag], ["AR" + tag], op1=ALU.max)
            P.act(AR[:], AR[:], AF.Sin, ["AR" + tag], ["AR" + tag])
            k = "AR" + tag
            P.copy("dve", cc_[:, :, 0:16], AR[:, :, 1, :], [k], [("cc", tag)])
            P.copy("dve", cc_[:, :, 16:32], AR[:, :, 1, :], [k], [("cc", tag)])
            P.ts("dve", ns_[:, :, 0:16], AR[:, :, 0, :], -1.0, None, ALU.mult, [k], [("ns", tag)])
            P.copy("dve", ns_[:, :, 16:32], AR[:, :, 0, :], [k], [("ns", tag)])
            P.copy("dve", cci_[:, :, 0:8], AR[:, :, 1, 0:16:2], [k], [("cci", tag)])
            P.copy("dve", cci_[:, :, 8:16], AR[:, :, 1, 0:16:2], [k], [("cci", tag)])
            P.ts("dve", nsi_[:, :, 0:8], AR[:, :, 0, 0:16:2], -1.0, None, ALU.mult, [k], [("nsi", tag)])
            P.copy("dve", nsi_[:, :, 8:16], AR[:, :, 0, 0:16:2], [k], [("nsi", tag)])

        make_tables(pos_d, NT, cc, ns, cci, nsi, "a")
        P.barrier()
        make_tables(poso_d, NOWN, cco, nso, ccio, nsio, "o")
        P.barrier()

        wstate = {"i": 0}

        def wload(src, c0, n):
            i = wstate["i"] % 3
            wstate["i"] += 1
            blk_ap = src[:, KC * c0:KC * (c0 + n)].rearrange("p (k c) -> p k c", k=KC)
            for h in range(2):
                P.dma("pool", wbuf[i][:, h * 8:(h + 1) * 8, 0:n], blk_ap[:, h * 8:(h + 1) * 8, :], [], [("wb", i)])
            return i

        def prefetch(blocks, k=3):
            return [wload(*blocks[j][:3]) for j in range(min(k, len(blocks)))]

        def stream(blocks, body, ahead=2, pre=None):
            nb = len(blocks)
            if pre:
                loaded = list(pre)
                for j in range(nb):
                    body(loaded[j], blocks[j][3])
                    if len(loaded) < nb:
                        loaded.append(wload(*blocks[len(loaded)][:3]))
                return
            loaded = []
            for j in range(min(ahead, nb)):
                loaded.append(wload(*blocks[j][:3]))
            for j in range(nb):
                if j + ahead < nb:
                    loaded.append(wload(*blocks[j + ahead][:3]))
                body(loaded[j], blocks[j][3])

        bank_rr = {"i": 0}

        def nbank(lo=0, hi=8):
            b = lo + bank_rr["i"] % (hi - lo)
            bank_rr["i"] += 1
            return b

        xt_alt = scrC[:, :].bitcast(F32)
        nt_state = {"i": 0}

        def norm_tile(src_rows, hview, hkey, col0):
            i_ = nt_state["i"] % 2
            nt_state["i"] += 1
            xin = xt[:] if i_ == 0 else xt_alt
            xk = ("xin", i_)
            o_ = 4 * i_
            sq_, s1_, s2_, rs_ = sml[:, o_:o_ + 1], sml[:, o_ + 1:o_ + 2], sml[:, o_ + 2:o_ + 3], sml[:, o_ + 3:o_ + 4]
            P.dma("sp", xin, src_rows, [], [xk])
            P.act(otok[:], xin, AF.Square, [xk], ["sqjunk", ("ssq", i_)], accum_out=sq_)
            P.ts("dve", s1_, sq_, 1.0 / D, EPS, ALU.mult, [("ssq", i_)], [("ssq1", i_)], op1=ALU.add)
            P.act(s2_, s1_, AF.Sqrt, [("ssq1", i_)], [("ssq2", i_)])
            P.add("dve", lambda e: e.reciprocal(out=rs_, in_=s2_), [("ssq2", i_)], [("rstd", i_)])
            P.ts("dve", xs[:], xin, rs_, None, ALU.mult, [xk, ("rstd", i_)], ["xs"])
            for q in range(4):
                b = nbank()
                pb = psb(b)
                P.add("pe", [P.tr(pb[:, j * 128:(j + 1) * 128], xs[:, (4 * q + j) * 128:(4 * q + j + 1) * 128], ident_b[:]) for j in range(4)],
                      ["xs", "ident_b"], [psk(b)])
                P.tt("dve", hview[:, 4 * q:4 * q + 4, col0:col0 + 128], pb[:, 0:512].rearrange("p (j t) -> p j t", j=4),
                     gn[:, 4 * q:4 * q + 4].unsqueeze(2).broadcast_to([128, 4, 128]), ALU.mult, [psk(b), "gn"], [hkey])

        DBG = 99

        def rope(dst, src, H, half, cct, nst, rk, wk, dst2=None, src2=None):
            h2 = 2 * half
            W = dst2.shape[1]
            dh = W // H
            rf = xs[:, :].bitcast(F32)[:, 0:W]
            if DBG >= 1:
                P.copy("act", rf, src2, rk, ["rf"])
            if DBG >= 2:
                P.copy("act", dst2, rf, ["rf"], [wk])
            fa = []
            for h in range(H):
                o = h * dh
                fa.append(lambda e, h=h, o=o: e.tensor_tensor(out=ra[:, h, 0:h2], in0=rf[:, o:o + h2], in1=cct, op=ALU.mult))
                fa.append(lambda e, h=h, o=o: e.tensor_tensor(out=rb[:, h, 0:half], in0=rf[:, o + half:o + h2], in1=nst[:, 0:half], op=ALU.mult))
                fa.append(lambda e, h=h, o=o: e.tensor_tensor(out=rb[:, h, half:h2], in0=rf[:, o:o + half], in1=nst[:, half:h2], op=ALU.mult))
            if DBG >= 3:
                P.add("dve", fa, ["rf", "tabs"], ["ra", "rb"])
            fb = [(lambda e, h=h, o=h * dh: e.tensor_tensor(out=dst2[:, o:o + h2], in0=ra[:, h, 0:h2], in1=rb[:, h, 0:h2], op=ALU.add)) for h in range(H)]
            if DBG >= 4:
                P.add("dve", fb, ["ra", "rb", wk], [wk])

        def in_blocks():
            bl = []
            for blk in range(12):
                bl.append((wall_d, C_QKV + blk * 256, 256, ("qkv", blk)))
            for blk in range(4):
                bl.append((wall_d, C_GZ + blk * 256, 256, ("gz", blk)))
            bl.append((wall_d, C_KV, 256, ("ak", 0)))
            bl.append((wall_d, C_KV + 256, 256, ("av", 0)))
            bl.append((wall_d, C_SM, 80, ("sm", 0)))
            for blk in range(4):
                bl.append((wall_d, C_AQ + blk * 256, 256, ("aq", blk)))
            for blk in range(4):
                bl.append((wall_d, C_AZ + blk * 256, 256, ("az", blk)))
            for blk in range(4):
                bl.append((wall_d, C_IQ + blk * 256, 256, ("iq", blk)))
            bl.append((wall_d, C_IW, 16, ("iw", 0)))
            return bl

        oblocks_all = [(wout_d, blk * 256, 256, ("o", blk)) for blk in range(8)]
        pre_in = prefetch(in_blocks()) if (stop >= 2 and nblk >= 3 and not dma_only) else None

        for m in range(n_macro if stop > 0 else 0):
            for tt_ in range(TPM):
                r0 = m * TM + tt_ * 128
                norm_tile(x_d[r0:r0 + 128, :], hT, "hT", tt_ * 128)
            for ot in range(2):
                r0 = (2 * m + ot) * 128
                norm_tile(xo_d[r0:r0 + 128, :], hTo, "hTo", ot * 128)
            P.barrier()
            if stop <= 1:
                break

            blocks = []
            for blk in range(12):
                blocks.append((wall_d, C_QKV + blk * 256, 256, ("qkv", blk)))
            for blk in range(4):
                blocks.append((wall_d, C_GZ + blk * 256, 256, ("gz", blk)))
            blocks.append((wall_d, C_KV, 256, ("ak", 0)))
            blocks.append((wall_d, C_KV + 256, 256, ("av", 0)))
            blocks.append((wall_d, C_SM, 80, ("sm", 0)))
            for blk in range(4):
                blocks.append((wall_d, C_AQ + blk * 256, 256, ("aq", blk)))
            for blk in range(4):
                blocks.append((wall_d, C_AZ + blk * 256, 256, ("az", blk)))
            for blk in range(4):
                blocks.append((wall_d, C_IQ + blk * 256, 256, ("iq", blk)))
            blocks.append((wall_d, C_IW, 16, ("iw", 0)))

            def tok_mm(b, hv, hk, col0, wi, n):
                P.add("pe", [P.mm(ps[b][:, 0:n], hv[:, kc, col0:col0 + 128], wbuf[wi][:, kc, 0:n], kc == 0, kc == KC - 1) for kc in range(KC)],
                      [hk, ("wb", wi)], [psk(b)])

            pending = []

            def flush_pending():
                for f_ in pending:
                    f_()
                del pending[:]

            def body2(wi, info):
                kind, blk = info
                if dma_only:
                    return
                if kind != "qkv":
                    flush_pending()
                if kind == "qkv":
                    for c2 in range(2):
                        ct = blk * 2 + c2
                        b = nbank()
                        P.add("pe", [P.mm(ps[b][:, :], wbuf[wi][:, kc, c2 * 128:(c2 + 1) * 128], hT[:, kc, :], kc == 0, kc == KC - 1) for kc in range(KC)],
                              ["hT", ("wb", wi)], [psk(b)])
                        flush_pending()
                        rw = raw[ct % 2]; rk = ("raw", ct % 2); ac = cacc[ct % 2]; ak_ = ("cacc", ct % 2)
                        P.copy("act", rw[:, 3:515], ps[b][:, :], [psk(b)], [rk])
                        P.copy("pool", rw[:, 0:3], halo[:, ct, 0:3], [("halo", ct)], [rk])
                        P.copy("pool", halo[:, ct, 0:3], rw[:, 512:515], [rk], [("halo", ct)])
                        P.ts("dve", ac, rw[:, 3:515], cw[:, ct, 3:4], None, ALU.mult, [rk, "cw"], [ak_])
                        for i in (2, 1, 0):
                            P.stt(ac, rw[:, i:i + 512], cw[:, ct, i:i + 1], ac, ALU.mult, ALU.add, [rk, "cw", ak_], [ak_])
                        if ct < 16:
                            isq = ct < 8; hd = ct % 8
                            sl_ = silt2[ct % 2]; sk_ = ("silt", ct % 2); sq_ = sqt2[ct % 2]; qk_ = ("sqt", ct % 2)
                            P.act(sl_, ac, AF.Silu, [ak_], [sk_])
                            P.act(sq_, sl_, AF.Square, [sk_], [qk_])

                            def tail(isq=isq, hd=hd, sl_=sl_, sk_=sk_, sq_=sq_, qk_=qk_):
                                b2 = nbank()
                                P.add("pe", [P.mm(ps[b2][:, :], ones_b[:], sq_, True, True)], [qk_, "ones_b"], [psk(b2)])
                                P.act(rtt, ps[b2][:, :], AF.Sqrt, [psk(b2)], ["rtt"], scale=(128.0 if isq else 1.0), bias=(128.0 * EPS if isq else EPS))
                                P.add("dve", lambda e: e.reciprocal(out=rtt, in_=rtt), ["rtt"], ["rtt"])
                                dstT = qT if isq else kT
                                P.tt("dve", dstT[:, hd, :], sl_, rtt, ALU.mult, [sk_, "rtt"], [("qT" if isq else "kT", hd)])
                            pending.append(tail)
                        else:
                            P.act(vT[:, ct - 16, :], ac, AF.Silu, [ak_], [("vT", ct - 16)])
                elif kind == "gz":
                    for tt_ in range(TPM):
                        b = nbank()
                        tok_mm(b, hT, "hT", tt_ * 128, wi, 256)
                        P.act(gzs[:, tt_, blk * 256:(blk + 1) * 256], ps[b][:, 0:256], AF.Silu, [psk(b)], [("gzs", tt_)])
                elif kind == "ak":
                    for tt_ in range(TPM):
                        b = nbank(); gt = m * TPM + tt_
                        tok_mm(b, hT, "hT", tt_ * 128, wi, 256)
                        LV = 9 if DBG >= 6 else (3 if DBG >= 5 else 2)
                        kv_ = rtmp[:, :].rearrange("p (h d) -> p h d", h=2)
                        if LV >= 2:
                            rope(kv_, ps[b][:, 0:256].rearrange("p (h d) -> p h d", h=2), 2, 16, cc[:, gt, :], ns[:, gt, :], [psk(b)], "rtmp", rtmp[:, 0:256], ps[b][:, 0:256])
                        elif LV >= 1:
                            P.copy("act", rtmp[:, 0:256], ps[b][:, 0:256], [psk(b)], ["rtmp"])
                        b2 = nbank(); pb = psb(b2)
                        if LV >= 3:
                            P.add("pe", [P.tr(pb[:, g * 128:(g + 1) * 128], rtmp[:, g * 128:(g + 1) * 128], ident_b[:]) for g in range(2)], ["rtmp", "ident_b"], [psk(b2)])
                        if LV >= 4:
                            for g in range(2):
                                if DBG == 7 and g == 1:
                                    continue
                                if DBG == 8 and g == 0:
                                    continue
                                eng_ = "dve" if g == 0 else "act"
                                if DBG == 9:
                                    eng_ = "dve"
                                P.copy(eng_, kT_res[:, g, gt * 128:(gt + 1) * 128], pb[:, g * 128:(g + 1) * 128], [psk(b2)], [("kT_res", gt, g)])
                elif kind == "av":
                    for tt_ in range(TPM):
                        b = nbank(); gt = m * TPM + tt_
                        tok_mm(b, hT, "hT", tt_ * 128, wi, 256)
                        for g in range(2):
                            P.copy("dve" if g == 0 else "act", v1_res[:, gt, g, 0:128], ps[b][:, g * 128:(g + 1) * 128], [psk(b)], [("v1_res", gt, g)])
                elif kind == "sm":
                    for tt_ in range(TPM):
                        b = nbank(); gt = m * TPM + tt_
                        tok_mm(b, hT, "hT", tt_ * 128, wi, 80)
                        pk = [psk(b)]
                        P.tt("dve", gsm[:, 0:8], ps[b][:, 0:8], dtb[:], ALU.add, pk + ["dtb"], ["g0"])
                        P.act(gsm[:, 8:16], gsm[:, 0:8], AF.Abs, ["g0"], ["g1"])
                        P.act(gsm[:, 16:24], gsm[:, 8:16], AF.Exp, ["g1"], ["g2"], scale=-1.0)
                        P.act(gsm[:, 24:32], gsm[:, 16:24], AF.Ln, ["g2"], ["g3"], bias=1.0)
                        P.stt(gsm[:, 32:40], gsm[:, 0:8], 0.0, gsm[:, 24:32], ALU.max, ALU.add, ["g0", "g3"], ["g4"])
                        P.tt("dve", gat[:, tt_, :], gsm[:, 32:40], negA[:], ALU.mult, ["g4", "negA"], [("gat", tt_)])
                        P.act(bet[:, tt_, :], ps[b][:, 8:16], AF.Sigmoid, pk, [("bet", tt_)])
                        ikv = rtmp[:, 0:64].rearrange("p (h d) -> p h d", h=1)
                        rope(ikv, ps[b][:, 16:80].rearrange("p (h d) -> p h d", h=1), 1, 8, cci[:, gt, :], nsi[:, gt, :], pk, "rtmp", rtmp[:, 0:64], ps[b][:, 16:80])
                        P.copy("pool", rtmp[:, 64:128], rtmp[:, 0:64], ["rtmp"], ["rtmp"])
                        b2 = nbank(); pb = psb(b2)
                        P.add("pe", [P.tr(pb[:, 0:128], rtmp[:, 0:128], ident_b[:])], ["rtmp", "ident_b"], [psk(b2)])
                        P.copy("dve", kiT_res[:, gt * 128:(gt + 1) * 128], pb[:, 0:128], [psk(b2)], [("kiT_res", gt)])
                elif kind == "aq":
                    for ot in range(2):
                        b = nbank(); oi = 2 * m + ot
                        tok_mm(b, hTo, "hTo", ot * 128, wi, 256)
                        qv = rtmp[:, :].rearrange("p (h d) -> p h d", h=2)
                        rope(qv, ps[b][:, 0:256].rearrange("p (h d) -> p h d", h=2), 2, 16, cco[:, oi, :], nso[:, oi, :], [psk(b)], "rtmp", rtmp[:, 0:256], ps[b][:, 0:256])
                        b2 = nbank(); pb = psb(b2)
                        P.add("pe", [P.tr(pb[:, g * 128:(g + 1) * 128], rtmp[:, g * 128:(g + 1) * 128], ident_b[:]) for g in range(2)], ["rtmp", "ident_b"], [psk(b2)])
                        P.copy("dve", qTo[:, ot, 2 * blk:2 * blk + 2, :].rearrange("p g t -> p (g t)"), pb[:, 0:256], [psk(b2)], [("qTo", ot)])
                elif kind == "az":
                    for ot in range(2):
                        b = nbank()
                        tok_mm(b, hTo, "hTo", ot * 128, wi, 256)
                        P.act(azs[:, ot, blk * 256:(blk + 1) * 256], ps[b][:, 0:256], AF.Silu, [psk(b)], [("azs", ot)])
                elif kind == "iq":
                    for ot in range(2):
                        b = nbank(); oi = 2 * m + ot
                        tok_mm(b, hTo, "hTo", ot * 128, wi, 256)
                        qv = rtmp[:, :].rearrange("p (h d) -> p h d", h=4)
                        rope(qv, ps[b][:, 0:256].rearrange("p (h d) -> p h d", h=4), 4, 8, ccio[:, oi, :], nsio[:, oi, :], [psk(b)], "rtmp", rtmp[:, 0:256], ps[b][:, 0:256])
                        b2 = nbank(); pb = psb(b2)
                        P.add("pe", [P.tr(pb[:, g * 128:(g + 1) * 128], rtmp[:, g * 128:(g + 1) * 128], ident_b[:]) for g in range(2)], ["rtmp", "ident_b"], [psk(b2)])
                        P.copy("dve", qiT[:, ot, 2 * blk:2 * blk + 2, :].rearrange("p g t -> p (g t)"), pb[:, 0:256], [psk(b2)], [("qiT", ot)])
                elif kind == "iw":
                    for ot in range(2):
                        b = nbank()
                        tok_mm(b, hTo, "hTo", ot * 128, wi, 16)
                        P.ts("dve", wq[:, ot, :], ps[b][:, 0:16], 1.0 / 32.0, None, ALU.mult, [psk(b)], [("wq", ot)])

            stream(blocks[:nblk], body2, pre=pre_in)
            pre_in = None
            P.barrier()
            if stop <= 2:
                break

            A32 = scrA[:, :].bitcast(F32)
            B32 = scrB[:, :].bitcast(F32)

            def tmpl(hg):
                o = hg * 2048
                t = {}
                t["Z"] = A32[:, o:o + 512]; t["E1"] = A32[:, o + 512:o + 1024]; t["E2"] = A32[:, o + 1024:o + 1536]
                t["gU"] = A32[:, o + 1536:o + 2048]
                ob = hg * 2048
                for i, nme in enumerate(["Mc", "Nc", "Pc", "AT"]):
                    t[nme] = scrB[:, ob + i * 512: ob + (i + 1) * 512]
                return t
            gtmp = [scrC[:, :].rearrange("p (a b) -> p a b", a=8), mixT[:, :, :].rearrange("p k t -> p (k t)").rearrange("p (a b) -> p a b", a=8)]

            def gdn_chunk_prep(tt_):
                b = nbank()
                P.add("pe", [P.mm(ps[b][:, 0:8], U_f, gat[:, tt_, :], True, True), P.mm(ps[b][:, 8:16], ones_f, gat[:, tt_, :], True, True)],
                      [("gat", tt_), "cst"], [psk(b)])
                P.copy("dve", gsm[:, 0:16], ps[b][:, 0:16], [psk(b)], ["gc"])
                P.act(gsm[:, 16:32], gsm[:, 0:16], AF.Exp, ["gc"], ["eg"])
                P.tt("dve", gsm[:, 32:40], gsm[:, 8:16], gsm[:, 0:8], ALU.subtract, ["gc"], ["ekl"])
                P.act(gsm[:, 32:40], gsm[:, 32:40], AF.Exp, ["ekl"], ["ek"])
                P.tt("dve", gsm[:, 40:48], bet[:, tt_, :], gsm[:, 16:24], ALU.mult, [("bet", tt_), "eg"], ["bk"])
                P.ts("dve", gsm[:, 48:56], bet[:, tt_, :], -1.0, None, ALU.mult, [("bet", tt_)], ["nbet"])

            def bc(ap8, hg):
                return ap8[:, 4 * hg:4 * hg + 4].unsqueeze(2).broadcast_to([128, 4, 128])

            def v3(ap):
                return ap.rearrange("p (h d) -> p h d", h=4)

            def gdn_hg(tt_, hg):
                t = tmpl(hg); K = lambda n: (n, hg)
                G = gtmp[hg]
                ktok = G[:, 0, :]; vtok = G[:, 1, :]; qg = G[:, 2, :]; kbg = G[:, 3, :]; vb = G[:, 4, :]; wTn = G[:, 5, :]; vn = G[:, 6, :]; kt2 = G[:, 7, :]
                tok = slice(tt_ * 128, (tt_ + 1) * 128)
                hs = range(4 * hg, 4 * hg + 4)
                gc = gsm[:, 0:8]; eg = gsm[:, 16:24]; egl = gsm[:, 24:32]; ek = gsm[:, 32:40]; bk = gsm[:, 40:48]; nbet = gsm[:, 48:56]
                base = 4 * hg
                bA, bB, bC, bD = base, base + 1, base + 2, base + 3
                pb = psb(bA)
                P.add("pe", [P.tr(pb[:, j * 128:(j + 1) * 128], kT[:, 4 * hg + j, tok], ident_b[:]) for j in range(4)], [("kT", h) for h in hs] + ["ident_b"], [psk(bA)])
                P.copy("act", ktok, pb[:, 0:512], [psk(bA)], [K("ktok")])
                pb2 = psb(bB)
                P.add("pe", [P.tr(pb2[:, j * 128:(j + 1) * 128], vT[:, 4 * hg + j, tok], ident_b[:]) for j in range(4)], [("vT", h) for h in hs] + ["ident_b"], [psk(bB)])
                P.copy("dve", vtok, pb2[:, 0:512], [psk(bB)], [K("vtok")])
                yield
                for j in range(4):
                    P.act(t["gU"][:, j * 128:(j + 1) * 128], U_f, AF.Copy, ["cst", ("gat", tt_)], [K("gU")], scale=gat[:, tt_, 4 * hg + j:4 * hg + j + 1])
                P.add("pe", [P.mm(ps[bC][:, :], ones_f, t["gU"], True, True)], [K("gU"), "cst"], [psk(bC)])
                P.tt("dve", v3(t["Z"]), v3(ps[bC][:, :]), bc(gc, hg), ALU.subtract, [psk(bC), "gc"], [K("Z")])
                P.act(t["gU"], ps[bC][:, :], AF.Exp, [psk(bC)], [K("gU")])
                yield
                P.stt(v3(t["E1"]), v3(t["Z"]), 0.0, Ub[:].unsqueeze(1).broadcast_to([128, 4, 128]), ALU.min, ALU.add, [K("Z"), "Ub"], [K("E1")])
                P.stt(v3(t["E2"]), v3(t["Z"]), 0.0, Lp[:].unsqueeze(1).broadcast_to([128, 4, 128]), ALU.max, ALU.add, [K("Z"), "Lp"], [K("E2")])
                P.act(t["E1"], t["E1"], AF.Exp, [K("E1")], [K("E1")])
                P.act(t["E2"], t["E2"], AF.Exp, [K("E2")], [K("E2")], scale=-1.0)
                P.tt("dve", v3(qg), qT[:, 4 * hg:4 * hg + 4, tok], v3(t["gU"]), ALU.mult, [("qT", h) for h in hs] + [K("gU")], [K("qg")])
                yield
                P.add("pe", [P.mm(ps[bA][:, j * 128:(j + 1) * 128], kT[:, 4 * hg + j, tok], kT[:, 4 * hg + j, tok], True, True) for j in range(4)], [("kT", h) for h in hs], [psk(bA)])
                P.add("pe", [P.mm(ps[bB][:, j * 128:(j + 1) * 128], kT[:, 4 * hg + j, tok], qT[:, 4 * hg + j, tok], True, True) for j in range(4)], [("kT", h) for h in hs] + [("qT", h) for h in hs], [psk(bB)])
                P.tt("dve", v3(t["Z"]), v3(ps[bA][:, :]), bc(nbet, hg), ALU.mult, [psk(bA), "nbet"], [K("Z")])
                P.tt("dve", t["Mc"], t["Z"], t["E2"], ALU.mult, [K("Z"), K("E2")], [K("Mc")])
                P.tt("dve", t["AT"], ps[bB][:, :], t["E1"], ALU.mult, [psk(bB), K("E1")], [K("AT")])
                yield
                pb = psb(bC)
                P.add("pe", [P.tr(pb[:, j * 128:(j + 1) * 128], t["Mc"][:, j * 128:(j + 1) * 128], ident_b[:]) for j in range(4)], [K("Mc"), "ident_b"], [psk(bC)])
                P.copy("act", t["Nc"], pb[:, 0:512], [psk(bC)], [K("Nc")])
                for j in range(4):
                    P.tt("dve", t["Pc"][:, j * 128:(j + 1) * 128], pb[:, j * 128:(j + 1) * 128], ident_f, ALU.add, [psk(bC), "cst"], [K("Pc")])
                yield
                Mn = G[:, 5, :]; Nn = G[:, 6, :]
                Mc, Nc, Pc = t["Mc"], t["Nc"], t["Pc"]
                kM, kN, kP = K("Mc"), K("Nc"), K("Pc")
                kMn, kNn = K("wTn"), K("vn")
                for lvl in range(1, 7):
                    P.add("pe", [P.mm(ps[bA][:, j * 128:(j + 1) * 128], Nc[:, j * 128:(j + 1) * 128], Mc[:, j * 128:(j + 1) * 128], True, True) for j in range(4)], [kM, kN], [psk(bA)])
                    if lvl < 6:
                        P.add("pe", [P.mm(ps[bB][:, j * 128:(j + 1) * 128], Mc[:, j * 128:(j + 1) * 128], Nc[:, j * 128:(j + 1) * 128], True, True) for j in range(4)], [kM, kN], [psk(bB)])
                    P.copy("act", Mn, ps[bA][:, :], [psk(bA)], [kMn])
                    if lvl < 6:
                        P.copy("dve", Nn, ps[bB][:, :], [psk(bB)], [kNn])
                    yield
                    fns = []
                    for j in range(4):
                        fns.append(P.mm(ps[bC][:, j * 128:(j + 1) * 128], ident_b[:], Pc[:, j * 128:(j + 1) * 128], True, False))
                        fns.append(P.mm(ps[bC][:, j * 128:(j + 1) * 128], Mn[:, j * 128:(j + 1) * 128], Pc[:, j * 128:(j + 1) * 128], False, True))
                    P.add("pe", fns, [kMn, kP, "ident_b"], [psk(bC)])
                    P.copy("pool" if False else "dve", Pc, ps[bC][:, :], [psk(bC)], [kP])
                    Mc, Mn = Mn, Mc; kM, kMn = kMn, kM
                    Nc, Nn = Nn, Nc; kN, kNn = kNn, kN
                    yield
                TT = Pc
                P.tt("dve", v3(kbg), v3(ktok), bc(bk, hg), ALU.mult, [K("ktok"), "bk"], [K("kbg")])
                for j in range(4):
                    hh2 = 4 * hg + j
                    P.act(vb[:, j * 128:(j + 1) * 128], vtok[:, j * 128:(j + 1) * 128], AF.Copy, [K("vtok"), ("bet", tt_)], [K("vb")], scale=bet[:, tt_, hh2:hh2 + 1])
                    P.act(kt2[:, j * 128:(j + 1) * 128], ktok[:, j * 128:(j + 1) * 128], AF.Copy, [K("ktok"), "ek"], [K("kt2")], scale=ek[:, hh2:hh2 + 1])
                yield
                P.add("pe", [P.mm(ps[bA][:, j * 128:(j + 1) * 128], kbg[:, j * 128:(j + 1) * 128], TT[:, j * 128:(j + 1) * 128], True, True) for j in range(4)], [K("kbg"), kP], [psk(bA)])
                P.ts("dve", wTn, ps[bA][:, :], -1.0, None, ALU.mult, [psk(bA)], [K("wTn")])
                yield
                fns = []
                for j in range(4):
                    fns.append(P.mm(ps[bB][:, j * 128:(j + 1) * 128], TT[:, j * 128:(j + 1) * 128], vb[:, j * 128:(j + 1) * 128], True, False))
                    fns.append(P.mm(ps[bB][:, j * 128:(j + 1) * 128], wTn[:, j * 128:(j + 1) * 128], S_b[:, 4 * hg + j, :], False, True))
                P.add("pe", fns, [kP, K("vb"), K("wTn"), ("S_b", hg)], [psk(bB)])
                P.copy("act", vn, ps[bB][:, :], [psk(bB)], [K("vn")])
                yield
                fns = []
                for j in range(4):
                    fns.append(P.mm(ps[bC][:, j * 128:(j + 1) * 128], qg[:, j * 128:(j + 1) * 128], S_b[:, 4 * hg + j, :], True, False))
                    fns.append(P.mm(ps[bC][:, j * 128:(j + 1) * 128], t["AT"][:, j * 128:(j + 1) * 128], vn[:, j * 128:(j + 1) * 128], False, True))
                P.add("pe", fns, [K("qg"), K("AT"), K("vn"), ("S_b", hg)], [psk(bC)])
                P.add("pe", [P.mm(ps[bD][:, j * 128:(j + 1) * 128], kt2[:, j * 128:(j + 1) * 128], vn[:, j * 128:(j + 1) * 128], True, True) for j in range(4)], [K("kt2"), K("vn")], [psk(bD)])
                Sv = S_f[:, 4 * hg:4 * hg + 4, :]
                for j in range(4):
                    hh2 = 4 * hg + j
                    P.stt(S_f[:, hh2, :], S_f[:, hh2, :], egl[:, hh2:hh2 + 1], ps[bD][:, j * 128:(j + 1) * 128], ALU.mult, ALU.add, [("S_f", hg), "eg", psk(bD)], [("S_f", hg)])
                yield
                P.copy("act", S_b[:, 4 * hg:4 * hg + 4, :].rearrange("p h d -> p (h d)"), Sv.rearrange("p h d -> p (h d)"), [("S_f", hg)], [("S_b", hg)])
                o32 = t["Z"]
                P.copy("act", o32, ps[bC][:, :], [psk(bC)], [K("Z")])
                sm = gsm[:, 56:60] if hg == 0 else gsm[:, 60:64]
                for j in range(4):
                    P.act(t["E1"][:, j * 128:(j + 1) * 128], o32[:, j * 128:(j + 1) * 128], AF.Square, [K("Z")], [K("E1"), K("osq")], accum_out=sm[:, j:j + 1])
                yield
                P.ts("dve", sm, sm, 1.0 / 128.0, EPS, ALU.mult, [K("osq")], [K("osq")], op1=ALU.add)
                P.act(sm, sm, AF.Sqrt, [K("osq")], [K("osq")])
                P.add("dve", lambda e: e.reciprocal(out=sm, in_=sm), [K("osq")], [K("osq")])
                P.tt("dve", v3(o32), v3(o32), sm.unsqueeze(2).broadcast_to([128, 4, 128]), ALU.mult, [K("Z"), K("osq")], [K("Z")])
                P.tt("dve", v3(o32), v3(o32), gdng[:].unsqueeze(1).broadcast_to([128, 4, 128]), ALU.mult, [K("Z"), "gdng"], [K("Z")])
                P.tt("dve", o32, o32, gzs[:, tt_, hg * 512:(hg + 1) * 512], ALU.mult, [K("Z"), ("gzs", tt_)], [K("Z")])
                ot = tt_ // 2
                dsl = oa_sel[:, ot, hg * 512:(hg + 1) * 512]
                if tt_ % 2 == 0:
                    P.ts("dve", dsl, o32, flg[:, 0:1], None, ALU.mult, [K("Z"), "flg"], [("oa_sel", ot, hg)])
                else:
                    P.stt(dsl, o32, flg[:, 1:2], dsl, ALU.mult, ALU.add, [K("Z"), "flg", ("oa_sel", ot, hg)], [("oa_sel", ot, hg)])
                yield

            for tt_ in range(TPM):
                gdn_chunk_prep(tt_)
                gens = [gdn_hg(tt_, 0), gdn_hg(tt_, 1)]
                alive = [True, True]
                while any(alive):
                    for gi in range(2):
                        if alive[gi]:
                            try:
                                next(gens[gi])
                            except StopIteration:
                                alive[gi] = False
            P.barrier()
            if stop <= 3:
                break

            pre_out = prefetch(oblocks_all) if stop > 4 else None
            for ot in range(2):
                oi = 2 * m + ot
                S_i = 256 * (oi + 1)
                nkb = (S_i + 511) // 512
                for kb in range(nkb):
                    k0 = kb * 512; n = min(512, S_i - k0)
                    for pr in range(8):
                        b0 = 2 * (pr % 2); b1 = b0 + 1
                        P.add("pe", [P.mm(ps[b0][:, 0:n], qiT[0:64, ot, pr, :], kiT_res[0:64, k0:k0 + n], True, True)], [("qiT", ot), "kiT_all"], [psk(b0)])
                        P.add("pe", [P.mm(ps[b1][:, 0:n], qiT[64:128, ot, pr, :], kiT_res[64:128, k0:k0 + n], True, True)], [("qiT", ot), "kiT_all"], [psk(b1)])
                        for hh_, bb in ((2 * pr, b0), (2 * pr + 1, b1)):
                            r_ = rl[hh_ % 2]; rk = ("rl", hh_ % 2)
                            P.act(r_[:, 0:n], ps[bb][:, 0:n], AF.Relu, [psk(bb)], [rk])
                            if hh_ == 0:
                                P.ts("dve", sc[:, k0:k0 + n], r_[:, 0:n], wq[:, ot, 0:1], None, ALU.mult, [rk, ("wq", ot)], [("sc", kb)])
                            else:
                                P.stt(sc[:, k0:k0 + n], r_[:, 0:n], wq[:, ot, hh_:hh_ + 1], sc[:, k0:k0 + n], ALU.mult, ALU.add, [rk, ("wq", ot), ("sc", kb)], [("sc", kb)])
                sck = [("sc", kb) for kb in range(nkb)]
                am = sml[:, 8:9]; lo = sml[:, 9:10]; hi = sml[:, 10:11]; d0 = sml[:, 11:12]; mid = sml[:, 12:13]; cnt = sml[:, 13:14]; gg = sml[:, 14:15]
                P.add("dve", lambda e, S_i=S_i: e.tensor_reduce(out=am, in_=sc[:, 0:S_i], axis=AX.X, op=ALU.max, apply_absolute_value=True), sck, ["am"])
                P.ts("dve", lo, am, -1.0, -1.0, ALU.mult, ["am"], ["lo"], op1=ALU.add)
                P.tt("dve", sc[:, S_i - 256:S_i], sc[:, S_i - 256:S_i], cmask[:], ALU.add, sck + ["cmask"], sck)
                P.add("dve", lambda e, S_i=S_i: e.tensor_reduce(out=hi, in_=sc[:, 0:S_i], axis=AX.X, op=ALU.max), sck, ["hi"])
                P.tt("dve", d0, hi, lo, ALU.subtract, ["hi", "lo"], ["d0"])
                P.tt("dve", Dtab[:, 0:NIT], d0.broadcast_to([128, NIT]), pw2[:, 0:NIT], ALU.mult, ["d0", "pw2all"], ["Dtab"])
                P.stt(mid, d0, 0.5, lo, ALU.mult, ALU.add, ["d0", "lo"], ["mid"])
                for it in range(NIT):
                    P.ts("dve", junkb[:, 0:S_i], sc[:, 0:S_i], mid, None, ALU.is_gt, sck + ["mid"], ["junkb", "cnt"], op1=ALU.add, accum_out=cnt)
                    P.ts("dve", gg, cnt, 255.5, 0.5, ALU.is_ge, ["cnt"], ["gg"], op1=ALU.subtract)
                    if it < NIT - 1:
                        P.stt(mid, gg, Dtab[:, it:it + 1], mid, ALU.mult, ALU.add, ["gg", "Dtab", "mid"], ["mid"])
                P.ts("dve", gg, gg, 0.5, None, ALU.subtract, ["gg"], ["gg"])
                P.stt(lo, gg, Dtab[:, NIT - 1:NIT], mid, ALU.mult, ALU.add, ["gg", "Dtab", "mid"], ["lo"])
                P.ts("dve", mb[:, 0:S_i], sc[:, 0:S_i], lo, -30000.0, ALU.is_le, sck + ["lo"], ["mb"], op1=ALU.mult)
                nsb = S_i // 128
                oacc = [ps[4], ps[5], ps[6]]
                hslot = {0: (0, 0), 1: (0, 1), 2: (0, 2), 3: (1, 0), 4: (1, 1), 5: (1, 2), 6: (2, 0), 7: (2, 1)}
                started = [False, False, False]
                scale = 128.0 ** -0.5
                it_ = 0
                for sbk in range(nsb):
                    for g in range(2):
                        bs = it_ % 2; it_ += 1
                        P.add("pe", [P.mm(ps[bs][:, :], kT_res[:, g, sbk * 128:(sbk + 1) * 128], qTo[:, ot, 4 * g:4 * g + 4, :].rearrange("p h q -> p (h q)"), True, False),
                                     P.mm(ps[bs][:, :], mb[:, sbk * 128:(sbk + 1) * 128], identrep[:].rearrange("p h q -> p (h q)"), False, True)],
                              ["kT_all", ("qTo", ot), "mb", "identrep"], [psk(bs)])
                        P.act(PT[bs], ps[bs][:, :], AF.Exp, [psk(bs)], [("PT", bs)], scale=scale)
                        fns = []
                        for j in range(4):
                            hd = 4 * g + j; bi, sl = hslot[hd]
                            stf = not started[bi]
                            started[bi] = True
                            fns.append(P.mm(oacc[bi][:, sl * 130:sl * 130 + 129], PT[bs][:, j * 128:(j + 1) * 128], v1_res[:, sbk, g, 0:129], stf, sbk == nsb - 1, skip_group_check=True))
                        P.add("pe", fns, [("PT", bs), "v1_all"], [psk(4), psk(5), psk(6)])
                den = sml[:, 16:24]
                for bi, nh in ((0, 3), (1, 3), (2, 2)):
                    ov = oacc[bi][:, 0:nh * 130].rearrange("p (h c) -> p h c", c=130)
                    P.copy("dve", den[:, 3 * bi:3 * bi + nh], ov[:, :, 128], [psk(4 + bi)], [("den", bi)])
                    P.add("dve", lambda e, bi=bi, nh=nh: e.reciprocal(out=den[:, 3 * bi:3 * bi + nh], in_=den[:, 3 * bi:3 * bi + nh]), [("den", bi)], [("den", bi)])
                    o32 = rl[0][:, 0:nh * 128].rearrange("p (h d) -> p h d", d=128)
                    P.tt("dve", o32, ov[:, :, 0:128], den[:, 3 * bi:3 * bi + nh].unsqueeze(2).broadcast_to([128, nh, 128]), ALU.mult, [psk(4 + bi), ("den", bi)], [("rl", 0)])
                    c0 = 3 * bi * 128
                    P.tt("dve", otok[:, 1024 + c0:1024 + c0 + nh * 128], rl[0][:, 0:nh * 128], azs[:, ot, c0:c0 + nh * 128], ALU.mult, [("rl", 0), ("azs", ot)], [("otok", ot)])
                P.copy("pool", otok[:, 0:1024], oa_sel[:, ot, :], [("oa_sel", ot, 0), ("oa_sel", ot, 1)], [("otok", ot)])
                if debug:
                    P.dma("sp", dbg_d[oi * 128:(oi + 1) * 128, :], otok[:], [("otok", ot)], [])
                for q in range(4):
                    b = 2 + (q % 2); pb = psb(b)
                    P.add("pe", [P.tr(pb[:, j * 128:(j + 1) * 128], otok[:, (4 * q + j) * 128:(4 * q + j + 1) * 128], ident_b[:]) for j in range(4)], [("otok", ot), "ident_b"], [psk(b)])
                    for j in range(4):
                        P.copy("dve" if j % 2 == 0 else "act", mixT[:, 4 * q + j, ot * 128:(ot + 1) * 128], pb[:, j * 128:(j + 1) * 128], [psk(b)], [("mixT", ot, 4 * q + j)])
                P.barrier()

            if stop <= 4:
                break
            for ot in range(2):
                oi = 2 * m + ot
                P.dma("sp", yres[:, ot, :], xo_d[oi * 128:(oi + 1) * 128, :], [], [("y", ot)])
            P.dma("sp", xt[:], fing_d.partition_broadcast(128), [], ["fing"])
            oblocks = oblocks_all

            def body5(wi, info):
                blk = info[1]
                for ot in range(2):
                    b = nbank()
                    P.add("pe", [P.mm(ps[b][:, 0:256], mixT[:, kc, ot * 128:(ot + 1) * 128], wbuf[wi][:, kc, 0:256], kc == 0, kc == KC - 1) for kc in range(KC)],
                          [("mixT", ot, kc) for kc in range(KC)] + [("wb", wi)], [psk(b)])
                    ysl = yres[:, ot, blk * 256:(blk + 1) * 256]
                    P.tt("dve", ysl, ysl, ps[b][:, 0:256], ALU.add, [psk(b), ("y", ot)], [("y", ot)])
            stream(oblocks, body5, pre=pre_out)
            for ot in range(2):
                oi = 2 * m + ot
                yk = ("y", ot)
                P.act(xs[:], yres[:, ot, :], AF.Square, [yk], ["xs", "fs"], accum_out=sml[:, 24:25])
                P.ts("dve", sml[:, 25:26], sml[:, 24:25], 1.0 / D, EPS, ALU.mult, ["fs"], ["fs1"], op1=ALU.add)
                P.act(sml[:, 26:27], sml[:, 25:26], AF.Sqrt, ["fs1"], ["fs2"])
                P.add("dve", lambda e: e.reciprocal(out=sml[:, 27:28], in_=sml[:, 26:27]), ["fs2"], ["fs3"])
                P.stt(yres[:, ot, :], yres[:, ot, :], sml[:, 27:28], xt[:], ALU.mult, ALU.mult, [yk, "fs3", "fing"], [yk])
                P.dma("sp", out_d[oi * 128:(oi + 1) * 128, :], yres[:, ot, :], [yk], [])
            P.barrier()
            if m + 1 < n_macro:
                pre_in = prefetch(in_blocks())

        P.finish_waits("sp")
        P.emit()
    return nc


def _prep_shared(inputs):
    f32 = np.float32
    w_in = np.asarray(inputs["w_in"], f32)[0]
    o = dict(gq=0, gk=1024, gv=2048, gz=3072, ga=4096, gb=4104, aq=4112, ak=5136, av=5392, az=5648, iq=6672, ik=7696, iw=7760)
    order = [("gq", 1024), ("gk", 1024), ("gv", 1024), ("gz", 1024), ("ak", 256), ("av", 256), ("ga", 8), ("gb", 8), ("ik", 64),
             ("aq", 1024), ("az", 1024), ("iq", 1024), ("iw", 16)]
    cols = np.concatenate([np.arange(o[n], o[n] + s) for n, s in order])
    wall3 = w_in[:, cols].reshape(KC, 128, NCOL).transpose(1, 0, 2)
    bounds = [(b * 256, 256) for b in range(18)] + [(C_SM, 80)] + [(C_AQ + b * 256, 256) for b in range(12)] + [(C_IW, 16)]
    wall = np.ascontiguousarray(np.concatenate([wall3[:, :, c0:c0 + n].reshape(128, KC * n) for c0, n in bounds], axis=1))
    wout3 = np.asarray(inputs["w_out"], f32)[0].reshape(KC, 128, D).transpose(1, 0, 2)
    wout = np.ascontiguousarray(np.concatenate([wout3[:, :, b * 256:(b + 1) * 256].reshape(128, KC * 256) for b in range(8)], axis=1))
    ii = np.arange(128)
    ident = (ii[:, None] == ii[None, :]).astype(f32)
    U = (ii[:, None] <= ii[None, :]).astype(f32)
    Ls = (ii[:, None] > ii[None, :]).astype(f32)
    cst = np.ascontiguousarray(np.concatenate([ident, U, Ls, np.ones((128, 128), f32)], axis=1))
    gn = np.ascontiguousarray(np.asarray(inputs["attn_norm_g"], f32)[0].reshape(KC, 128).T)
    cw = np.ascontiguousarray(np.asarray(inputs["gdn_conv_w"], f32)[0].reshape(4, 24, 128).transpose(2, 1, 0).reshape(128, 96))
    invf = (np.float32(500000.0) ** (-(np.arange(16, dtype=f32) * f32(2.0) / f32(32.0)))).astype(f32)
    return dict(wall=wall, wout=wout, cst=cst, gn=gn, cw=cw, invf=invf,
                fing=np.ascontiguousarray(np.asarray(inputs["final_norm_g"], f32)),
                gdng=np.ascontiguousarray(np.asarray(inputs["gdn_norm_g"], f32)[0]),
                alog=np.ascontiguousarray(np.asarray(inputs["gdn_a_log"], f32)[0]),
                dtb=np.ascontiguousarray(np.asarray(inputs["gdn_dt_bias"], f32)[0]))


def _core_inputs(inputs, shared, b, hh):
    f32 = np.float32
    x = np.asarray(inputs["x"], f32)[b]
    pos = np.asarray(inputs["positions"], np.int32)[b]
    xo = np.ascontiguousarray(x.reshape(NT, 128, D)[hh::2].reshape(T // 2, D))
    pos_t = np.ascontiguousarray(pos.reshape(NT, 128).T)
    poso = np.ascontiguousarray(pos.reshape(NT, 128)[hh::2].T)
    ii = np.arange(128)
    tril = np.where(ii[None, :] <= ii[:, None], 0.0, -3.0e38).astype(f32)
    NEG = np.full((128, 128), -3.0e38, f32); Z = np.zeros((128, 128), f32)
    cmask = np.concatenate([tril, NEG], 1) if hh == 0 else np.concatenate([Z, tril], 1)
    flg = np.tile(np.array([[1.0 - hh, float(hh)]], f32), (128, 1))
    d = dict(shared)
    d.update(x=np.ascontiguousarray(x), xo=xo, pos=pos_t, poso=poso, cmask=np.ascontiguousarray(cmask), flg=np.ascontiguousarray(flg))
    return d


_NC_CACHE = {}


def kernel(**inputs):
    _nco = inputs.pop("_ncores", None)
    debug = bool(inputs.pop("_debug", False))
    n_macro = int(inputs.pop("_n_macro", NM))
    stop = int(inputs.pop("_stop", 99))
    nblk = int(inputs.pop("_nblk", 99))
    key = (debug, n_macro, stop, nblk)
    if key not in _NC_CACHE:
        _NC_CACHE[key] = build(debug=debug, n_macro=n_macro, stop=stop, nblk=nblk)
    nc = _NC_CACHE[key]
    shared = _prep_shared(inputs)
    ncores = int(_nco) if _nco is not None else 8
    in_maps = [_core_inputs(inputs, shared, c // 2, c % 2) for c in range(ncores)]
    if ncores < 8:
        res = run_bass_kernel_spmd(nc, in_maps, core_ids=list(range(ncores)), trace=True)
        print("EXEC_NS", res.exec_time_ns)
        return [res.results[c] for c in range(ncores)]
    res = run_bass_kernel_spmd(nc, in_maps, core_ids=list(range(ncores)))
    out = np.zeros((4, NT, 128, D), np.float32)
    for c in range(8):
        b, hh = c // 2, c % 2
        out[b, hh::2] = np.asarray(res.results[c]["out"], np.float32).reshape(NOWN, 128, D)
    out = out.reshape(4, T, D)
    if debug:
        dbg = [np.asarray(res.results[c]["dbg"]) for c in range(8)]
        return out, dbg
    return out
```

```python
from contextlib import ExitStack
import math
import numpy as np
import concourse.bass as bass
import concourse.mybir as mybir

F32 = mybir.dt.float32
BF16 = mybir.dt.bfloat16
I32 = mybir.dt.int32
ALU = mybir.AluOpType
AF = mybir.ActivationFunctionType
AX = mybir.AxisListType

ENGS = ("pe", "act", "dve", "pool", "sp")
SEM_EPOCH = 4000
DMA_SLOTS = 6
DMA_EPOCH = 1500


class Prog:
    def __init__(self, nc, stack):
        self.nc = nc
        self.stack = stack
        self.streams = {e: [] for e in ENGS}
        self.cur = {e: None for e in ENGS}
        self.waited = {e: {} for e in ENGS}
        self.lw = {}
        self.rd = {}
        self.nsem = 0
        self.slots = {e: [[None, 0, None] for _ in range(DMA_SLOTS)] for e in ENGS}
        self.slot_i = {e: 0 for e in ENGS}
        self.n_inst = 0

    def _newsem(self):
        self.nsem += 1
        return self.stack.enter_context(self.nc.semaphore(f"s{self.nsem}"))

    def add(self, eng, fns, r=(), w=(), dma=False):
        if not isinstance(fns, (list, tuple)):
            fns = [fns]
        r = list(r)
        w = list(w) + [k for k in r if isinstance(k, tuple) and k and k[0] == "ps" and k not in w]
        deps = []
        for k in r:
            t = self.lw.get(k)
            if t is not None:
                deps.append(t)
        for k in w:
            t = self.lw.get(k)
            if t is not None:
                deps.append(t)
            for s, v in self.rd.get(k, {}).items():
                deps.append((s, v))
        if dma:
            i = self.slot_i[eng]
            self.slot_i[eng] = (i + 1) % DMA_SLOTS
            slot = self.slots[eng][i]
            if slot[0] is None or slot[1] >= 16 * DMA_EPOCH:
                if slot[2] is not None:
                    deps.append(slot[2])
                slot[0] = self._newsem()
                slot[1] = 0
            elif slot[2] is not None:
                deps.append(slot[2])
            slot[1] += 16
            tok = (slot[0], slot[1])
            slot[2] = tok
            inc = 16
        else:
            c = self.cur[eng]
            if c is None or c[1] >= SEM_EPOCH:
                c = self.cur[eng] = [self._newsem(), 0]
            c[1] += 1
            tok = (c[0], c[1])
            inc = 1
        waits = []
        wd = self.waited[eng]
        own = self.cur[eng][0] if (self.cur[eng] is not None) else None
        for s, v in deps:
            if eng == "pe" and not dma and s is own:
                continue
            if wd.get(id(s), 0) < v:
                wd[id(s)] = v
                waits.append((s, v))
        ww = {}
        for s, v in waits:
            if id(s) not in ww or ww[id(s)][1] < v:
                ww[id(s)] = (s, v)
        self.streams[eng].append((list(ww.values()), fns, tok[0], inc))
        self.n_inst += len(fns) + len(ww)
        for k in w:
            self.lw[k] = tok
            self.rd[k] = {}
        for k in r:
            if k in w:
                continue
            d = self.rd.setdefault(k, {})
            d[tok[0]] = tok[1]
        return tok

    def barrier(self):
        toks = []
        for e in ENGS:
            if self.cur[e] is not None:
                toks.append((self.cur[e][0], self.cur[e][1]))
            for slot in self.slots[e]:
                if slot[2] is not None:
                    toks.append(slot[2])
        for e in ENGS:
            wd = self.waited[e]
            waits = []
            for s, v in toks:
                if v > 0 and wd.get(id(s), 0) < v:
                    wd[id(s)] = v
                    waits.append((s, v))
            self.streams[e].append((waits, [], None, 0))
            self.n_inst += len(waits)
        self.lw.clear()
        self.rd.clear()

    def finish_waits(self, eng="sp"):
        waits = []
        for e in ENGS:
            for slot in self.slots[e]:
                if slot[2] is not None:
                    waits.append(slot[2])
        self.streams[eng].append((waits, [], None, 0))

    def emit(self):
        nc = self.nc
        streams = self.streams

        def replay(name, eng):
            for waits, fns, sem, inc in streams[name]:
                for s, v in waits:
                    eng.wait_ge(s, v)
                for f in fns[:-1]:
                    f(eng)
                if fns:
                    fns[-1](eng).then_inc(sem, inc)

        with nc.Block() as block:
            @block.tensor
            def _(e):
                replay("pe", e)

            @block.scalar
            def _(e):
                replay("act", e)

            @block.vector
            def _(e):
                replay("dve", e)

            @block.gpsimd
            def _(e):
                replay("pool", e)

            @block.sync
            def _(e):
                replay("sp", e)

    def dma(self, eng, out, in_, r, w, **kw):
        return self.add(eng, lambda e: e.dma_start(out=out, in_=in_, **kw), r, w, dma=True)

    def act(self, out, in_, func, r, w, eng="act", **kw):
        return self.add(eng, lambda e: e.activation(out=out, in_=in_, func=func, **kw), r, w)

    def ts(self, eng, out, in0, s1, s2, op0, r, w, op1=None, **kw):
        if op1 is None:
            return self.add(eng, lambda e: e.tensor_scalar(out=out, in0=in0, scalar1=s1, scalar2=None, op0=op0, **kw), r, w)
        return self.add(eng, lambda e: e.tensor_scalar(out=out, in0=in0, scalar1=s1, scalar2=s2, op0=op0, op1=op1, **kw), r, w)

    def tt(self, eng, out, in0, in1, op, r, w):
        return self.add(eng, lambda e: e.tensor_tensor(out=out, in0=in0, in1=in1, op=op), r, w)

    def stt(self, out, in0, scalar, in1, op0, op1, r, w, **kw):
        return self.add("dve", lambda e: e.scalar_tensor_tensor(out=out, in0=in0, scalar=scalar, in1=in1, op0=op0, op1=op1, **kw), r, w)

    def copy(self, eng, out, in_, r, w):
        if eng == "act":
            return self.add(eng, lambda e: e.copy(out=out, in_=in_), r, w)
        return self.add(eng, lambda e: e.tensor_copy(out=out, in_=in_), r, w)

    def memset(self, eng, ap, val, w):
        return self.add(eng, lambda e: e.memset(ap, val), (), w)

    def mm(self, out, lhsT, rhs, start, stop, **kw):
        return lambda e: e.matmul(out, lhsT, rhs, start=start, stop=stop, **kw)

    def tr(self, out, in_, ident):
        return lambda e: e.transpose(out, in_, ident)
from concourse.bass_utils import run_bass_kernel_spmd


D = 2048; KC = 16; T = 4096; NT = 32; TM = 512; NM = 8; TPM = 4; NOWN = 16
EPS = 1e-6
C_QKV = 0; C_GZ = 3072; C_KV = 4096; C_SM = 4608; C_AQ = 4688; C_AZ = 5712; C_IQ = 6736; C_IW = 7760; NCOL = 7776
NIT = 24
TWO_PI = 2.0 * math.pi


def build(debug=False, n_macro=NM, stop=99, nblk=99, dma_only=False):
    nc = bass.Bass("TRN2", target_bir_lowering=False)
    dt_in = lambda n, s, d=F32: nc.dram_tensor(n, s, d, kind="ExternalInput").ap()
    x_d = dt_in("x", [T, D]); xo_d = dt_in("xo", [T // 2, D])
    wall_d = dt_in("wall", [128, KC * NCOL]); wout_d = dt_in("wout", [128, KC * D])
    cst_d = dt_in("cst", [128, 4 * 128])
    gn_d = dt_in("gn", [128, KC]); fing_d = dt_in("fing", [D]); gdng_d = dt_in("gdng", [128])
    alog_d = dt_in("alog", [8]); dtb_d = dt_in("dtb", [8]); cw_d = dt_in("cw", [128, 24 * 4])
    pos_d = dt_in("pos", [128, NT], I32); poso_d = dt_in("poso", [128, NOWN], I32)
    invf_d = dt_in("invf", [16]); flg_d = dt_in("flg", [128, 2]); cmask_d = dt_in("cmask", [128, 256])
    out_d = nc.dram_tensor("out", [T // 2, D], F32, kind="ExternalOutput").ap()
    if debug:
        dbg_d = nc.dram_tensor("dbg", [T // 2, D], BF16, kind="ExternalOutput").ap()

    with ExitStack() as st:
        P = Prog(nc, st)
        sb = lambda n, s, d: st.enter_context(nc.sbuf_tensor("s_" + n, s, d))
        ps = [st.enter_context(nc.psum_tensor(f"ps{i}", [128, 512], F32)) for i in range(8)]
        psk = lambda i: ("ps", i)
        psb = lambda i: ps[i][:, :].bitcast(BF16)

        ra = sb("ra", [128, 4, 32], F32); rb = sb("rb", [128, 4, 32], F32); rtmp = sb("rtmp", [128, 256], BF16)
        cst = sb("cst", [128, 512], F32)
        ident_f = cst[:, 0:128]; U_f = cst[:, 128:256]; Ls_f = cst[:, 256:384]; ones_f = cst[:, 384:512]
        ident_b = sb("ident_b", [128, 128], BF16); ones_b = sb("ones_b", [128, 128], BF16)
        identrep = sb("identrep", [128, 4, 128], BF16)
        gn = sb("gn", [128, KC], F32); gdng = sb("gdng", [128, 128], F32)
        alog = sb("alog", [128, 8], F32); dtb = sb("dtb", [128, 8], F32); negA = sb("negA", [128, 8], F32)
        cw = sb("cw", [128, 24, 4], F32)
        flg = sb("flg", [128, 2], F32); cmask = sb("cmask", [128, 256], F32)
        invf = sb("invf", [128, 16], F32)
        cc = sb("cc", [128, NT, 32], F32); ns = sb("ns", [128, NT, 32], F32)
        cci = sb("cci", [128, NT, 16], F32); nsi = sb("nsi", [128, NT, 16], F32)
        cco = sb("cco", [128, NOWN, 32], F32); nso = sb("nso", [128, NOWN, 32], F32)
        ccio = sb("ccio", [128, NOWN, 16], F32); nsio = sb("nsio", [128, NOWN, 16], F32)
        kT_res = sb("kT_res", [128, 2, T], BF16)
        v1_res = sb("v1_res", [128, NT, 2, 130], BF16)
        kiT_res = sb("kiT_res", [128, T], BF16)
        halo = sb("halo", [128, 24, 4], F32)
        S_f = sb("S_f", [128, 8, 128], F32); S_b = sb("S_b", [128, 8, 128], BF16)
        xt = sb("xt", [128, D], F32); xs = sb("xs", [128, D], BF16)
        sml = sb("sml", [128, 64], F32)
        scrA = sb("scrA", [128, 8192], BF16)
        scrB = sb("scrB", [128, 4096], BF16)
        scrC = sb("scrC", [128, 4096], BF16)
        wbuf = [sb(f"wbuf{i}", [128, KC, 256], BF16) for i in range(3)]
        qT = sb("qT", [128, 8, TM], BF16); kT = sb("kT", [128, 8, TM], BF16); vT = sb("vT", [128, 8, TM], BF16)
        gzs = sb("gzs", [128, TPM, 1024], BF16)
        gat = sb("gat", [128, TPM, 8], F32); bet = sb("bet", [128, TPM, 8], F32)
        qTo = sb("qTo", [128, 2, 8, 128], BF16); azs = sb("azs", [128, 2, 1024], BF16)
        qiT = sb("qiT", [128, 2, 8, 128], BF16); wq = sb("wq", [128, 2, 16], F32)
        oa_sel = sb("oa_sel", [128, 2, 1024], BF16)
        otok = sb("otok", [128, D], BF16)
        mixT = sb("mixT", [128, KC, 256], BF16)
        gsm = sb("gsm", [128, 64], F32)

        junkb = scrC[:, :]
        hT = scrA[:, :].rearrange("p (k t) -> p k t", k=KC)
        hTo = scrB[:, :].rearrange("p (k t) -> p k t", k=KC)
        sc = scrA[:, :].bitcast(F32)
        mb = scrB[:, :]
        yres = scrA[:, :].bitcast(F32).rearrange("p (o d) -> p o d", o=2)
        scrCf = scrC[:, :].bitcast(F32)
        raw = [scrCf[:, 0:516], scrCf[:, 516:1032]]
        cacc = [xt[:, 0:512], xt[:, 512:1024]]
        rl = cacc
        silt = xt[:, 1024:1536]
        PT = [xs[:, 0:512], xs[:, 512:1024]]
        rtt = xt[:, 1536:2048]; sqt = sb("sqt", [128, 512], BF16)
        silt_b = sb("silt_b", [128, 512], F32); sqt_b = sb("sqt_b", [128, 512], BF16)
        sqt_c = sb("sqt_c", [128, 512], BF16)
        otok_f = otok[:, :].bitcast(F32)
        silt2 = [silt, silt_b[:, :], otok_f[:, 0:512]]; sqt2 = [sqt[:, :], sqt_b[:, :], sqt_c[:, :]]
        rtt2 = [rtt, otok_f[:, 512:1024]]
        tail_state = {"i": 0}

        P.dma("sp", cst[:], cst_d, [], ["cst"])
        P.dma("sp", gn[:], gn_d, [], ["gn"])
        P.dma("sp", gdng[:], gdng_d.partition_broadcast(128), [], ["gdng"])
        P.dma("sp", alog[:], alog_d.partition_broadcast(128), [], ["alog"])
        P.dma("sp", dtb[:], dtb_d.partition_broadcast(128), [], ["dtb"])
        P.dma("sp", cw[:], cw_d.rearrange("p (c i) -> p c i", i=4), [], ["cw"])
        P.dma("sp", flg[:], flg_d, [], ["flg"])
        P.dma("sp", cmask[:], cmask_d, [], ["cmask"])
        P.dma("sp", invf[:], invf_d.partition_broadcast(128), [], ["invf"])
        P.copy("dve", ident_b[:], ident_f, ["cst"], ["ident_b"])
        P.copy("dve", ones_b[:], ones_f, ["cst"], ["ones_b"])
        P.copy("dve", identrep[:], ident_f.unsqueeze(1).broadcast_to([128, 4, 128]), ["cst"], ["identrep"])
        P.act(negA[:], alog[:], AF.Exp, ["alog"], ["negA"])
        P.ts("dve", negA[:], negA[:], -1.0, None, ALU.mult, ["negA"], ["negA"])
        Ub = sb("Ub", [128, 128], F32); Lp = sb("Lp", [128, 128], F32)
        P.ts("dve", Ub[:], U_f, 30000.0, -30000.0, ALU.mult, ["cst"], ["Ub"], op1=ALU.add)
        P.ts("dve", Lp[:], Ls_f, -30000.0, 30000.0, ALU.mult, ["cst"], ["Lp"], op1=ALU.add)
        pw2 = sb("pw2", [128, 32], F32); Dtab = sb("Dtab", [128, 32], F32)
        for k_ in range(NIT):
            P.memset("pool", pw2[:, k_:k_ + 1], 2.0 ** (-(k_ + 1)), [("pw2", k_)])
        P.memset("pool", halo[:], 0.0, ["halo"])
        P.memset("pool", S_f[:], 0.0, ["S_f"])
        P.memset("pool", S_b[:], 0.0, ["S_b"])
        P.memset("pool", v1_res[:], 1.0, ["v1_res"])

        A32s = scrA[:, :].bitcast(F32)

        def make_tables(pos_dram, ntl, cc_, ns_, cci_, nsi_, tag):
            pi_ = sb("pi" + tag, [128, ntl], I32); pf = sb("pf" + tag, [128, ntl], F32)
            ne = ntl * 32
            AR = A32s[:, 0:ne].rearrange("p (t a f) -> p t a f", a=2, f=16)
            KI = A32s[:, 1024:1024 + ne].bitcast(I32).rearrange("p (t a f) -> p t a f", a=2, f=16)
            KF = A32s[:, 2048:2048 + ne].rearrange("p (t a f) -> p t a f", a=2, f=16)
            tag = ""
            P.dma("sp", pi_[:], pos_dram, [], ["pi" + tag])
            P.copy("dve", pf[:], pi_[:], ["pi" + tag], ["pf" + tag])
            P.tt("dve", AR[:, :, 0, :], pf[:].unsqueeze(2).broadcast_to([128, ntl, 16]),
                 invf[:].unsqueeze(1).broadcast_to([128, ntl, 16]), ALU.mult, ["pf" + tag, "invf"], ["AR" + tag])
            P.ts("dve", AR[:, :, 1, :], AR[:, :, 0, :], math.pi / 2, None, ALU.add, ["AR" + tag], ["AR" + tag])
            P.ts("dve", KF[:], AR[:], 1.0 / TWO_PI, None, ALU.mult, ["AR" + tag], ["KF" + tag])
            P.copy("dve", KI[:], KF[:], ["KF" + tag], ["KI" + tag])
            P.copy("dve", KF[:], KI[:], ["KI" + tag], ["KF" + tag])
            P.stt(AR[:], KF[:], -TWO_PI, AR[:], ALU.mult, ALU.add, ["KF" + tag, "AR" + tag], ["AR" + tag])
            P.ts("dve", AR[:], AR[:], 3.14159, -3.14159, ALU.min, ["AR" + tag], ["AR" + tag], op1=ALU.max)
            P.act(AR[:], AR[:], AF.Sin, ["AR" + tag], ["AR" + tag])
            k = "AR" + tag
            P.copy("dve", cc_[:, :, 0:16], AR[:, :, 1, :], [k], [("cc", tag)])
            P.copy("dve", cc_[:, :, 16:32], AR[:, :, 1, :], [k], [("cc", tag)])
            P.ts("dve", ns_[:, :, 0:16], AR[:, :, 0, :], -1.0, None, ALU.mult, [k], [("ns", tag)])
            P.copy("dve", ns_[:, :, 16:32], AR[:, :, 0, :], [k], [("ns", tag)])
            P.copy("dve", cci_[:, :, 0:8], AR[:, :, 1, 0:16:2], [k], [("cci", tag)])
            P.copy("dve", cci_[:, :, 8:16], AR[:, :, 1, 0:16:2], [k], [("cci", tag)])
            P.ts("dve", nsi_[:, :, 0:8], AR[:, :, 0, 0:16:2], -1.0, None, ALU.mult, [k], [("nsi", tag)])
            P.copy("dve", nsi_[:, :, 8:16], AR[:, :, 0, 0:16:2], [k], [("nsi", tag)])

        make_tables(pos_d, NT, cc, ns, cci, nsi, "a")
        P.barrier()
        make_tables(poso_d, NOWN, cco, nso, ccio, nsio, "o")
        P.barrier()

        wstate = {"i": 0}

        def wload(src, c0, n):
            i = wstate["i"] % 3
            wstate["i"] += 1
            blk_ap = src[:, KC * c0:KC * (c0 + n)].rearrange("p (k c) -> p k c", k=KC)
            for h in range(2):
                P.dma("pool", wbuf[i][:, h * 8:(h + 1) * 8, 0:n], blk_ap[:, h * 8:(h + 1) * 8, :], [], [("wb", i)])
            return i

        def prefetch(blocks, k=3):
            return [wload(*blocks[j][:3]) for j in range(min(k, len(blocks)))]

        def stream(blocks, body, ahead=2, pre=None):
            nb = len(blocks)
            if pre:
                loaded = list(pre)
                for j in range(nb):
                    body(loaded[j], blocks[j][3])
                    if len(loaded) < nb:
                        loaded.append(wload(*blocks[len(loaded)][:3]))
                return
            loaded = []
            for j in range(min(ahead, nb)):
                loaded.append(wload(*blocks[j][:3]))
            for j in range(nb):
                if j + ahead < nb:
                    loaded.append(wload(*blocks[j + ahead][:3]))
                body(loaded[j], blocks[j][3])

        bank_rr = {"i": 0}

        def nbank(lo=0, hi=8):
            b = lo + bank_rr["i"] % (hi - lo)
            bank_rr["i"] += 1
            return b

        xt_alt = scrC[:, :].bitcast(F32)
        nt_state = {"i": 0}

        def norm_tile(src_rows, hview, hkey, col0):
            i_ = nt_state["i"] % 2
            nt_state["i"] += 1
            xin = xt[:] if i_ == 0 else xt_alt
            xk = ("xin", i_)
            o_ = 4 * i_
            sq_, s1_, s2_, rs_ = sml[:, o_:o_ + 1], sml[:, o_ + 1:o_ + 2], sml[:, o_ + 2:o_ + 3], sml[:, o_ + 3:o_ + 4]
            P.dma("sp", xin, src_rows, [], [xk])
            P.act(otok[:], xin, AF.Square, [xk], ["sqjunk", ("ssq", i_)], accum_out=sq_)
            P.ts("dve", s1_, sq_, 1.0 / D, EPS, ALU.mult, [("ssq", i_)], [("ssq1", i_)], op1=ALU.add)
            P.act(s2_, s1_, AF.Sqrt, [("ssq1", i_)], [("ssq2", i_)])
            P.add("dve", lambda e: e.reciprocal(out=rs_, in_=s2_), [("ssq2", i_)], [("rstd", i_)])
            P.ts("dve", xs[:], xin, rs_, None, ALU.mult, [xk, ("rstd", i_)], ["xs"])
            for q in range(4):
                b = nbank()
                pb = psb(b)
                P.add("pe", [P.tr(pb[:, j * 128:(j + 1) * 128], xs[:, (4 * q + j) * 128:(4 * q + j + 1) * 128], ident_b[:]) for j in range(4)],
                      ["xs", "ident_b"], [psk(b)])
                P.tt("dve", hview[:, 4 * q:4 * q + 4, col0:col0 + 128], pb[:, 0:512].rearrange("p (j t) -> p j t", j=4),
                     gn[:, 4 * q:4 * q + 4].unsqueeze(2).broadcast_to([128, 4, 128]), ALU.mult, [psk(b), "gn"], [hkey])

        DBG = 99

        def rope(dst, src, H, half, cct, nst, rk, wk, dst2=None, src2=None):
            h2 = 2 * half
            W = dst2.shape[1]
            dh = W // H
            rf = xs[:, :].bitcast(F32)[:, 0:W]
            if DBG >= 1:
                P.copy("act", rf, src2, rk, ["rf"])
            if DBG >= 2:
                P.copy("act", dst2, rf, ["rf"], [wk])
            fa = []
            for h in range(H):
                o = h * dh
                fa.append(lambda e, h=h, o=o: e.tensor_tensor(out=ra[:, h, 0:h2], in0=rf[:, o:o + h2], in1=cct, op=ALU.mult))
                fa.append(lambda e, h=h, o=o: e.tensor_tensor(out=rb[:, h, 0:half], in0=rf[:, o + half:o + h2], in1=nst[:, 0:half], op=ALU.mult))
                fa.append(lambda e, h=h, o=o: e.tensor_tensor(out=rb[:, h, half:h2], in0=rf[:, o:o + half], in1=nst[:, half:h2], op=ALU.mult))
            if DBG >= 3:
                P.add("dve", fa, ["rf", "tabs"], ["ra", "rb"])
            fb = [(lambda e, h=h, o=h * dh: e.tensor_tensor(out=dst2[:, o:o + h2], in0=ra[:, h, 0:h2], in1=rb[:, h, 0:h2], op=ALU.add)) for h in range(H)]
            if DBG >= 4:
                P.add("dve", fb, ["ra", "rb", wk], [wk])

        def in_blocks():
            bl = []
            for blk in range(12):
                bl.append((wall_d, C_QKV + blk * 256, 256, ("qkv", blk)))
            for blk in range(4):
                bl.append((wall_d, C_GZ + blk * 256, 256, ("gz", blk)))
            bl.append((wall_d, C_KV, 256, ("ak", 0)))
            bl.append((wall_d, C_KV + 256, 256, ("av", 0)))
            bl.append((wall_d, C_SM, 80, ("sm", 0)))
            for blk in range(4):
                bl.append((wall_d, C_AQ + blk * 256, 256, ("aq", blk)))
            for blk in range(4):
                bl.append((wall_d, C_AZ + blk * 256, 256, ("az", blk)))
            for blk in range(4):
                bl.append((wall_d, C_IQ + blk * 256, 256, ("iq", blk)))
            bl.append((wall_d, C_IW, 16, ("iw", 0)))
            return bl

        oblocks_all = [(wout_d, blk * 256, 256, ("o", blk)) for blk in range(8)]
        pre_in = prefetch(in_blocks()) if (stop >= 2 and nblk >= 3 and not dma_only) else None

        for m in range(n_macro if stop > 0 else 0):
            for tt_ in range(TPM):
                r0 = m * TM + tt_ * 128
                norm_tile(x_d[r0:r0 + 128, :], hT, "hT", tt_ * 128)
            for ot in range(2):
                r0 = (2 * m + ot) * 128
                norm_tile(xo_d[r0:r0 + 128, :], hTo, "hTo", ot * 128)
            P.barrier()
            if stop <= 1:
                break

            blocks = []
            for blk in range(12):
                blocks.append((wall_d, C_QKV + blk * 256, 256, ("qkv", blk)))
            for blk in range(4):
                blocks.append((wall_d, C_GZ + blk * 256, 256, ("gz", blk)))
            blocks.append((wall_d, C_KV, 256, ("ak", 0)))
            blocks.append((wall_d, C_KV + 256, 256, ("av", 0)))
            blocks.append((wall_d, C_SM, 80, ("sm", 0)))
            for blk in range(4):
                blocks.append((wall_d, C_AQ + blk * 256, 256, ("aq", blk)))
            for blk in range(4):
                blocks.append((wall_d, C_AZ + blk * 256, 256, ("az", blk)))
            for blk in range(4):
                blocks.append((wall_d, C_IQ + blk * 256, 256, ("iq", blk)))
            blocks.append((wall_d, C_IW, 16, ("iw", 0)))

            def tok_mm(b, hv, hk, col0, wi, n):
                P.add("pe", [P.mm(ps[b][:, 0:n], hv[:, kc, col0:col0 + 128], wbuf[wi][:, kc, 0:n], kc == 0, kc == KC - 1) for kc in range(KC)],
                      [hk, ("wb", wi)], [psk(b)])

            pending = []

            def flush_pending():
                for f_ in pending:
                    f_()
                del pending[:]

            def body2(wi, info):
                kind, blk = info
                if dma_only:
                    return
                if kind != "qkv":
                    flush_pending()
                if kind == "qkv":
                    for c2 in range(2):
                        ct = blk * 2 + c2
                        b = nbank()
                        P.add("pe", [P.mm(ps[b][:, :], wbuf[wi][:, kc, c2 * 128:(c2 + 1) * 128], hT[:, kc, :], kc == 0, kc == KC - 1) for kc in range(KC)],
                              ["hT", ("wb", wi)], [psk(b)])
                        if len(pending) >= 2:
                            flush_pending()
                        rw = raw[ct % 2]; rk = ("raw", ct % 2); ac = cacc[ct % 2]; ak_ = ("cacc", ct % 2)
                        P.copy("act", rw[:, 3:515], ps[b][:, :], [psk(b)], [rk])
                        P.copy("pool", rw[:, 0:3], halo[:, ct, 0:3], [("halo", ct)], [rk])
                        P.copy("pool", halo[:, ct, 0:3], rw[:, 512:515], [rk], [("halo", ct)])
                        P.ts("dve", ac, rw[:, 3:515], cw[:, ct, 3:4], None, ALU.mult, [rk, "cw"], [ak_])
                        for i in (2, 1, 0):
                            P.stt(ac, rw[:, i:i + 512], cw[:, ct, i:i + 1], ac, ALU.mult, ALU.add, [rk, "cw", ak_], [ak_])
                        if ct < 16:
                            isq = ct < 8; hd = ct % 8
                            sl_ = silt2[ct % 3]; sk_ = ("silt", ct % 3); sq_ = sqt2[ct % 3]; qk_ = ("sqt", ct % 3)
                            P.act(sl_, ac, AF.Silu, [ak_], [sk_])
                            P.act(sq_, sl_, AF.Square, [sk_], [qk_])

                            def tail(isq=isq, hd=hd, sl_=sl_, sk_=sk_, sq_=sq_, qk_=qk_):
                                b2 = nbank()
                                ri_ = tail_state["i"] % 2
                                tail_state["i"] += 1
                                rt_ = rtt2[ri_]; rk2 = ("rtt", ri_)
                                P.add("pe", [P.mm(ps[b2][:, :], ones_b[:], sq_, True, True)], [qk_, "ones_b"], [psk(b2)])
                                P.act(rt_, ps[b2][:, :], AF.Sqrt, [psk(b2)], [rk2], scale=(128.0 if isq else 1.0), bias=(128.0 * EPS if isq else EPS))
                                P.add("dve", lambda e: e.reciprocal(out=rt_, in_=rt_), [rk2], [rk2])
                                dstT = qT if isq else kT
                                P.tt("dve", dstT[:, hd, :], sl_, rt_, ALU.mult, [sk_, rk2], [("qT" if isq else "kT", hd)])
                            pending.append(tail)
                        else:
                            P.act(vT[:, ct - 16, :], ac, AF.Silu, [ak_], [("vT", ct - 16)])
                elif kind == "gz":
                    for tt_ in range(TPM):
                        b = nbank()
                        tok_mm(b, hT, "hT", tt_ * 128, wi, 256)
                        P.act(gzs[:, tt_, blk * 256:(blk + 1) * 256], ps[b][:, 0:256], AF.Silu, [psk(b)], [("gzs", tt_)])
                elif kind == "ak":
                    for tt_ in range(TPM):
                        b = nbank(); gt = m * TPM + tt_
                        tok_mm(b, hT, "hT", tt_ * 128, wi, 256)
                        LV = 9 if DBG >= 6 else (3 if DBG >= 5 else 2)
                        kv_ = rtmp[:, :].rearrange("p (h d) -> p h d", h=2)
                        if LV >= 2:
                            rope(kv_, ps[b][:, 0:256].rearrange("p (h d) -> p h d", h=2), 2, 16, cc[:, gt, :], ns[:, gt, :], [psk(b)], "rtmp", rtmp[:, 0:256], ps[b][:, 0:256])
                        elif LV >= 1:
                            P.copy("act", rtmp[:, 0:256], ps[b][:, 0:256], [psk(b)], ["rtmp"])
                        b2 = nbank(); pb = psb(b2)
                        if LV >= 3:
                            P.add("pe", [P.tr(pb[:, g * 128:(g + 1) * 128], rtmp[:, g * 128:(g + 1) * 128], ident_b[:]) for g in range(2)], ["rtmp", "ident_b"], [psk(b2)])
                        if LV >= 4:
                            for g in range(2):
                                if DBG == 7 and g == 1:
                                    continue
                                if DBG == 8 and g == 0:
                                    continue
                                eng_ = "dve" if g == 0 else "act"
                                if DBG == 9:
                                    eng_ = "dve"
                                P.copy(eng_, kT_res[:, g, gt * 128:(gt + 1) * 128], pb[:, g * 128:(g + 1) * 128], [psk(b2)], [("kT_res", gt, g)])
                elif kind == "av":
                    for tt_ in range(TPM):
                        b = nbank(); gt = m * TPM + tt_
                        tok_mm(b, hT, "hT", tt_ * 128, wi, 256)
                        for g in range(2):
                            P.copy("dve" if g == 0 else "act", v1_res[:, gt, g, 0:128], ps[b][:, g * 128:(g + 1) * 128], [psk(b)], [("v1_res", gt, g)])
                elif kind == "sm":
                    for tt_ in range(TPM):
                        b = nbank(); gt = m * TPM + tt_
                        tok_mm(b, hT, "hT", tt_ * 128, wi, 80)
                        pk = [psk(b)]
                        P.tt("dve", gsm[:, 0:8], ps[b][:, 0:8], dtb[:], ALU.add, pk + ["dtb"], ["g0"])
                        P.act(gsm[:, 8:16], gsm[:, 0:8], AF.Abs, ["g0"], ["g1"])
                        P.act(gsm[:, 16:24], gsm[:, 8:16], AF.Exp, ["g1"], ["g2"], scale=-1.0)
                        P.act(gsm[:, 24:32], gsm[:, 16:24], AF.Ln, ["g2"], ["g3"], bias=1.0)
                        P.stt(gsm[:, 32:40], gsm[:, 0:8], 0.0, gsm[:, 24:32], ALU.max, ALU.add, ["g0", "g3"], ["g4"])
                        P.tt("dve", gat[:, tt_, :], gsm[:, 32:40], negA[:], ALU.mult, ["g4", "negA"], [("gat", tt_)])
                        P.act(bet[:, tt_, :], ps[b][:, 8:16], AF.Sigmoid, pk, [("bet", tt_)])
                        ikv = rtmp[:, 0:64].rearrange("p (h d) -> p h d", h=1)
                        rope(ikv, ps[b][:, 16:80].rearrange("p (h d) -> p h d", h=1), 1, 8, cci[:, gt, :], nsi[:, gt, :], pk, "rtmp", rtmp[:, 0:64], ps[b][:, 16:80])
                        P.copy("pool", rtmp[:, 64:128], rtmp[:, 0:64], ["rtmp"], ["rtmp"])
                        b2 = nbank(); pb = psb(b2)
                        P.add("pe", [P.tr(pb[:, 0:128], rtmp[:, 0:128], ident_b[:])], ["rtmp", "ident_b"], [psk(b2)])
                        P.copy("dve", kiT_res[:, gt * 128:(gt + 1) * 128], pb[:, 0:128], [psk(b2)], [("kiT_res", gt)])
                elif kind == "aq":
                    for ot in range(2):
                        b = nbank(); oi = 2 * m + ot
                        tok_mm(b, hTo, "hTo", ot * 128, wi, 256)
                        qv = rtmp[:, :].rearrange("p (h d) -> p h d", h=2)
                        rope(qv, ps[b][:, 0:256].rearrange("p (h d) -> p h d", h=2), 2, 16, cco[:, oi, :], nso[:, oi, :], [psk(b)], "rtmp", rtmp[:, 0:256], ps[b][:, 0:256])
                        b2 = nbank(); pb = psb(b2)
                        P.add("pe", [P.tr(pb[:, g * 128:(g + 1) * 128], rtmp[:, g * 128:(g + 1) * 128], ident_b[:]) for g in range(2)], ["rtmp", "ident_b"], [psk(b2)])
                        P.copy("dve", qTo[:, ot, 2 * blk:2 * blk + 2, :].rearrange("p g t -> p (g t)"), pb[:, 0:256], [psk(b2)], [("qTo", ot)])
                elif kind == "az":
                    for ot in range(2):
                        b = nbank()
                        tok_mm(b, hTo, "hTo", ot * 128, wi, 256)
                        P.act(azs[:, ot, blk * 256:(blk + 1) * 256], ps[b][:, 0:256], AF.Silu, [psk(b)], [("azs", ot)])
                elif kind == "iq":
                    for ot in range(2):
                        b = nbank(); oi = 2 * m + ot
                        tok_mm(b, hTo, "hTo", ot * 128, wi, 256)
                        qv = rtmp[:, :].rearrange("p (h d) -> p h d", h=4)
                        rope(qv, ps[b][:, 0:256].rearrange("p (h d) -> p h d", h=4), 4, 8, ccio[:, oi, :], nsio[:, oi, :], [psk(b)], "rtmp", rtmp[:, 0:256], ps[b][:, 0:256])
                        b2 = nbank(); pb = psb(b2)
                        P.add("pe", [P.tr(pb[:, g * 128:(g + 1) * 128], rtmp[:, g * 128:(g + 1) * 128], ident_b[:]) for g in range(2)], ["rtmp", "ident_b"], [psk(b2)])
                        P.copy("dve", qiT[:, ot, 2 * blk:2 * blk + 2, :].rearrange("p g t -> p (g t)"), pb[:, 0:256], [psk(b2)], [("qiT", ot)])
                elif kind == "iw":
                    for ot in range(2):
                        b = nbank()
                        tok_mm(b, hTo, "hTo", ot * 128, wi, 16)
                        P.ts("dve", wq[:, ot, :], ps[b][:, 0:16], 1.0 / 32.0, None, ALU.mult, [psk(b)], [("wq", ot)])

            stream(blocks[:nblk], body2, pre=pre_in)
            pre_in = None
            P.barrier()
            if stop <= 2:
                break

            A32 = scrA[:, :].bitcast(F32)
            B32 = scrB[:, :].bitcast(F32)

            def tmpl(hg):
                o = hg * 2048
                t = {}
                t["Z"] = A32[:, o:o + 512]; t["E1"] = A32[:, o + 512:o + 1024]; t["E2"] = A32[:, o + 1024:o + 1536]
                t["gU"] = A32[:, o + 1536:o + 2048]
                ob = hg * 2048
                for i, nme in enumerate(["Mc", "Nc", "Pc", "AT"]):
                    t[nme] = scrB[:, ob + i * 512: ob + (i + 1) * 512]
                return t
            gtmp = [scrC[:, :].rearrange("p (a b) -> p a b", a=8), mixT[:, :, :].rearrange("p k t -> p (k t)").rearrange("p (a b) -> p a b", a=8)]

            def gdn_chunk_prep(tt_):
                b = nbank()
                P.add("pe", [P.mm(ps[b][:, 0:8], U_f, gat[:, tt_, :], True, True), P.mm(ps[b][:, 8:16], ones_f, gat[:, tt_, :], True, True)],
                      [("gat", tt_), "cst"], [psk(b)])
                P.copy("dve", gsm[:, 0:16], ps[b][:, 0:16], [psk(b)], ["gc"])
                P.act(gsm[:, 16:32], gsm[:, 0:16], AF.Exp, ["gc"], ["eg"])
                P.tt("dve", gsm[:, 32:40], gsm[:, 8:16], gsm[:, 0:8], ALU.subtract, ["gc"], ["ekl"])
                P.act(gsm[:, 32:40], gsm[:, 32:40], AF.Exp, ["ekl"], ["ek"])
                P.tt("dve", gsm[:, 40:48], bet[:, tt_, :], gsm[:, 16:24], ALU.mult, [("bet", tt_), "eg"], ["bk"])
                P.ts("dve", gsm[:, 48:56], bet[:, tt_, :], -1.0, None, ALU.mult, [("bet", tt_)], ["nbet"])

            def bc(ap8, hg):
                return ap8[:, 4 * hg:4 * hg + 4].unsqueeze(2).broadcast_to([128, 4, 128])

            def v3(ap):
                return ap.rearrange("p (h d) -> p h d", h=4)

            def gdn_hg(tt_, hg):
                t = tmpl(hg); K = lambda n: (n, hg)
                G = gtmp[hg]
                ktok = G[:, 0, :]; vtok = G[:, 1, :]; qg = G[:, 2, :]; kbg = G[:, 3, :]; vb = G[:, 4, :]; wTn = G[:, 5, :]; vn = G[:, 6, :]; kt2 = G[:, 7, :]
                tok = slice(tt_ * 128, (tt_ + 1) * 128)
                hs = range(4 * hg, 4 * hg + 4)
                gc = gsm[:, 0:8]; eg = gsm[:, 16:24]; egl = gsm[:, 24:32]; ek = gsm[:, 32:40]; bk = gsm[:, 40:48]; nbet = gsm[:, 48:56]
                base = 4 * hg
                bA, bB, bC, bD = base, base + 1, base + 2, base + 3
                pb = psb(bA)
                P.add("pe", [P.tr(pb[:, j * 128:(j + 1) * 128], kT[:, 4 * hg + j, tok], ident_b[:]) for j in range(4)], [("kT", h) for h in hs] + ["ident_b"], [psk(bA)])
                P.copy("act", ktok, pb[:, 0:512], [psk(bA)], [K("ktok")])
                pb2 = psb(bB)
                P.add("pe", [P.tr(pb2[:, j * 128:(j + 1) * 128], vT[:, 4 * hg + j, tok], ident_b[:]) for j in range(4)], [("vT", h) for h in hs] + ["ident_b"], [psk(bB)])
                P.copy("dve", vtok, pb2[:, 0:512], [psk(bB)], [K("vtok")])
                yield
                for j in range(4):
                    P.act(t["gU"][:, j * 128:(j + 1) * 128], U_f, AF.Copy, ["cst", ("gat", tt_)], [K("gU")], scale=gat[:, tt_, 4 * hg + j:4 * hg + j + 1])
                P.add("pe", [P.mm(ps[bC][:, :], ones_f, t["gU"], True, True)], [K("gU"), "cst"], [psk(bC)])
                P.tt("dve", v3(t["Z"]), v3(ps[bC][:, :]), bc(gc, hg), ALU.subtract, [psk(bC), "gc"], [K("Z")])
                P.act(t["gU"], ps[bC][:, :], AF.Exp, [psk(bC)], [K("gU")])
                yield
                P.stt(v3(t["E1"]), v3(t["Z"]), 0.0, Ub[:].unsqueeze(1).broadcast_to([128, 4, 128]), ALU.min, ALU.add, [K("Z"), "Ub"], [K("E1")])
                P.stt(v3(t["E2"]), v3(t["Z"]), 0.0, Lp[:].unsqueeze(1).broadcast_to([128, 4, 128]), ALU.max, ALU.add, [K("Z"), "Lp"], [K("E2")])
                P.act(t["E1"], t["E1"], AF.Exp, [K("E1")], [K("E1")])
                P.act(t["E2"], t["E2"], AF.Exp, [K("E2")], [K("E2")], scale=-1.0)
                P.tt("dve", v3(qg), qT[:, 4 * hg:4 * hg + 4, tok], v3(t["gU"]), ALU.mult, [("qT", h) for h in hs] + [K("gU")], [K("qg")])
                yield
                P.add("pe", [P.mm(ps[bA][:, j * 128:(j + 1) * 128], kT[:, 4 * hg + j, tok], kT[:, 4 * hg + j, tok], True, True) for j in range(4)], [("kT", h) for h in hs], [psk(bA)])
                P.add("pe", [P.mm(ps[bB][:, j * 128:(j + 1) * 128], kT[:, 4 * hg + j, tok], qT[:, 4 * hg + j, tok], True, True) for j in range(4)], [("kT", h) for h in hs] + [("qT", h) for h in hs], [psk(bB)])
                P.tt("dve", v3(t["Z"]), v3(ps[bA][:, :]), bc(nbet, hg), ALU.mult, [psk(bA), "nbet"], [K("Z")])
                P.tt("dve", t["Mc"], t["Z"], t["E2"], ALU.mult, [K("Z"), K("E2")], [K("Mc")])
                P.tt("dve", t["AT"], ps[bB][:, :], t["E1"], ALU.mult, [psk(bB), K("E1")], [K("AT")])
                yield
                pb = psb(bC)
                P.add("pe", [P.tr(pb[:, j * 128:(j + 1) * 128], t["Mc"][:, j * 128:(j + 1) * 128], ident_b[:]) for j in range(4)], [K("Mc"), "ident_b"], [psk(bC)])
                P.copy("act", t["Nc"], pb[:, 0:512], [psk(bC)], [K("Nc")])
                for j in range(4):
                    P.tt("dve", t["Pc"][:, j * 128:(j + 1) * 128], pb[:, j * 128:(j + 1) * 128], ident_f, ALU.add, [psk(bC), "cst"], [K("Pc")])
                yield
                Mn = G[:, 5, :]; Nn = G[:, 6, :]
                Mc, Nc, Pc = t["Mc"], t["Nc"], t["Pc"]
                kM, kN, kP = K("Mc"), K("Nc"), K("Pc")
                kMn, kNn = K("wTn"), K("vn")
                for lvl in range(1, 7):
                    P.add("pe", [P.mm(ps[bA][:, j * 128:(j + 1) * 128], Nc[:, j * 128:(j + 1) * 128], Mc[:, j * 128:(j + 1) * 128], True, True) for j in range(4)], [kM, kN], [psk(bA)])
                    if lvl < 6:
                        P.add("pe", [P.mm(ps[bB][:, j * 128:(j + 1) * 128], Mc[:, j * 128:(j + 1) * 128], Nc[:, j * 128:(j + 1) * 128], True, True) for j in range(4)], [kM, kN], [psk(bB)])
                    P.copy("act", Mn, ps[bA][:, :], [psk(bA)], [kMn])
                    if lvl < 6:
                        P.copy("dve", Nn, ps[bB][:, :], [psk(bB)], [kNn])
                    yield
                    fns = []
                    for j in range(4):
                        fns.append(P.mm(ps[bC][:, j * 128:(j + 1) * 128], ident_b[:], Pc[:, j * 128:(j + 1) * 128], True, False))
                        fns.append(P.mm(ps[bC][:, j * 128:(j + 1) * 128], Mn[:, j * 128:(j + 1) * 128], Pc[:, j * 128:(j + 1) * 128], False, True))
                    P.add("pe", fns, [kMn, kP, "ident_b"], [psk(bC)])
                    P.copy("pool" if False else "dve", Pc, ps[bC][:, :], [psk(bC)], [kP])
                    Mc, Mn = Mn, Mc; kM, kMn = kMn, kM
                    Nc, Nn = Nn, Nc; kN, kNn = kNn, kN
                    yield
                TT = Pc
                P.tt("dve", v3(kbg), v3(ktok), bc(bk, hg), ALU.mult, [K("ktok"), "bk"], [K("kbg")])
                for j in range(4):
                    hh2 = 4 * hg + j
                    P.act(vb[:, j * 128:(j + 1) * 128], vtok[:, j * 128:(j + 1) * 128], AF.Copy, [K("vtok"), ("bet", tt_)], [K("vb")], scale=bet[:, tt_, hh2:hh2 + 1])
                    P.act(kt2[:, j * 128:(j + 1) * 128], ktok[:, j * 128:(j + 1) * 128], AF.Copy, [K("ktok"), "ek"], [K("kt2")], scale=ek[:, hh2:hh2 + 1])
                yield
                P.add("pe", [P.mm(ps[bA][:, j * 128:(j + 1) * 128], kbg[:, j * 128:(j + 1) * 128], TT[:, j * 128:(j + 1) * 128], True, True) for j in range(4)], [K("kbg"), kP], [psk(bA)])
                P.ts("dve", wTn, ps[bA][:, :], -1.0, None, ALU.mult, [psk(bA)], [K("wTn")])
                yield
                fns = []
                for j in range(4):
                    fns.append(P.mm(ps[bB][:, j * 128:(j + 1) * 128], TT[:, j * 128:(j + 1) * 128], vb[:, j * 128:(j + 1) * 128], True, False))
                    fns.append(P.mm(ps[bB][:, j * 128:(j + 1) * 128], wTn[:, j * 128:(j + 1) * 128], S_b[:, 4 * hg + j, :], False, True))
                P.add("pe", fns, [kP, K("vb"), K("wTn"), ("S_b", hg)], [psk(bB)])
                P.copy("act", vn, ps[bB][:, :], [psk(bB)], [K("vn")])
                yield
                fns = []
                for j in range(4):
                    fns.append(P.mm(ps[bC][:, j * 128:(j + 1) * 128], qg[:, j * 128:(j + 1) * 128], S_b[:, 4 * hg + j, :], True, False))
                    fns.append(P.mm(ps[bC][:, j * 128:(j + 1) * 128], t["AT"][:, j * 128:(j + 1) * 128], vn[:, j * 128:(j + 1) * 128], False, True))
                P.add("pe", fns, [K("qg"), K("AT"), K("vn"), ("S_b", hg)], [psk(bC)])
                P.add("pe", [P.mm(ps[bD][:, j * 128:(j + 1) * 128], kt2[:, j * 128:(j + 1) * 128], vn[:, j * 128:(j + 1) * 128], True, True) for j in range(4)], [K("kt2"), K("vn")], [psk(bD)])
                Sv = S_f[:, 4 * hg:4 * hg + 4, :]
                for j in range(4):
                    hh2 = 4 * hg + j
                    P.stt(S_f[:, hh2, :], S_f[:, hh2, :], egl[:, hh2:hh2 + 1], ps[bD][:, j * 128:(j + 1) * 128], ALU.mult, ALU.add, [("S_f", hg), "eg", psk(bD)], [("S_f", hg)])
                yield
                P.copy("act", S_b[:, 4 * hg:4 * hg + 4, :].rearrange("p h d -> p (h d)"), Sv.rearrange("p h d -> p (h d)"), [("S_f", hg)], [("S_b", hg)])
                o32 = t["Z"]
                P.copy("act", o32, ps[bC][:, :], [psk(bC)], [K("Z")])
                sm = gsm[:, 56:60] if hg == 0 else gsm[:, 60:64]
                for j in range(4):
                    P.act(t["E1"][:, j * 128:(j + 1) * 128], o32[:, j * 128:(j + 1) * 128], AF.Square, [K("Z")], [K("E1"), K("osq")], accum_out=sm[:, j:j + 1])
                yield
                P.ts("dve", sm, sm, 1.0 / 128.0, EPS, ALU.mult, [K("osq")], [K("osq")], op1=ALU.add)
                P.act(sm, sm, AF.Sqrt, [K("osq")], [K("osq")])
                P.add("dve", lambda e: e.reciprocal(out=sm, in_=sm), [K("osq")], [K("osq")])
                P.tt("dve", v3(o32), v3(o32), sm.unsqueeze(2).broadcast_to([128, 4, 128]), ALU.mult, [K("Z"), K("osq")], [K("Z")])
                P.tt("dve", v3(o32), v3(o32), gdng[:].unsqueeze(1).broadcast_to([128, 4, 128]), ALU.mult, [K("Z"), "gdng"], [K("Z")])
                P.tt("dve", o32, o32, gzs[:, tt_, hg * 512:(hg + 1) * 512], ALU.mult, [K("Z"), ("gzs", tt_)], [K("Z")])
                ot = tt_ // 2
                dsl = oa_sel[:, ot, hg * 512:(hg + 1) * 512]
                if tt_ % 2 == 0:
                    P.ts("dve", dsl, o32, flg[:, 0:1], None, ALU.mult, [K("Z"), "flg"], [("oa_sel", ot, hg)])
                else:
                    P.stt(dsl, o32, flg[:, 1:2], dsl, ALU.mult, ALU.add, [K("Z"), "flg", ("oa_sel", ot, hg)], [("oa_sel", ot, hg)])
                yield

            for tt_ in range(TPM):
                gdn_chunk_prep(tt_)
                gens = [gdn_hg(tt_, 0), gdn_hg(tt_, 1)]
                alive = [True, True]
                while any(alive):
                    for gi in range(2):
                        if alive[gi]:
                            try:
                                next(gens[gi])
                            except StopIteration:
                                alive[gi] = False
            P.barrier()
            if stop <= 3:
                break

            pre_out = prefetch(oblocks_all) if stop > 4 else None
            for ot in range(2):
                oi = 2 * m + ot
                S_i = 256 * (oi + 1)
                nkb = (S_i + 511) // 512
                for kb in range(nkb):
                    k0 = kb * 512; n = min(512, S_i - k0)
                    for pr in range(8):
                        b0 = 2 * (pr % 2); b1 = b0 + 1
                        P.add("pe", [P.mm(ps[b0][:, 0:n], qiT[0:64, ot, pr, :], kiT_res[0:64, k0:k0 + n], True, True)], [("qiT", ot), "kiT_all"], [psk(b0)])
                        P.add("pe", [P.mm(ps[b1][:, 0:n], qiT[64:128, ot, pr, :], kiT_res[64:128, k0:k0 + n], True, True)], [("qiT", ot), "kiT_all"], [psk(b1)])
                        for hh_, bb in ((2 * pr, b0), (2 * pr + 1, b1)):
                            r_ = rl[hh_ % 2]; rk = ("rl", hh_ % 2)
                            P.act(r_[:, 0:n], ps[bb][:, 0:n], AF.Relu, [psk(bb)], [rk])
                            if hh_ == 0:
                                P.ts("dve", sc[:, k0:k0 + n], r_[:, 0:n], wq[:, ot, 0:1], None, ALU.mult, [rk, ("wq", ot)], [("sc", kb)])
                            else:
                                P.stt(sc[:, k0:k0 + n], r_[:, 0:n], wq[:, ot, hh_:hh_ + 1], sc[:, k0:k0 + n], ALU.mult, ALU.add, [rk, ("wq", ot), ("sc", kb)], [("sc", kb)])
                sck = [("sc", kb) for kb in range(nkb)]
                am = sml[:, 8:9]; lo = sml[:, 9:10]; hi = sml[:, 10:11]; d0 = sml[:, 11:12]; mid = sml[:, 12:13]; cnt = sml[:, 13:14]; gg = sml[:, 14:15]
                P.add("dve", lambda e, S_i=S_i: e.tensor_reduce(out=am, in_=sc[:, 0:S_i], axis=AX.X, op=ALU.max, apply_absolute_value=True), sck, ["am"])
                P.ts("dve", lo, am, -1.0, -1.0, ALU.mult, ["am"], ["lo"], op1=ALU.add)
                P.tt("dve", sc[:, S_i - 256:S_i], sc[:, S_i - 256:S_i], cmask[:], ALU.add, sck + ["cmask"], sck)
                P.add("dve", lambda e, S_i=S_i: e.tensor_reduce(out=hi, in_=sc[:, 0:S_i], axis=AX.X, op=ALU.max), sck, ["hi"])
                P.tt("dve", d0, hi, lo, ALU.subtract, ["hi", "lo"], ["d0"])
                P.tt("dve", Dtab[:, 0:NIT], d0.broadcast_to([128, NIT]), pw2[:, 0:NIT], ALU.mult, ["d0", "pw2all"], ["Dtab"])
                P.stt(mid, d0, 0.5, lo, ALU.mult, ALU.add, ["d0", "lo"], ["mid"])
                for it in range(NIT):
                    P.ts("dve", junkb[:, 0:S_i], sc[:, 0:S_i], mid, None, ALU.is_gt, sck + ["mid"], ["junkb", "cnt"], op1=ALU.add, accum_out=cnt)
                    P.ts("dve", gg, cnt, 255.5, 0.5, ALU.is_ge, ["cnt"], ["gg"], op1=ALU.subtract)
                    if it < NIT - 1:
                        P.stt(mid, gg, Dtab[:, it:it + 1], mid, ALU.mult, ALU.add, ["gg", "Dtab", "mid"], ["mid"])
                P.ts("dve", gg, gg, 0.5, None, ALU.subtract, ["gg"], ["gg"])
                P.stt(lo, gg, Dtab[:, NIT - 1:NIT], mid, ALU.mult, ALU.add, ["gg", "Dtab", "mid"], ["lo"])
                P.ts("dve", mb[:, 0:S_i], sc[:, 0:S_i], lo, -30000.0, ALU.is_le, sck + ["lo"], ["mb"], op1=ALU.mult)
                nsb = S_i // 128
                oacc = [ps[4], ps[5], ps[6]]
                hslot = {0: (0, 0), 1: (0, 1), 2: (0, 2), 3: (1, 0), 4: (1, 1), 5: (1, 2), 6: (2, 0), 7: (2, 1)}
                started = [False, False, False]
                scale = 128.0 ** -0.5
                it_ = 0
                for sbk in range(nsb):
                    for g in range(2):
                        bs = it_ % 2; it_ += 1
                        P.add("pe", [P.mm(ps[bs][:, :], kT_res[:, g, sbk * 128:(sbk + 1) * 128], qTo[:, ot, 4 * g:4 * g + 4, :].rearrange("p h q -> p (h q)"), True, False),
                                     P.mm(ps[bs][:, :], mb[:, sbk * 128:(sbk + 1) * 128], identrep[:].rearrange("p h q -> p (h q)"), False, True)],
                              ["kT_all", ("qTo", ot), "mb", "identrep"], [psk(bs)])
                        P.act(PT[bs], ps[bs][:, :], AF.Exp, [psk(bs)], [("PT", bs)], scale=scale)
                        fns = []
                        for j in range(4):
                            hd = 4 * g + j; bi, sl = hslot[hd]
                            stf = not started[bi]
                            started[bi] = True
                            fns.append(P.mm(oacc[bi][:, sl * 130:sl * 130 + 129], PT[bs][:, j * 128:(j + 1) * 128], v1_res[:, sbk, g, 0:129], stf, sbk == nsb - 1, skip_group_check=True))
                        P.add("pe", fns, [("PT", bs), "v1_all"], [psk(4), psk(5), psk(6)])
                den = sml[:, 16:24]
                for bi, nh in ((0, 3), (1, 3), (2, 2)):
                    ov = oacc[bi][:, 0:nh * 130].rearrange("p (h c) -> p h c", c=130)
                    P.copy("dve", den[:, 3 * bi:3 * bi + nh], ov[:, :, 128], [psk(4 + bi)], [("den", bi)])
                    P.add("dve", lambda e, bi=bi, nh=nh: e.reciprocal(out=den[:, 3 * bi:3 * bi + nh], in_=den[:, 3 * bi:3 * bi + nh]), [("den", bi)], [("den", bi)])
                    o32 = rl[0][:, 0:nh * 128].rearrange("p (h d) -> p h d", d=128)
                    P.tt("dve", o32, ov[:, :, 0:128], den[:, 3 * bi:3 * bi + nh].unsqueeze(2).broadcast_to([128, nh, 128]), ALU.mult, [psk(4 + bi), ("den", bi)], [("rl", 0)])
                    c0 = 3 * bi * 128
                    P.tt("dve", otok[:, 1024 + c0:1024 + c0 + nh * 128], rl[0][:, 0:nh * 128], azs[:, ot, c0:c0 + nh * 128], ALU.mult, [("rl", 0), ("azs", ot)], [("otok", ot)])
                P.copy("pool", otok[:, 0:1024], oa_sel[:, ot, :], [("oa_sel", ot, 0), ("oa_sel", ot, 1)], [("otok", ot)])
                if debug:
                    P.dma("sp", dbg_d[oi * 128:(oi + 1) * 128, :], otok[:], [("otok", ot)], [])
                for q in range(4):
                    b = 2 + (q % 2); pb = psb(b)
                    P.add("pe", [P.tr(pb[:, j * 128:(j + 1) * 128], otok[:, (4 * q + j) * 128:(4 * q + j + 1) * 128], ident_b[:]) for j in range(4)], [("otok", ot), "ident_b"], [psk(b)])
                    for j in range(4):
                        P.copy("dve" if j % 2 == 0 else "act", mixT[:, 4 * q + j, ot * 128:(ot + 1) * 128], pb[:, j * 128:(j + 1) * 128], [psk(b)], [("mixT", ot, 4 * q + j)])
                P.barrier()

            if stop <= 4:
                break
            for ot in range(2):
                oi = 2 * m + ot
                P.dma("sp", yres[:, ot, :], xo_d[oi * 128:(oi + 1) * 128, :], [], [("y", ot)])
            P.dma("sp", xt[:], fing_d.partition_broadcast(128), [], ["fing"])
            oblocks = oblocks_all

            def body5(wi, info):
                blk = info[1]
                for ot in range(2):
                    b = nbank()
                    P.add("pe", [P.mm(ps[b][:, 0:256], mixT[:, kc, ot * 128:(ot + 1) * 128], wbuf[wi][:, kc, 0:256], kc == 0, kc == KC - 1) for kc in range(KC)],
                          [("mixT", ot, kc) for kc in range(KC)] + [("wb", wi)], [psk(b)])
                    ysl = yres[:, ot, blk * 256:(blk + 1) * 256]
                    P.tt("dve", ysl, ysl, ps[b][:, 0:256], ALU.add, [psk(b), ("y", ot)], [("y", ot)])
            stream(oblocks, body5, pre=pre_out)
            for ot in range(2):
                oi = 2 * m + ot
                yk = ("y", ot)
                P.act(xs[:], yres[:, ot, :], AF.Square, [yk], ["xs", "fs"], accum_out=sml[:, 24:25])
                P.ts("dve", sml[:, 25:26], sml[:, 24:25], 1.0 / D, EPS, ALU.mult, ["fs"], ["fs1"], op1=ALU.add)
                P.act(sml[:, 26:27], sml[:, 25:26], AF.Sqrt, ["fs1"], ["fs2"])
                P.add("dve", lambda e: e.reciprocal(out=sml[:, 27:28], in_=sml[:, 26:27]), ["fs2"], ["fs3"])
                P.stt(yres[:, ot, :], yres[:, ot, :], sml[:, 27:28], xt[:], ALU.mult, ALU.mult, [yk, "fs3", "fing"], [yk])
                P.dma("sp", out_d[oi * 128:(oi + 1) * 128, :], yres[:, ot, :], [yk], [])
            P.barrier()
            if m + 1 < n_macro:
                pre_in = prefetch(in_blocks())

        P.finish_waits("sp")
        P.emit()
    return nc


def _prep_shared(inputs):
    f32 = np.float32
    w_in = np.asarray(inputs["w_in"], f32)[0]
    o = dict(gq=0, gk=1024, gv=2048, gz=3072, ga=4096, gb=4104, aq=4112, ak=5136, av=5392, az=5648, iq=6672, ik=7696, iw=7760)
    order = [("gq", 1024), ("gk", 1024), ("gv", 1024), ("gz", 1024), ("ak", 256), ("av", 256), ("ga", 8), ("gb", 8), ("ik", 64),
             ("aq", 1024), ("az", 1024), ("iq", 1024), ("iw", 16)]
    cols = np.concatenate([np.arange(o[n], o[n] + s) for n, s in order])
    wall3 = w_in[:, cols].reshape(KC, 128, NCOL).transpose(1, 0, 2)
    bounds = [(b * 256, 256) for b in range(18)] + [(C_SM, 80)] + [(C_AQ + b * 256, 256) for b in range(12)] + [(C_IW, 16)]
    wall = np.ascontiguousarray(np.concatenate([wall3[:, :, c0:c0 + n].reshape(128, KC * n) for c0, n in bounds], axis=1))
    wout3 = np.asarray(inputs["w_out"], f32)[0].reshape(KC, 128, D).transpose(1, 0, 2)
    wout = np.ascontiguousarray(np.concatenate([wout3[:, :, b * 256:(b + 1) * 256].reshape(128, KC * 256) for b in range(8)], axis=1))
    ii = np.arange(128)
    ident = (ii[:, None] == ii[None, :]).astype(f32)
    U = (ii[:, None] <= ii[None, :]).astype(f32)
    Ls = (ii[:, None] > ii[None, :]).astype(f32)
    cst = np.ascontiguousarray(np.concatenate([ident, U, Ls, np.ones((128, 128), f32)], axis=1))
    gn = np.ascontiguousarray(np.asarray(inputs["attn_norm_g"], f32)[0].reshape(KC, 128).T)
    cw = np.ascontiguousarray(np.asarray(inputs["gdn_conv_w"], f32)[0].reshape(4, 24, 128).transpose(2, 1, 0).reshape(128, 96))
    invf = (np.float32(500000.0) ** (-(np.arange(16, dtype=f32) * f32(2.0) / f32(32.0)))).astype(f32)
    return dict(wall=wall, wout=wout, cst=cst, gn=gn, cw=cw, invf=invf,
                fing=np.ascontiguousarray(np.asarray(inputs["final_norm_g"], f32)),
                gdng=np.ascontiguousarray(np.asarray(inputs["gdn_norm_g"], f32)[0]),
                alog=np.ascontiguousarray(np.asarray(inputs["gdn_a_log"], f32)[0]),
                dtb=np.ascontiguousarray(np.asarray(inputs["gdn_dt_bias"], f32)[0]))


def _core_inputs(inputs, shared, b, hh):
    f32 = np.float32
    x = np.asarray(inputs["x"], f32)[b]
    pos = np.asarray(inputs["positions"], np.int32)[b]
    xo = np.ascontiguousarray(x.reshape(NT, 128, D)[hh::2].reshape(T // 2, D))
    pos_t = np.ascontiguousarray(pos.reshape(NT, 128).T)
    poso = np.ascontiguousarray(pos.reshape(NT, 128)[hh::2].T)
    ii = np.arange(128)
    tril = np.where(ii[None, :] <= ii[:, None], 0.0, -3.0e38).astype(f32)
    NEG = np.full((128, 128), -3.0e38, f32); Z = np.zeros((128, 128), f32)
    cmask = np.concatenate([tril, NEG], 1) if hh == 0 else np.concatenate([Z, tril], 1)
    flg = np.tile(np.array([[1.0 - hh, float(hh)]], f32), (128, 1))
    d = dict(shared)
    d.update(x=np.ascontiguousarray(x), xo=xo, pos=pos_t, poso=poso, cmask=np.ascontiguousarray(cmask), flg=np.ascontiguousarray(flg))
    return d


_NC_CACHE = {}


def kernel(**inputs):
    _nco = inputs.pop("_ncores", None)
    debug = bool(inputs.pop("_debug", False))
    n_macro = int(inputs.pop("_n_macro", NM))
    stop = int(inputs.pop("_stop", 99))
    nblk = int(inputs.pop("_nblk", 99))
    key = (debug, n_macro, stop, nblk)
    if key not in _NC_CACHE:
        _NC_CACHE[key] = build(debug=debug, n_macro=n_macro, stop=stop, nblk=nblk)
    nc = _NC_CACHE[key]
    shared = _prep_shared(inputs)
    ncores = int(_nco) if _nco is not None else 8
    in_maps = [_core_inputs(inputs, shared, c // 2, c % 2) for c in range(ncores)]
    if ncores < 8:
        res = run_bass_kernel_spmd(nc, in_maps, core_ids=list(range(ncores)), trace=True)
        print("EXEC_NS", res.exec_time_ns)
        return [res.results[c] for c in range(ncores)]
    res = run_bass_kernel_spmd(nc, in_maps, core_ids=list(range(ncores)))
    out = np.zeros((4, NT, 128, D), np.float32)
    for c in range(8):
        b, hh = c // 2, c % 2
        out[b, hh::2] = np.asarray(res.results[c]["out"], np.float32).reshape(NOWN, 128, D)
    out = out.reshape(4, T, D)
    if debug:
        dbg = [np.asarray(res.results[c]["dbg"]) for c in range(8)]
        return out, dbg
    return out
```
